# Optimizing a Trainium2 kernel written in Bass

```python
import math
import jax, jax.numpy as jnp
from jax import lax
import numpy as np

D_MODEL = 1024
BATCH = 2
SEQ = 8192
DEPTH = 4

GRID_W = 64
CTX_LEN = 256
N_MIXERS = 2
N_ATTN_LAYERS = (DEPTH + 1) // 2
N_HYENA_LAYERS = DEPTH // 2
N_HEADS = 16
N_KV_HEADS = 4
HEAD_DIM = D_MODEL // N_HEADS
KV_GROUP = N_HEADS // N_KV_HEADS
ROPE_THETA = 10000.0
Q_BLOCK = 128
D_FF = 256 * ((8 * D_MODEL // 3 + 255) // 256)
HYENA_ORDER = 2
HYENA_DIRS = 2
SHORT_CONV = 3
FILTER_BANDS = 16
FILTER_EMB = 1 + 2 * FILTER_BANDS
FILTER_HIDDEN = 64
DECAY_TARGET = 1e-2
FAST_DECAY_PCT = 0.3
SLOW_DECAY_PCT = 1.5
N_MOD = 9
EPS = 1e-6

kernel_name = "hybrid_gqa_hyena_macaron_dit"


def rms_norm(x, w):
    xf = x.astype(jnp.float32)
    y = xf * lax.rsqrt(jnp.mean(xf * xf, axis=-1, keepdims=True) + EPS)
    return (y * w.astype(jnp.float32)).astype(x.dtype)


def modulate(x, g, shift, scale):
    return rms_norm(x, g) * (1 + scale) + shift


def swiglu(h, w_gu, w_down):
    g, u = jnp.split(h @ w_gu, 2, axis=-1)
    return (jax.nn.silu(g) * u) @ w_down


def ffn_sublayer(s, mod, k, norm_g, w_gu, w_down):
    h = modulate(s, norm_g, mod[3 * k], mod[3 * k + 1])
    return s + 0.5 * mod[3 * k + 2] * swiglu(h, w_gu, w_down)


def grid_positions(n):
    n_rows = n // GRID_W
    rows = jnp.repeat(jnp.arange(n_rows, dtype=jnp.int32), GRID_W)
    cols = jnp.tile(jnp.arange(GRID_W, dtype=jnp.int32), n_rows)
    return rows, cols


def rope_axis(x, pos):
    half = x.shape[-1] // 2
    freqs = ROPE_THETA ** (-jnp.arange(half, dtype=jnp.float32) / half)
    ang = pos.astype(jnp.float32)[:, None] * freqs
    cos = jnp.cos(ang)[:, None, :].astype(x.dtype)
    sin = jnp.sin(ang)[:, None, :].astype(x.dtype)
    x1, x2 = x[..., :half], x[..., half:]
    return jnp.concatenate([x1 * cos - x2 * sin, x1 * sin + x2 * cos], axis=-1)


def rope_2d(x, rows, cols):
    h = x.shape[-1] // 2
    return jnp.concatenate([rope_axis(x[..., :h], rows), rope_axis(x[..., h:], cols)], axis=-1)


def qkv_heads(h, w_qkv, q_norm, k_norm):
    B, L, _ = h.shape
    nq, nk = N_HEADS * HEAD_DIM, N_KV_HEADS * HEAD_DIM
    qkv = h @ w_qkv
    q = qkv[..., :nq].reshape(B, L, N_HEADS, HEAD_DIM)
    k = qkv[..., nq:nq + nk].reshape(B, L, N_KV_HEADS, HEAD_DIM)
    v = qkv[..., nq + nk:].reshape(B, L, N_KV_HEADS, HEAD_DIM)
    return rms_norm(q, q_norm), rms_norm(k, k_norm), v


def gqa(q, k, v):
    B, Lq = q.shape[0], q.shape[1]
    q = q.reshape(B, Lq, N_KV_HEADS, KV_GROUP, HEAD_DIM) * (HEAD_DIM ** -0.5)
    s = jnp.einsum('bqkgd,bnkd->bkgqn', q, k, preferred_element_type=jnp.float32)
    p = jax.nn.softmax(s, axis=-1).astype(v.dtype)
    o = jnp.einsum('bkgqn,bnkd->bqkgd', p, v)
    return o.reshape(B, Lq, N_HEADS * HEAD_DIM)


def attention_mixer(h_lat, h_ctx, w_qkv, w_o, q_norm, k_norm, with_ctx_out):
    B, S, _ = h_lat.shape
    rows, cols = grid_positions(S)
    q_l, k_l, v_l = qkv_heads(h_lat, w_qkv, q_norm, k_norm)
    q_l = rope_2d(q_l, rows, cols)
    k_l = rope_2d(k_l, rows, cols)
    q_c, k_c, v_c = qkv_heads(h_ctx, w_qkv, q_norm, k_norm)
    k_all = jnp.concatenate([k_c, k_l], axis=1)
    v_all = jnp.concatenate([v_c, v_l], axis=1)
    n_blk = S // Q_BLOCK
    q_blocks = q_l.reshape(B, n_blk, Q_BLOCK, N_HEADS, HEAD_DIM).swapaxes(0, 1)
    o_blocks = lax.map(lambda qb: gqa(qb, k_all, v_all), q_blocks)
    o_lat = o_blocks.swapaxes(0, 1).reshape(B, S, N_HEADS * HEAD_DIM) @ w_o
    o_ctx = gqa(q_c, k_c, v_c) @ w_o if with_ctx_out else None
    return o_lat, o_ctx


def hyena_filters(L, f_w1, f_b1, f_w2, f_b2, f_w3, f_b3, f_wout, f_freq):
    f32 = jnp.float32
    t = jnp.linspace(0.0, 1.0, L, dtype=f32)[:, None]
    w = 2.0 * math.pi * jnp.arange(L, dtype=f32)[:, None] / L
    bands = jnp.linspace(1e-4, FILTER_BANDS - 1, FILTER_BANDS, dtype=f32)
    feats = jnp.concatenate([t, jnp.cos(bands * w), -jnp.sin(bands * w)], axis=-1)
    a = f_freq.astype(f32)
    hid = jnp.sin(a * (feats @ f_w1.astype(f32) + f_b1.astype(f32)))
    hid = jnp.sin(a * (hid @ f_w2.astype(f32) + f_b2.astype(f32)))
    hid = jnp.sin(a * (hid @ f_w3.astype(f32) + f_b3.astype(f32)))
    filt = (hid @ f_wout.astype(f32)).reshape(L, HYENA_ORDER, HYENA_DIRS, D_MODEL)
    min_decay = math.log(DECAY_TARGET) / SLOW_DECAY_PCT
    max_decay = math.log(DECAY_TARGET) / FAST_DECAY_PCT
    deltas = jnp.abs(jnp.linspace(min_decay, max_decay, D_MODEL, dtype=f32))
    decay = jnp.exp(-t * deltas)
    return filt * decay[:, None, None, :]


def bidir_filter_taps(filt_o):
    fwd, bwd = filt_o[:, 0], filt_o[:, 1]
    zero = jnp.zeros((1, fwd.shape[-1]), fwd.dtype)
    return jnp.concatenate([fwd[:1] + bwd[:1], fwd[1:], zero, bwd[:0:-1]], axis=0)


def fft_conv(z, taps, bias):
    L = z.shape[1]
    zf = z.astype(jnp.float32)
    zq = jnp.fft.rfft(zf, n=2 * L, axis=1)
    tq = jnp.fft.rfft(taps, n=2 * L, axis=0)[None]
    y = jnp.fft.irfft(zq * tq, n=2 * L, axis=1)[:, :L]
    return (y + zf * bias.astype(jnp.float32)).astype(z.dtype)


def short_conv(u, w, b):
    up = jnp.pad(u, ((0, 0), (1, 1), (0, 0)))
    return up[:, :-2] * w[0] + up[:, 1:-1] * w[1] + up[:, 2:] * w[2] + b


def hyena_mixer(h, w_in, b_in, conv_w, conv_b, f_w1, f_b1, f_w2, f_b2, f_w3, f_b3, f_wout, f_freq, f_bias, w_out, b_out):
    L = h.shape[1]
    u = short_conv(h @ w_in + b_in, conv_w, conv_b)
    v, x1, x2 = jnp.split(u, 3, axis=-1)
    filt = hyena_filters(L, f_w1, f_b1, f_w2, f_b2, f_w3, f_b3, f_wout, f_freq)
    z = x1 * fft_conv(v, bidir_filter_taps(filt[:, 0]), f_bias[0])
    y = x2 * fft_conv(z, bidir_filter_taps(filt[:, 1]), f_bias[1])
    return y @ w_out + b_out


def setup_inputs(seed: int = 0) -> dict:
    key = jax.random.key(seed)
    ks = iter(jax.random.split(key, 32))
    f32 = jnp.float32

    def nrm(shape, scale):
        return jax.random.normal(next(ks), shape, f32) * scale

    D, F = D_MODEL, D_FF
    QKV = (N_HEADS + 2 * N_KV_HEADS) * HEAD_DIM
    nA, nH = N_ATTN_LAYERS, N_HYENA_LAYERS
    FH = FILTER_HIDDEN
    return {
        "x": nrm((BATCH, SEQ, D), 1.0),
        "c": nrm((BATCH, D), 1.0),
        "ctx": nrm((BATCH, CTX_LEN, D), 1.0),
        "c_ctx": nrm((D,), 1.0),
        "w_mod": nrm((DEPTH, D, N_MOD * D), 0.5 * D ** -0.5),
        "b_mod": nrm((DEPTH, N_MOD * D), 0.02),
        "norm_w": 1.0 + nrm((DEPTH, 3, D), 0.05),
        "ffn_w_gate_up": nrm((DEPTH, 2, D, 2 * F), D ** -0.5),
        "ffn_w_down": nrm((DEPTH, 2, F, D), F ** -0.5),
        "attn_w_qkv": nrm((nA, D, QKV), D ** -0.5),
        "attn_w_o": nrm((nA, N_HEADS * HEAD_DIM, D), (N_HEADS * HEAD_DIM) ** -0.5),
        "attn_q_norm": 1.0 + nrm((nA, HEAD_DIM), 0.05),
        "attn_k_norm": 1.0 + nrm((nA, HEAD_DIM), 0.05),
        "hy_w_in": nrm((nH, D, 3 * D), D ** -0.5),
        "hy_b_in": nrm((nH, 3 * D), 0.02),
        "hy_conv_w": nrm((nH, SHORT_CONV, 3 * D), SHORT_CONV ** -0.5),
        "hy_conv_b": nrm((nH, 3 * D), 0.02),
        "hy_f_w1": nrm((nH, FILTER_EMB, FH), FILTER_EMB ** -0.5),
        "hy_f_b1": nrm((nH, FH), 0.1),
        "hy_f_w2": nrm((nH, FH, FH), FH ** -0.5),
        "hy_f_b2": nrm((nH, FH), 0.1),
        "hy_f_w3": nrm((nH, FH, FH), FH ** -0.5),
        "hy_f_b3": nrm((nH, FH), 0.1),
        "hy_f_wout": nrm((nH, FH, HYENA_ORDER * HYENA_DIRS * D), 0.03 * FH ** -0.5),
        "hy_f_freq": 1.0 + nrm((nH, FH), 0.05),
        "hy_f_bias": nrm((nH, HYENA_ORDER, D), 0.5),
        "hy_w_out": nrm((nH, D, D), D ** -0.5),
        "hy_b_out": nrm((nH, D), 0.02),
    }


def reference(x, c, ctx, c_ctx, w_mod, b_mod, norm_w, ffn_w_gate_up, ffn_w_down,
              attn_w_qkv, attn_w_o, attn_q_norm, attn_k_norm,
              hy_w_in, hy_b_in, hy_conv_w, hy_conv_b, hy_f_w1, hy_f_b1, hy_f_w2, hy_f_b2,
              hy_f_w3, hy_f_b3, hy_f_wout, hy_f_freq, hy_f_bias, hy_w_out, hy_b_out):
    B, D = x.shape[0], x.shape[-1]
    for l in range(DEPTH):
        mod_x = (jax.nn.silu(c) @ w_mod[l] + b_mod[l]).reshape(B, N_MOD, 1, D).swapaxes(0, 1)
        mod_c = (jax.nn.silu(c_ctx) @ w_mod[l] + b_mod[l]).reshape(N_MOD, 1, 1, D)
        is_attn = (l % N_MIXERS) == 0
        ctx_out = l < DEPTH - 1
        ctx_live = ctx_out or is_attn

        x = ffn_sublayer(x, mod_x, 0, norm_w[l, 0], ffn_w_gate_up[l, 0], ffn_w_down[l, 0])
        if ctx_live:
            ctx = ffn_sublayer(ctx, mod_c, 0, norm_w[l, 0], ffn_w_gate_up[l, 0], ffn_w_down[l, 0])

        h_x = modulate(x, norm_w[l, 1], mod_x[3], mod_x[4])
        if is_attn:
            a = l // N_MIXERS
            h_c = modulate(ctx, norm_w[l, 1], mod_c[3], mod_c[4])
            o_x, o_c = attention_mixer(h_x, h_c, attn_w_qkv[a], attn_w_o[a],
                                       attn_q_norm[a], attn_k_norm[a], ctx_out)
        else:
            j = l // N_MIXERS
            hp = (hy_w_in[j], hy_b_in[j], hy_conv_w[j], hy_conv_b[j], hy_f_w1[j], hy_f_b1[j],
                  hy_f_w2[j], hy_f_b2[j], hy_f_w3[j], hy_f_b3[j], hy_f_wout[j], hy_f_freq[j],
                  hy_f_bias[j], hy_w_out[j], hy_b_out[j])
            o_x = hyena_mixer(h_x, *hp)
            o_c = hyena_mixer(modulate(ctx, norm_w[l, 1], mod_c[3], mod_c[4]), *hp) if ctx_out else None
        x = x + mod_x[5] * o_x
        if ctx_out:
            ctx = ctx + mod_c[5] * o_c

        x = ffn_sublayer(x, mod_x, 2, norm_w[l, 2], ffn_w_gate_up[l, 1], ffn_w_down[l, 1])
        if ctx_out:
            ctx = ffn_sublayer(ctx, mod_c, 2, norm_w[l, 2], ffn_w_gate_up[l, 1], ffn_w_down[l, 1])
    return x
```

```python
import numpy as np
import ml_dtypes
from contextlib import ExitStack
import concourse.bass as bass
import concourse.mybir as mybir
from concourse.bass_utils import run_bass_kernel_spmd

F32 = mybir.dt.float32
BF16 = mybir.dt.bfloat16
AF = mybir.ActivationFunctionType
ALU = mybir.AluOpType
NPBF = ml_dtypes.bfloat16

D = 1024
DC = 8
FF = 2816
FC = 22
SEQ = 8192
CTX = 256
NCORES = 8
T_LAT = 2048
T_CTX = 64
T = T_LAT + T_CTX
TILES = [(0, 512, 0), (512, 512, 0), (1024, 512, 0), (1536, 512, 0), (2048, 64, 1)]
EPS = 1e-6


class Track:
    def __init__(self, kb, name):
        self.name = name
        self.h = kb.es.enter_context(kb.nc.semaphore(name))
        self.v = 0

    def inc(self, instr, amt=1):
        instr.then_inc(self.h, amt)
        self.v += amt
        return (self, self.v)


class KB:
    def __init__(self):
        self.nc = bass.Bass("TRN2", target_bir_lowering=False)
        self.es = ExitStack()
        self.waited = {}
        self.pe = Track(self, "t_pe")
        self.act = Track(self, "t_act")
        self.dve = Track(self, "t_dve")
        self.pool = Track(self, "t_pool")
        self.banks = [self.es.enter_context(self.nc.psum_tensor(f"bank{i}", [128, 512], F32)) for i in range(8)]
        self.bank_free = [None] * 8
        self.out_evs = []

    def dram(self, name, shape, dt, kind):
        return self.nc.dram_tensor(name, list(shape), dt, kind=kind).ap()

    def sb(self, name, shape, dt=F32):
        return self.es.enter_context(self.nc.sbuf_tensor(name, list(shape), dt))

    def track(self, name):
        return Track(self, name)

    def wait(self, engname, ev):
        if ev is None:
            return
        tr, v = ev
        key = (engname, tr.name)
        if self.waited.get(key, 0) >= v:
            return
        self.waited[key] = v
        getattr(self.nc, engname).wait_ge(tr.h, v)

    def finish(self):
        for ev in self.out_evs:
            self.wait("sync", ev)
        self.es.close()
        return self.nc


class WStream:
    def __init__(self, kb, nslots=3, width=2048):
        self.kb = kb
        self.n = nslots
        self.buf = [kb.sb(f"wbuf{i}", [128, width], BF16) for i in range(nslots)]
        self.ld = [kb.track(f"t_wld{i}") for i in range(nslots)]
        self.free_ev = [None] * nslots
        self.cnt = 0

    def load(self, srcs):
        kb = self.kb
        s = self.cnt % self.n
        self.cnt += 1
        kb.wait("gpsimd", self.free_ev[s])
        ev = None
        for dst_fn, src in srcs:
            ev = self.ld[s].inc(kb.nc.gpsimd.dma_start(out=dst_fn(self.buf[s]), in_=src), 16)
        return s, ev

    def release(self, s, ev):
        self.free_ev[s] = ev


def dense(kb, ws, KC, rhs_fn, units, evac, tiles, G=1, banks=(0, 1, 2, 3), pre_ev=None):
    nc = kb.nc
    nb = len(banks) // G
    pend = []

    def issue(ui):
        srcs = []
        for g in range(G):
            srcs.append((lambda b, g=g: b[:, g * KC * 128:(g + 1) * KC * 128].rearrange("p (k m) -> p k m", m=128),
                         units[ui][g].rearrange("(k p) m -> p k m", p=128)))
        return ws.load(srcs)

    PF = ws.n - 1
    for ui in range(min(PF, len(units))):
        pend.append(issue(ui))
    it = 0
    for ui in range(len(units)):
        s, ld_ev = pend.pop(0)
        kb.wait("tensor", ld_ev)
        kb.wait("tensor", pre_ev)
        wv = ws.buf[s]
        last_ev = None
        for ti, (t0, tn, vec) in enumerate(tiles):
            bsel = [banks[(it % nb) * G + g] for g in range(G)]
            it += 1
            for g in range(G):
                kb.wait("tensor", kb.bank_free[bsel[g]])
            mm = None
            for g in range(G):
                for kc in range(KC):
                    mm = nc.tensor.matmul(kb.banks[bsel[g]][:, 0:tn],
                                          wv[:, (g * KC + kc) * 128:(g * KC + kc + 1) * 128],
                                          rhs_fn(kc, ti), start=(kc == 0), stop=(kc == KC - 1))
            pe_ev = kb.pe.inc(mm)
            last_ev = pe_ev
            fe = evac(ui, ti, [kb.banks[b][:, 0:tn] for b in bsel], pe_ev)
            for b in bsel:
                kb.bank_free[b] = fe
        ws.release(s, last_ev)
        if ui + PF < len(units):
            pend.append(issue(ui + PF))


def build_ts(pre, post):
    kb = KB()
    nc = kb.nc
    IN, OUT = "ExternalInput", "ExternalOutput"
    xT_in = kb.dram("xT_in", [D, T], F32, IN)
    xT_out = kb.dram("xT_out", [D, T], F32, OUT)
    if pre:
        oT_in = kb.dram("oT_in", [D, T], BF16, IN)
        w_o = kb.dram("w_o", [D, D], F32, IN)
        b_o = kb.dram("b_o", [128, DC], F32, IN)
        modp_in = kb.dram("modp_in", [128, DC * 18], F32, IN)
        w_gu2 = kb.dram("w_gu2", [D, 2 * FF], F32, IN)
        w_dn2 = kb.dram("w_dn2", [FF, D], F32, IN)
    if post:
        cT = kb.dram("cT", [128, DC * 2], F32, IN)
        w_mod = kb.dram("w_mod", [D, 9 * D], F32, IN)
        b_mod = kb.dram("b_mod", [128, 72], F32, IN)
        norm_w = kb.dram("norm_w", [128, 3 * DC], F32, IN)
        w_gu1 = kb.dram("w_gu1", [D, 2 * FF], F32, IN)
        w_dn1 = kb.dram("w_dn1", [FF, D], F32, IN)
        mod_out = kb.dram("mod_out", [128, DC * 18], F32, OUT)
        if post == "attn":
            hT_out = kb.dram("hT_out", [D, T], BF16, OUT)
        else:
            w_in = kb.dram("w_in", [D, 3 * D], F32, IN)
            b_in = kb.dram("b_in", [128, 24], F32, IN)
            pT_out = kb.dram("pT_out", [3 * D, T], BF16, OUT)

    xT = kb.sb("xT", [128, DC, T], F32)
    hT = kb.sb("hT", [128, DC, T], BF16)
    aT = kb.sb("aT", [128, 11, T], BF16)
    sqb = kb.sb("sqb", [128, DC, 512], BF16)
    rstd = kb.sb("rstd", [128, 512], F32)
    tmp = [kb.sb(f"tmp{i}", [128, 512], F32) for i in range(2)]
    sg = [kb.sb(f"sg{i}", [128, 512], F32) for i in range(2)]
    ones = kb.sb("ones", [128, 128], BF16)
    modt = kb.sb("modt", [128, DC, 3, 3, 2], F32)
    modp = kb.sb("modp", [128, DC, 3, 3, 2], F32)
    bo_t = kb.sb("bo_t", [128, DC], F32)
    ws = WStream(kb)
    ld = kb.track("t_ld")
    ld_m = kb.track("t_ldm")
    ld_o = kb.track("t_ldo")
    ld_c = kb.track("t_ldc")
    ld_b = kb.track("t_ldb")
    st = kb.track("t_st")
    st_slot = [kb.track("t_st0"), kb.track("t_st1")]

    ev_ones = kb.pool.inc(nc.gpsimd.memset(ones[:], 1.0))
    epsb = kb.sb("epsb", [128, 1], F32)
    ev_eps = kb.pool.inc(nc.gpsimd.memset(epsb[:], EPS))
    ev_x = None
    for c in range(DC):
        eng = nc.sync if c % 2 == 0 else nc.scalar
        ev_x = ld.inc(eng.dma_start(out=xT[:, c, :], in_=xT_in[c * 128:(c + 1) * 128, :]), 16)
    x_ready = {"vector": ev_x, "scalar": ev_x}

    tmp_free = [None, None]
    sg_free = [None, None]
    state = {"x_ev": ev_x, "sqb_free": None, "stat_free": None, "tmpi": 0, "sgi": 0, "rstd_free": None}

    def norm_mod(mt, k, h_free_ev):
        last = None
        for ti, (t0, tn, vec) in enumerate(TILES):
            kb.wait("scalar", state["x_ev"])
            kb.wait("scalar", state["sqb_free"])
            ev_sq = None
            for c in range(DC):
                ev_sq = kb.act.inc(nc.scalar.activation(out=sqb[:, c, 0:tn], in_=xT[:, c, t0:t0 + tn], func=AF.Square))
            kb.wait("tensor", ev_sq)
            kb.wait("tensor", ev_ones)
            kb.wait("tensor", kb.bank_free[4])
            mm = None
            for c in range(DC):
                mm = nc.tensor.matmul(kb.banks[4][:, 0:tn], ones[:, :], sqb[:, c, 0:tn], start=(c == 0), stop=(c == DC - 1))
            ev_stat = kb.pe.inc(mm)
            state["sqb_free"] = ev_stat
            kb.wait("scalar", ev_stat)
            kb.wait("scalar", ev_eps)
            kb.wait("scalar", state["rstd_free"])
            ev_sd = kb.act.inc(nc.scalar.activation(out=rstd[:, 0:tn], in_=kb.banks[4][:, 0:tn], func=AF.Sqrt,
                                                    bias=epsb[:, 0:1], scale=1.0 / D))
            kb.bank_free[4] = ev_sd
            kb.wait("vector", ev_sd)
            kb.wait("vector", state["x_ev"])
            ev_r = kb.dve.inc(nc.vector.reciprocal(out=rstd[:, 0:tn], in_=rstd[:, 0:tn]))
            for c in range(DC):
                i = state["tmpi"] % 2
                state["tmpi"] += 1
                kb.wait("vector", tmp_free[i])
                ev_t = kb.dve.inc(nc.vector.scalar_tensor_tensor(
                    out=tmp[i][:, 0:tn], in0=xT[:, c, t0:t0 + tn], scalar=mt[:, c, k, 0, vec:vec + 1],
                    in1=rstd[:, 0:tn], op0=ALU.mult, op1=ALU.mult))
                kb.wait("scalar", ev_t)
                kb.wait("scalar", h_free_ev)
                last = kb.act.inc(nc.scalar.activation(out=hT[:, c, t0:t0 + tn], in_=tmp[i][:, 0:tn], func=AF.Identity,
                                                       bias=mt[:, c, k, 1, vec:vec + 1], scale=1.0))
                tmp_free[i] = last
            state["rstd_free"] = (kb.dve, kb.dve.v)
        return last

    def ffn(mt, k, w_gu, w_dn, h_ev):
        wg = w_gu
        pe_last = None
        for hf in range(2):
            j0 = hf * 11
            units = [[wg[:, (j0 + j) * 128:(j0 + j + 1) * 128], wg[:, FF + (j0 + j) * 128:FF + (j0 + j + 1) * 128]]
                     for j in range(11)]
            a_free = pe_last
            dve_last = [None]

            def evac_up(ui, ti, ps, pe_ev):
                t0, tn, vec = TILES[ti]
                i = state["sgi"] % 2
                state["sgi"] += 1
                kb.wait("scalar", pe_ev)
                kb.wait("scalar", sg_free[i])
                ev_s = kb.act.inc(nc.scalar.activation(out=sg[i][:, 0:tn], in_=ps[0], func=AF.Silu))
                kb.wait("vector", ev_s)
                kb.wait("vector", a_free)
                ev_d = kb.dve.inc(nc.vector.tensor_tensor(out=aT[:, ui, t0:t0 + tn], in0=sg[i][:, 0:tn], in1=ps[1], op=ALU.mult))
                sg_free[i] = ev_d
                dve_last[0] = ev_d
                return ev_d

            dense(kb, ws, DC, lambda kc, ti: hT[:, kc, TILES[ti][0]:TILES[ti][0] + TILES[ti][1]], units, evac_up, TILES,
                  G=2, banks=(0, 1, 2, 3), pre_ev=h_ev)

            units_d = [[w_dn[j0 * 128:(j0 + 11) * 128, m * 128:(m + 1) * 128]] for m in range(DC)]
            x_last = [None]

            def evac_dn(ui, ti, ps, pe_ev):
                t0, tn, vec = TILES[ti]
                kb.wait("vector", pe_ev)
                ev = kb.dve.inc(nc.vector.scalar_tensor_tensor(
                    out=xT[:, ui, t0:t0 + tn], in0=ps[0], scalar=mt[:, ui, k, 2, vec:vec + 1],
                    in1=xT[:, ui, t0:t0 + tn], op0=ALU.mult, op1=ALU.add))
                x_last[0] = ev
                return ev

            dense(kb, ws, 11, lambda kc, ti: aT[:, kc, TILES[ti][0]:TILES[ti][0] + TILES[ti][1]], units_d, evac_dn, TILES,
                  G=1, banks=(0, 1, 2, 3), pre_ev=dve_last[0])
            pe_last = (kb.pe, kb.pe.v)
            state["x_ev"] = x_last[0]
        return pe_last

    h_free = None

    if pre:
        ev_m = ld_m.inc(nc.sync.dma_start(out=modp[:].rearrange("p a b c d -> p (a b c d)"), in_=modp_in), 16)
        ev_m = ld_m.inc(nc.sync.dma_start(out=bo_t[:], in_=b_o), 16)
        ev_o = None
        for c in range(DC):
            eng = nc.sync if c % 2 == 0 else nc.scalar
            ev_o = ld_o.inc(eng.dma_start(out=hT[:, c, :], in_=oT_in[c * 128:(c + 1) * 128, :]), 16)
        units = [[w_o[:, m * 128:(m + 1) * 128]] for m in range(DC)]
        x_last = [None]

        def evac_o(ui, ti, ps, pe_ev):
            t0, tn, vec = TILES[ti]
            i = state["tmpi"] % 2
            state["tmpi"] += 1
            kb.wait("scalar", pe_ev)
            kb.wait("scalar", ev_m)
            kb.wait("scalar", tmp_free[i])
            ev_a = kb.act.inc(nc.scalar.activation(out=tmp[i][:, 0:tn], in_=ps[0], func=AF.Identity,
                                                   bias=bo_t[:, ui:ui + 1], scale=1.0))
            kb.wait("vector", ev_a)
            kb.wait("vector", ev_m)
            kb.wait("vector", ev_x)
            ev = kb.dve.inc(nc.vector.scalar_tensor_tensor(
                out=xT[:, ui, t0:t0 + tn], in0=tmp[i][:, 0:tn], scalar=modp[:, ui, 1, 2, vec:vec + 1],
                in1=xT[:, ui, t0:t0 + tn], op0=ALU.mult, op1=ALU.add))
            tmp_free[i] = ev
            x_last[0] = ev
            return ev_a

        dense(kb, ws, DC, lambda kc, ti: hT[:, kc, TILES[ti][0]:TILES[ti][0] + TILES[ti][1]], units, evac_o, TILES,
              G=1, banks=(0, 1, 2, 3), pre_ev=ev_o)
        state["x_ev"] = x_last[0]
        h_free = (kb.pe, kb.pe.v)
        h_ev = norm_mod(modp, 2, h_free)
        h_free = ffn(modp, 2, w_gu2, w_dn2, h_ev)

    if post:
        c_sb = kb.sb("c_sb", [128, DC, 2], F32)
        sc = kb.sb("sc", [128, DC, 2], F32)
        bm = kb.sb("bm", [128, 72], F32)
        nw = kb.sb("nw", [128, 3, DC], F32)
        raw = kb.sb("raw", [128, 9, DC, 2], F32)
        wm = [kb.sb(f"wm{i}", [128, DC, 128], F32) for i in range(3)]
        wm_ld = [kb.track(f"t_wm{i}") for i in range(3)]
        wm_free = [None] * 3
        ev_c = ld_c.inc(nc.sync.dma_start(out=c_sb[:].rearrange("p a b -> p (a b)"), in_=cT), 16)
        ev_c = ld_c.inc(nc.sync.dma_start(out=bm[:], in_=b_mod), 16)
        ev_c = ld_c.inc(nc.sync.dma_start(out=nw[:].rearrange("p a b -> p (a b)"), in_=norm_w), 16)
        kb.wait("scalar", ev_c)
        ev_sc = kb.act.inc(nc.scalar.activation(out=sc[:], in_=c_sb[:], func=AF.Silu))
        kb.wait("tensor", ev_sc)
        kb.wait("tensor", kb.bank_free[5])
        psm = kb.banks[5]
        mm = None
        for n in range(72):
            s = n % 3
            kb.wait("sync", wm_free[s])
            ev_w = wm_ld[s].inc(nc.sync.dma_start(out=wm[s][:], in_=w_mod[:, n * 128:(n + 1) * 128].rearrange("(k p) m -> p k m", p=128)), 16)
            kb.wait("tensor", ev_w)
            for kc in range(DC):
                mm = nc.tensor.matmul(psm[:, 2 * n:2 * n + 2], wm[s][:, kc, :], sc[:, kc, :], start=(kc == 0), stop=(kc == DC - 1))
            wm_free[s] = kb.pe.inc(mm)
        ev_pm = wm_free[(72 - 1) % 3]
        kb.wait("vector", ev_pm)
        kb.wait("vector", ev_c)
        nc.vector.tensor_tensor(out=raw[:].rearrange("p m c v -> p (m c) v"),
                                in0=psm[:, 0:144].rearrange("p (n v) -> p n v", v=2),
                                in1=bm[:].unsqueeze(2).to_broadcast([128, 72, 2]), op=ALU.add)
        for k in range(3):
            nc.vector.scalar_tensor_tensor(out=modt[:, :, k, 0, :], in0=raw[:, 3 * k + 1, :, :], scalar=1.0,
                                           in1=nw[:, k, :].unsqueeze(2).to_broadcast([128, DC, 2]),
                                           op0=ALU.add, op1=ALU.mult)
            nc.vector.tensor_copy(out=modt[:, :, k, 1, :], in_=raw[:, 3 * k, :, :])
            ev_mt = kb.dve.inc(nc.vector.tensor_scalar(out=modt[:, :, k, 2, :], in0=raw[:, 3 * k + 2, :, :],
                                                       scalar1=(1.0 if k == 1 else 0.5), scalar2=None, op0=ALU.mult))
        kb.bank_free[5] = ev_mt
        kb.wait("scalar", ev_mt)
        kb.wait("sync", ev_mt)
        kb.out_evs.append(st.inc(nc.sync.dma_start(out=mod_out, in_=modt[:].rearrange("p a b c d -> p (a b c d)")), 16))

        h_ev = norm_mod(modt, 0, h_free)
        h_free = ffn(modt, 0, w_gu1, w_dn1, h_ev)
        h_ev = norm_mod(modt, 1, h_free)
        if post == "attn":
            kb.wait("sync", h_ev)
            for c in range(DC):
                eng = nc.sync
                kb.out_evs.append(st.inc(eng.dma_start(out=hT_out[c * 128:(c + 1) * 128, :], in_=hT[:, c, :]), 16))
        else:
            bi = kb.sb("bi", [128, 24], F32)
            ev_bi = ld_b.inc(nc.sync.dma_start(out=bi[:], in_=b_in), 16)
            units = [[w_in[:, m * 128:(m + 1) * 128]] for m in range(24)]
            st_free = {}

            def evac_p(ui, ti, ps, pe_ev):
                t0, tn, vec = TILES[ti]
                slot = ui % 2
                kb.wait("scalar", pe_ev)
                kb.wait("scalar", ev_bi)
                if ti == 0:
                    kb.wait("scalar", st_free.get(slot))
                ev_a = kb.act.inc(nc.scalar.activation(out=aT[:, slot, t0:t0 + tn], in_=ps[0], func=AF.Identity,
                                                       bias=bi[:, ui:ui + 1], scale=1.0))
                if ti == len(TILES) - 1:
                    kb.wait("sync", ev_a)
                    e = st_slot[slot].inc(nc.sync.dma_start(out=pT_out[ui * 128:(ui + 1) * 128, :], in_=aT[:, slot, :]), 16)
                    st_free[slot] = e
                    kb.out_evs.append(e)
                return ev_a

            kb.wait("scalar", h_free)
            dense(kb, ws, DC, lambda kc, ti: hT[:, kc, TILES[ti][0]:TILES[ti][0] + TILES[ti][1]], units, evac_p, TILES,
                  G=1, banks=(0, 1, 2, 3), pre_ev=h_ev)

    kb.wait("sync", state["x_ev"])
    for c in range(DC):
        kb.out_evs.append(st.inc(nc.sync.dma_start(out=xT_out[c * 128:(c + 1) * 128, :], in_=xT[:, c, :]), 16))
    return kb.finish()


NTOK = CTX + SEQ
ATILES = [(0, 256)] + [(256 + 512 * i, 512) for i in range(16)]
NKC = NTOK // 128


def build_attn():
    kb = KB()
    nc = kb.nc
    IN, OUT = "ExternalInput", "ExternalOutput"
    hT = kb.dram("hT", [D, NTOK], BF16, IN)
    wq = kb.dram("wq", [D, 256], F32, IN)
    wkk = kb.dram("wkk", [D, 128], F32, IN)
    wv = kb.dram("wv", [D, 64], F32, IN)
    nrm = kb.dram("nrm", [128, 2], F32, IN)
    cosT = kb.dram("cosT", [128, NTOK], F32, IN)
    sinT = kb.dram("sinT", [128, NTOK], F32, IN)
    rotm = kb.dram("rotm", [128, 128], F32, IN)
    blk = kb.dram("blk", [128, 128], F32, IN)
    oT = kb.dram("oT", [256, NTOK], BF16, OUT)

    qT = kb.sb("qT", [128, 2, NTOK], BF16)
    kT2 = kb.sb("kT2", [128, NTOK], BF16)
    vaug = kb.sb("vaug", [128, NKC, 128], BF16)
    wq_sb = kb.sb("wq_sb", [128, DC, 256], BF16)
    wkk_sb = kb.sb("wkk_sb", [128, DC, 128], BF16)
    wv_sb = kb.sb("wv_sb", [128, DC, 64], BF16)
    rot_sb = kb.sb("rot_sb", [128, 128], BF16)
    blk_sb = kb.sb("blk_sb", [128, 128], BF16)
    nrm_sb = kb.sb("nrm_sb", [128, 2], F32)
    epsb = kb.sb("epsb", [128, 1], F32)
    htile = [kb.sb(f"htile{i}", [128, DC, 512], BF16) for i in range(2)]
    ctile = [kb.sb(f"ctile{i}", [128, 512], F32) for i in range(2)]
    stile = [kb.sb(f"stile{i}", [128, 512], F32) for i in range(2)]
    sqt = kb.sb("sqt", [128, 512], BF16)
    sd = kb.sb("sd", [128, 512], F32)
    qn = kb.sb("qn", [128, 512], BF16)
    t1 = kb.sb("t1", [128, 512], F32)
    t2 = kb.sb("t2", [128, 512], F32)
    pt = [kb.sb(f"pt{i}", [128, 512], BF16) for i in range(3)]
    rec = kb.sb("rec", [128, 512], F32)
    ostage = [kb.sb(f"ostage{i}", [64, 512], BF16) for i in range(2)]

    ld_w = kb.track("t_ldw")
    ld_h = [kb.track("t_ldh0"), kb.track("t_ldh1")]
    st_o = [kb.track("t_sto0"), kb.track("t_sto1")]

    ev_w = ld_w.inc(nc.gpsimd.dma_start(out=wq_sb[:], in_=wq.rearrange("(k p) m -> p k m", p=128)), 16)
    ev_w = ld_w.inc(nc.gpsimd.dma_start(out=wkk_sb[:], in_=wkk.rearrange("(k p) m -> p k m", p=128)), 16)
    ev_w = ld_w.inc(nc.gpsimd.dma_start(out=wv_sb[:], in_=wv.rearrange("(k p) m -> p k m", p=128)), 16)
    ev_w = ld_w.inc(nc.gpsimd.dma_start(out=rot_sb[:], in_=rotm), 16)
    ev_w = ld_w.inc(nc.gpsimd.dma_start(out=blk_sb[:], in_=blk), 16)
    ev_w = ld_w.inc(nc.gpsimd.dma_start(out=nrm_sb[:], in_=nrm), 16)
    nc.gpsimd.memset(epsb[:], EPS)
    ev_pool = kb.pool.inc(nc.gpsimd.memset(vaug[:, :, 64:128], 1.0))

    h_free = [None, None]
    c_free = [None, None]

    def load_tile(ti):
        t0, tn = ATILES[ti]
        s = ti % 2
        kb.wait("sync", h_free[s])
        kb.wait("sync", c_free[s])
        ev = ld_h[s].inc(nc.sync.dma_start(out=htile[s][:, :, 0:tn], in_=hT[:, t0:t0 + tn].rearrange("(k p) n -> p k n", p=128)), 16)
        ev = ld_h[s].inc(nc.sync.dma_start(out=ctile[s][:, 0:tn], in_=cosT[:, t0:t0 + tn]), 16)
        ev = ld_h[s].inc(nc.sync.dma_start(out=stile[s][:, 0:tn], in_=sinT[:, t0:t0 + tn]), 16)
        return ev

    kb.wait("tensor", ev_w)
    kb.wait("vector", ev_w)
    kb.wait("scalar", ev_w)
    kb.wait("scalar", ev_pool)
    kb.wait("vector", ev_pool)
    kb.wait("tensor", ev_pool)
    pend = load_tile(0)
    sqt_free = None
    sd_free = None
    qn_free = None
    dve_last = None
    for ti, (t0, tn) in enumerate(ATILES):
        s = ti % 2
        ev_h = pend
        if ti + 1 < len(ATILES):
            pend = load_tile(ti + 1)
        kb.wait("tensor", ev_h)
        kb.wait("vector", ev_h)
        groups = [(wq_sb, 0, 0, lambda: qT[:, 0, t0:t0 + tn]), (wq_sb, 128, 0, lambda: qT[:, 1, t0:t0 + tn]),
                  (wkk_sb, 0, 1, lambda: kT2[:, t0:t0 + tn])]
        for (wsb, c0, ni, dst) in groups:
            kb.wait("tensor", kb.bank_free[5])
            mm = None
            for kc in range(DC):
                mm = nc.tensor.matmul(kb.banks[5][:, 0:tn], wsb[:, kc, c0:c0 + 128], htile[s][:, kc, 0:tn],
                                      start=(kc == 0), stop=(kc == DC - 1))
            ev_a = kb.pe.inc(mm)
            kb.wait("scalar", ev_a)
            kb.wait("scalar", sqt_free)
            ev_sq = kb.act.inc(nc.scalar.activation(out=sqt[:, 0:tn], in_=kb.banks[5][:, 0:tn], func=AF.Square))
            kb.wait("tensor", ev_sq)
            kb.wait("tensor", kb.bank_free[6])
            ev_b = kb.pe.inc(nc.tensor.matmul(kb.banks[6][:, 0:tn], blk_sb[:, :], sqt[:, 0:tn], start=True, stop=True))
            sqt_free = ev_b
            kb.wait("scalar", ev_b)
            kb.wait("scalar", sd_free)
            ev_sd = kb.act.inc(nc.scalar.activation(out=sd[:, 0:tn], in_=kb.banks[6][:, 0:tn], func=AF.Sqrt,
                                                    bias=epsb[:, 0:1], scale=1.0 / 64))
            kb.bank_free[6] = ev_sd
            kb.wait("vector", ev_sd)
            nc.vector.reciprocal(out=sd[:, 0:tn], in_=sd[:, 0:tn])
            kb.wait("vector", ev_a)
            kb.wait("vector", qn_free)
            ev_qn = kb.dve.inc(nc.vector.scalar_tensor_tensor(out=qn[:, 0:tn], in0=kb.banks[5][:, 0:tn],
                                                              scalar=nrm_sb[:, ni:ni + 1], in1=sd[:, 0:tn],
                                                              op0=ALU.mult, op1=ALU.mult))
            kb.bank_free[5] = ev_qn
            sd_free = ev_qn
            kb.wait("tensor", ev_qn)
            kb.wait("tensor", kb.bank_free[7])
            ev_c = kb.pe.inc(nc.tensor.matmul(kb.banks[7][:, 0:tn], rot_sb[:, :], qn[:, 0:tn], start=True, stop=True))
            nc.vector.tensor_tensor(out=t1[:, 0:tn], in0=qn[:, 0:tn], in1=ctile[s][:, 0:tn], op=ALU.mult)
            kb.wait("vector", ev_c)
            nc.vector.tensor_tensor(out=t2[:, 0:tn], in0=kb.banks[7][:, 0:tn], in1=stile[s][:, 0:tn], op=ALU.mult)
            dve_last = kb.dve.inc(nc.vector.tensor_tensor(out=dst(), in0=t1[:, 0:tn], in1=t2[:, 0:tn], op=ALU.add))
            kb.bank_free[7] = dve_last
            qn_free = ev_c
        nch = tn // 128
        kb.wait("tensor", kb.bank_free[3])
        mm = None
        for ci in range(nch):
            for kc in range(DC):
                mm = nc.tensor.matmul(kb.banks[3][:, ci * 64:(ci + 1) * 64], htile[s][:, kc, ci * 128:(ci + 1) * 128],
                                      wv_sb[:, kc, :], start=(kc == 0), stop=(kc == DC - 1))
        ev_v = kb.pe.inc(mm)
        h_free[s] = ev_v
        c_free[s] = dve_last
        kb.wait("scalar", ev_v)
        c_first = t0 // 128
        ev_vc = kb.act.inc(nc.scalar.activation(out=vaug[:, c_first:c_first + nch, 0:64],
                                                in_=kb.banks[3][:, 0:nch * 64].rearrange("p (c d) -> p c d", d=64),
                                                func=AF.Identity))
        kb.bank_free[3] = ev_vc
    ev_A_dve = dve_last
    ev_A_act = (kb.act, kb.act.v)

    iters = []
    for hd in range(4):
        for qi, (q0, nq) in enumerate(ATILES):
            nk = 2 if qi == 0 else NKC
            for kc in range(nk):
                iters.append((hd, qi, kc, kc == 0, kc == nk - 1))
    N = len(iters)
    evS = [None] * N
    evP = [None] * N
    pt_free = [None] * 3
    o_st_free = [None, None]
    grp = [0]
    SB = (0, 1, 2)
    OB = (3, 4)

    def emit_S(n):
        hd, qi, kc, first, last = iters[n]
        q0, nq = ATILES[qi]
        r0 = (hd % 2) * 64
        b = SB[n % 3]
        kb.wait("tensor", kb.bank_free[b])
        kb.wait("tensor", ev_A_dve)
        evS[n] = kb.pe.inc(nc.tensor.matmul(kb.banks[b][:, 0:nq], kT2[r0:r0 + 64, kc * 128:(kc + 1) * 128],
                                            qT[r0:r0 + 64, hd // 2, q0:q0 + nq], start=True, stop=True))

    emit_S(0)
    for n in range(N):
        hd, qi, kc, first, last = iters[n]
        q0, nq = ATILES[qi]
        if n + 1 < N:
            emit_S(n + 1)
        b = SB[n % 3]
        sl = n % 3
        kb.wait("scalar", evS[n])
        kb.wait("scalar", pt_free[sl])
        evP[n] = kb.act.inc(nc.scalar.activation(out=pt[sl][:, 0:nq], in_=kb.banks[b][:, 0:nq], func=AF.Exp, scale=0.125))
        kb.bank_free[b] = evP[n]
        ob = OB[grp[0] % 2]
        kb.wait("tensor", evP[n])
        if first:
            kb.wait("tensor", kb.bank_free[ob])
            kb.wait("tensor", ev_A_act)
        ev_pv = kb.pe.inc(nc.tensor.matmul(kb.banks[ob][:, 0:nq], vaug[:, kc, :], pt[sl][:, 0:nq], start=first, stop=last))
        pt_free[sl] = ev_pv
        if last:
            os_ = grp[0] % 2
            kb.wait("vector", ev_pv)
            nc.vector.reciprocal(out=rec[64:128, 0:nq], in_=kb.banks[ob][64:128, 0:nq])
            kb.wait("vector", o_st_free[os_])
            ev_e = kb.dve.inc(nc.vector.tensor_tensor(out=ostage[os_][:, 0:nq], in0=kb.banks[ob][0:64, 0:nq],
                                                      in1=rec[64:128, 0:nq], op=ALU.mult))
            kb.bank_free[ob] = ev_e
            kb.wait("sync", ev_e)
            e = st_o[os_].inc(nc.sync.dma_start(out=oT[hd * 64:(hd + 1) * 64, q0:q0 + nq], in_=ostage[os_][:, 0:nq]), 16)
            o_st_free[os_] = e
            kb.out_evs.append(e)
            grp[0] += 1
    return kb.finish()


def attn_consts():
    p = np.arange(128)
    d = p % 64
    fidx = (d % 16).astype(np.float32)
    freqs = (np.float32(10000.0) ** (-fidx / np.float32(16.0))).astype(np.float32)
    t = np.arange(SEQ)
    rows = (t // 64).astype(np.float32)
    cols = (t % 64).astype(np.float32)
    pos = np.where((d < 32)[:, None], rows[None, :], cols[None, :]).astype(np.float32)
    ang = (pos * freqs[:, None]).astype(np.float32)
    cosT = np.ones((128, NTOK), np.float32)
    sinT = np.zeros((128, NTOK), np.float32)
    cosT[:, CTX:] = np.cos(ang)
    sinT[:, CTX:] = np.sin(ang)
    rotm = np.zeros((128, 128), np.float32)
    for m in range(128):
        if (m % 32) < 16:
            rotm[m + 16, m] = -1.0
        else:
            rotm[m - 16, m] = 1.0
    blk = np.zeros((128, 128), np.float32)
    blk[:64, :64] = 1.0
    blk[64:, 64:] = 1.0
    return cosT, sinT, rotm, blk


NFFT = 16384
CG = 32
NG = 4
HALF_PI = float(np.pi / 2)


def dft_tables():
    a = np.arange(128)
    ang = 2.0 * np.pi * np.outer(a, a) / 128.0
    C = np.cos(ang)
    S = np.sin(ang)
    angt = 2.0 * np.pi * np.outer(a, a) / NFFT
    Twr = np.cos(angt)
    Twi = -np.sin(angt)
    f = lambda *xs: np.ascontiguousarray(np.concatenate(xs, axis=1)).astype(np.float32)
    tabs = {
        "FT1": f(C, -S),
        "FS1": np.ascontiguousarray(np.concatenate([np.concatenate([C[:64], -S[:64]], 1),
                                                    np.concatenate([S[:64], C[:64]], 1)], 0)).astype(np.float32),
        "TW": f(Twr, Twi),
        "FS3": f(C, S, -S),
        "R12": f(C, S, -S, C),
        "LAB": f(C[:, :64], S[:, :64], -S[:, :64], C[:, :64]),
    }
    return tabs


def build_hf(debug=False):
    kb = KB()
    nc = kb.nc
    IN, OUT = "ExternalInput", "ExternalOutput"
    featsT = kb.dram("featsT", [33, NFFT], F32, IN)
    w1 = kb.dram("w1", [33, 64], F32, IN)
    w23 = kb.dram("w23", [64, 128], F32, IN)
    fvec = kb.dram("fvec", [64, 4], F32, IN)
    wout = kb.dram("wout", [64, NG * 128], F32, IN)
    decF = kb.dram("decF", [NG, 128, 128 * CG], F32, IN)
    decB = kb.dram("decB", [NG, 128, 128 * CG], F32, IN)
    FT1 = kb.dram("FT1", [128, 256], F32, IN)
    TW = kb.dram("TW", [128, 256], F32, IN)
    FS3 = kb.dram("FS3", [128, 384], F32, IN)
    Hq = kb.dram("Hq", [2, 128, 2 * 128 * 128], BF16, OUT)

    if debug:
        dbg_hid = kb.dram("dbg_hid", [64, NFFT], F32, OUT)
        dbg_taps = kb.dram("dbg_taps", [128, 2 * CG * 128], BF16, OUT)
        dbg_ap = kb.dram("dbg_ap", [128, 2 * 2 * CG * 128], BF16, OUT)
    hid3 = kb.sb("hid3", [64, NFFT], F32)
    w1_sb = kb.sb("w1_sb", [33, 64], F32)
    w23_sb = kb.sb("w23_sb", [64, 128], F32)
    fv = kb.sb("fv", [64, 4], F32)
    a4 = kb.sb("a4", [64, 1], F32)
    ab4 = kb.sb("ab4", [64, 3], F32)
    hpi = kb.sb("hpi", [64, 1], F32)
    wout_sb = kb.sb("wout_sb", [64, NG * 128], F32)
    ft1_sb = kb.sb("ft1_sb", [128, 256], BF16)
    tw_sb = kb.sb("tw_sb", [128, 256], F32)
    fs3_sb = kb.sb("fs3_sb", [128, 384], BF16)
    ftile = [kb.sb(f"ftile{i}", [33, 512], F32) for i in range(2)]
    mq = [kb.sb(f"mq{i}", [64, 512], F32) for i in range(2)]
    ms = [kb.sb(f"ms{i}", [64, 512], F32) for i in range(2)]
    mc = [kb.sb(f"mc{i}", [64, 512], F32) for i in range(2)]
    mh = [kb.sb(f"mh{i}", [64, 512], F32) for i in range(2)]
    dF = kb.sb("dF", [128, 128, CG], F32)
    dB = kb.sb("dB", [128, 128, CG], F32)
    taps = kb.sb("taps", [128, 2, CG, 128], BF16)
    Ap = kb.sb("Ap", [128, 2, 2 * CG, 128], BF16)
    tA = kb.sb("tA", [128, 256], F32)
    tB = kb.sb("tB", [128, 256], F32)
    tt = [kb.sb(f"tt{i}", [128, 256], F32) for i in range(4)]
    stg = [kb.sb(f"stg{i}", [128, 2, 4, 128], BF16) for i in range(2)]

    ld_c = kb.track("t_ldc")
    ld_f = [kb.track("t_ldf0"), kb.track("t_ldf1")]
    ld_d = kb.track("t_ldd")
    st_s = [kb.track("t_sts0"), kb.track("t_sts1")]

    ev_c = ld_c.inc(nc.sync.dma_start(out=w1_sb[:], in_=w1), 16)
    ev_c = ld_c.inc(nc.sync.dma_start(out=w23_sb[:], in_=w23), 16)
    ev_c = ld_c.inc(nc.sync.dma_start(out=fv[:], in_=fvec), 16)
    ev_c = ld_c.inc(nc.sync.dma_start(out=wout_sb[:], in_=wout), 16)
    ev_c = ld_c.inc(nc.sync.dma_start(out=tw_sb[:], in_=TW), 16)
    ev_c = ld_c.inc(nc.gpsimd.dma_start(out=ft1_sb[:], in_=FT1), 16)
    ev_c = ld_c.inc(nc.gpsimd.dma_start(out=fs3_sb[:], in_=FS3), 16)
    for e in ("tensor", "vector", "scalar"):
        kb.wait(e, ev_c)
    nc.vector.memset(hpi[:], HALF_PI)
    ev_k = kb.dve.inc(nc.vector.tensor_scalar(out=a4[:], in0=fv[:, 0:1], scalar1=0.25, scalar2=None, op0=ALU.mult))
    kb.wait("vector", ev_k)
    ev_k = kb.dve.inc(nc.vector.tensor_scalar(out=ab4[:], in0=fv[:, 1:4], scalar1=a4[:, 0:1], scalar2=None, op0=ALU.mult))
    kb.wait("vector", ev_k)
    kb.wait("scalar", ev_k)

    f_free = [None, None]

    def load_feats(ti):
        s = ti % 2
        kb.wait("sync", f_free[s])
        return ld_f[s].inc(nc.sync.dma_start(out=ftile[s][:, :], in_=featsT[:, ti * 512:(ti + 1) * 512]), 16)

    buf_free = [None, None]
    q_free = [None, None]

    def layer(ti, ly, rhs_ap, K, w_ap, out_ap, rhs_ev, bank):
        s = ti % 2
        kb.wait("tensor", rhs_ev)
        kb.wait("tensor", kb.bank_free[bank])
        ev_m = kb.pe.inc(nc.tensor.matmul(kb.banks[bank][0:64, 0:512], w_ap, rhs_ap, start=True, stop=True))
        kb.wait("vector", ev_m)
        kb.wait("vector", q_free[s])
        ev_q = kb.dve.inc(nc.vector.tensor_scalar(out=mq[s][:, :], in0=kb.banks[bank][0:64, 0:512], scalar1=a4[:, 0:1],
                                                  scalar2=ab4[:, ly:ly + 1], op0=ALU.mult, op1=ALU.add))
        kb.bank_free[bank] = ev_q
        kb.wait("scalar", ev_q)
        nc.scalar.activation(out=ms[s][:, :], in_=mq[s][:, :], func=AF.Sin)
        ev_s = kb.act.inc(nc.scalar.activation(out=mc[s][:, :], in_=mq[s][:, :], func=AF.Sin, bias=hpi[:, 0:1], scale=1.0))
        q_free[s] = ev_s
        kb.wait("vector", ev_s)
        nc.vector.tensor_tensor(out=mc[s][:, :], in0=ms[s][:, :], in1=mc[s][:, :], op=ALU.mult)
        nc.vector.tensor_tensor(out=ms[s][:, :], in0=ms[s][:, :], in1=ms[s][:, :], op=ALU.mult)
        nc.vector.tensor_scalar(out=ms[s][:, :], in0=ms[s][:, :], scalar1=-8.0, scalar2=4.0, op0=ALU.mult, op1=ALU.add)
        if out_ap is None:
            kb.wait("vector", buf_free[s])
            o = mh[s][:, :]
        else:
            o = out_ap
        ev_h = kb.dve.inc(nc.vector.tensor_tensor(out=o, in0=mc[s][:, :], in1=ms[s][:, :], op=ALU.mult))
        return ev_h, ev_m

    pend = [load_feats(0), load_feats(1)]
    for tp in range(0, 32, 2):
        evs = [pend[0], pend[1]]
        hev = [None, None]
        for ly in range(3):
            for j in range(2):
                ti = tp + j
                s = ti % 2
                if ly == 0:
                    hev[j], ev_m = layer(ti, 0, ftile[s][:, :], 33, w1_sb[:, :], None, evs[j], 5 + j)
                    f_free[s] = ev_m
                elif ly == 1:
                    hev[j], ev_m = layer(ti, 1, mh[s][:, :], 64, w23_sb[:, 0:64], None, hev[j], 5 + j)
                    buf_free[s] = ev_m
                else:
                    hev[j], ev_m = layer(ti, 2, mh[s][:, :], 64, w23_sb[:, 64:128], hid3[:, ti * 512:(ti + 1) * 512], hev[j], 5 + j)
                    buf_free[s] = ev_m
        if tp + 2 < 32:
            pend = [load_feats(tp + 2), load_feats(tp + 3)]
    ev_hid = (kb.dve, kb.dve.v)

    d_free = None
    taps_free = None
    ap_free = None
    it4 = [0]
    st_free_hf = [None, None]
    for g in range(NG):
        kb.wait("sync", d_free)
        kb.wait("scalar", d_free)
        ev_d = ld_d.inc(nc.sync.dma_start(out=dF[:].rearrange("p n c -> p (n c)"), in_=decF[g]), 16)
        ev_d = ld_d.inc(nc.scalar.dma_start(out=dB[:].rearrange("p n c -> p (n c)"), in_=decB[g]), 16)
        kb.wait("vector", ev_d)
        kb.wait("tensor", ev_hid)
        for nb in range(32):
            bank = nb % 2
            kb.wait("tensor", kb.bank_free[bank])
            mm = None
            for q in range(4):
                n2 = nb * 4 + q
                mm = nc.tensor.matmul(kb.banks[bank][:, q * 128:(q + 1) * 128], hid3[:, n2:NFFT:128],
                                      wout_sb[:, g * 128:(g + 1) * 128], start=True, stop=True)
            ev_m = kb.pe.inc(mm)
            kb.wait("vector", ev_m)
            psv = kb.banks[bank][:, 0:512].rearrange("p (n o d c) -> p n o d c", n=4, o=2, d=2)
            nc.vector.tensor_tensor(out=tA[:, :].rearrange("p (n o c) -> p n o c", n=4, o=2), in0=psv[:, :, :, 0, :],
                                    in1=dF[:, nb * 4:nb * 4 + 4, :].unsqueeze(2).to_broadcast([128, 4, 2, CG]), op=ALU.mult)
            ev_b = kb.dve.inc(nc.vector.tensor_tensor(out=tB[:, :].rearrange("p (n o c) -> p n o c", n=4, o=2), in0=psv[:, :, :, 1, :],
                                                      in1=dB[:, nb * 4:nb * 4 + 4, :].unsqueeze(2).to_broadcast([128, 4, 2, CG]), op=ALU.mult))
            kb.bank_free[bank] = ev_b
            if nb == 0:
                kb.wait("vector", taps_free)
            ev_t = kb.dve.inc(nc.vector.tensor_tensor(out=taps[:, :, :, nb * 4:nb * 4 + 4].rearrange("p o c n -> p n o c"),
                                                      in0=tA[:, :].rearrange("p (n o c) -> p n o c", n=4, o=2),
                                                      in1=tB[:, :].rearrange("p (n o c) -> p n o c", n=4, o=2), op=ALU.add))
        d_free = ev_t
        kb.wait("tensor", ev_t)
        for pr in range(CG):
            bank = 2 + pr % 2
            kb.wait("tensor", kb.bank_free[bank])
            mm = None
            for j in range(2):
                sq = 2 * pr + j
                o, c = sq // CG, sq % CG
                mm = nc.tensor.matmul(kb.banks[bank][:, j * 256:(j + 1) * 256], taps[:, o, c, :], ft1_sb[:, :], start=True, stop=True)
            ev_m = kb.pe.inc(mm)
            kb.wait("vector", ev_m)
            if pr == 0:
                kb.wait("vector", ap_free)
            psv = kb.banks[bank][:, 0:512].rearrange("p (s r k) -> p s r k", s=2, r=2)
            twr = tw_sb[:, 0:128].unsqueeze(1).to_broadcast([128, 2, 128])
            twi = tw_sb[:, 128:256].unsqueeze(1).to_broadcast([128, 2, 128])
            v4 = [t[:, :].rearrange("p (s k) -> p s k", s=2) for t in tt]
            nc.vector.tensor_tensor(out=v4[0], in0=psv[:, :, 0, :], in1=twr, op=ALU.mult)
            nc.vector.tensor_tensor(out=v4[1], in0=psv[:, :, 1, :], in1=twi, op=ALU.mult)
            nc.vector.tensor_tensor(out=v4[2], in0=psv[:, :, 0, :], in1=twi, op=ALU.mult)
            ev_b = kb.dve.inc(nc.vector.tensor_tensor(out=v4[3], in0=psv[:, :, 1, :], in1=twr, op=ALU.mult))
            kb.bank_free[bank] = ev_b
            nc.vector.tensor_tensor(out=Ap[:, 0, 2 * pr:2 * pr + 2, :], in0=v4[0], in1=v4[1], op=ALU.subtract)
            ev_a = kb.dve.inc(nc.vector.tensor_tensor(out=Ap[:, 1, 2 * pr:2 * pr + 2, :], in0=v4[2], in1=v4[3], op=ALU.add))
        taps_free = (kb.pe, kb.pe.v)
        kb.wait("tensor", ev_a)
        for blk4 in range(16):
            s0 = blk4 * 4
            o, c0 = s0 // CG, s0 % CG
            for bsel in (4, 5):
                kb.wait("tensor", kb.bank_free[bsel])
            rr = Ap[:, 0, s0:s0 + 4, :]
            ri_ = Ap[:, 1, s0:s0 + 4, :]
            nc.tensor.matmul(kb.banks[4][:, 0:512], fs3_sb[:, 0:128], rr, start=True, stop=False)
            nc.tensor.matmul(kb.banks[4][:, 0:512], fs3_sb[:, 128:256], ri_, start=False, stop=True)
            nc.tensor.matmul(kb.banks[5][:, 0:512], fs3_sb[:, 256:384], rr, start=True, stop=False)
            ev_m = kb.pe.inc(nc.tensor.matmul(kb.banks[5][:, 0:512], fs3_sb[:, 0:128], ri_, start=False, stop=True))
            sl = it4[0] % 2
            it4[0] += 1
            kb.wait("scalar", ev_m)
            kb.wait("scalar", st_free_hf[sl])
            nc.scalar.activation(out=stg[sl][:, 0, :, :], in_=kb.banks[4][:, 0:512].rearrange("p (s k) -> p s k", s=4),
                                 func=AF.Identity, scale=1.0 / NFFT)
            ev_e = kb.act.inc(nc.scalar.activation(out=stg[sl][:, 1, :, :], in_=kb.banks[5][:, 0:512].rearrange("p (s k) -> p s k", s=4),
                                                   func=AF.Identity, scale=1.0 / NFFT))
            kb.bank_free[4] = ev_e
            kb.bank_free[5] = ev_e
            kb.wait("sync", ev_e)
            cc = g * CG + c0
            dst = Hq[o].rearrange("p (r c k) -> p r c k", r=2, c=128)[:, :, cc:cc + 4, :]
            e = st_s[sl].inc(nc.sync.dma_start(out=dst, in_=stg[sl][:, :, :, :]), 16)
            st_free_hf[sl] = e
            kb.out_evs.append(e)
        ap_free = (kb.pe, kb.pe.v)
    if debug:
        kb.wait("sync", (kb.dve, kb.dve.v))
        kb.wait("sync", (kb.pe, kb.pe.v))
        for dst, src in ((dbg_hid, hid3[:, :]), (dbg_taps, taps[:].rearrange("p o c n -> p (o c n)")),
                         (dbg_ap, Ap[:].rearrange("p r s k -> p (r s k)"))):
            kb.out_evs.append(st_s[0].inc(nc.sync.dma_start(out=dst, in_=src), 16))
    return kb.finish()


def build_hc():
    kb = KB()
    nc = kb.nc
    IN, OUT = "ExternalInput", "ExternalOutput"
    p3 = kb.dram("p3", [NG, 128, 3 * CG * 130], BF16, IN)
    cw = kb.dram("cw", [128, 9 * 128], F32, IN)
    cb = kb.dram("cb", [128, 3 * 128], F32, IN)
    fb = kb.dram("fb", [128, 2 * 128], F32, IN)
    mask = kb.dram("mask", [128, 1], F32, IN)
    Hq = kb.dram("Hq", [2, 128, 2 * 128 * 128], BF16, IN)
    FS1 = kb.dram("FS1", [128, 256], F32, IN)
    TW = kb.dram("TW", [128, 256], F32, IN)
    FS3 = kb.dram("FS3", [128, 384], F32, IN)
    R12 = kb.dram("R12", [128, 512], F32, IN)
    LAB = kb.dram("LAB", [128, 256], F32, IN)
    yTL = kb.dram("yTL", [128, 128 * 128], BF16, OUT)

    fs1_sb = kb.sb("fs1_sb", [128, 256], BF16)
    tw_sb = kb.sb("tw_sb", [128, 256], F32)
    fs3_sb = kb.sb("fs3_sb", [128, 384], BF16)
    r12_sb = kb.sb("r12_sb", [128, 512], BF16)
    lab_sb = kb.sb("lab_sb", [128, 256], BF16)
    cw_sb = kb.sb("cw_sb", [128, 3, 3, 128], F32)
    cb_sb = kb.sb("cb_sb", [128, 3, 128], F32)
    fb_sb = kb.sb("fb_sb", [128, 2, 128], F32)
    mask_sb = kb.sb("mask_sb", [128, 1], F32)
    pt_ = kb.sb("pt_", [128, 3, CG, 130], BF16)
    u = kb.sb("u", [128, 3, CG, 128], BF16)
    uf = kb.sb("uf", [128, CG, 128], F32)
    uf2 = kb.sb("uf2", [128, CG, 128], F32)
    z = kb.sb("z", [128, CG, 128], BF16)
    vfb = kb.sb("vfb", [128, CG, 128], F32)
    Ap = kb.sb("Ap", [128, 2, CG, 128], BF16)
    Y = kb.sb("Y", [128, 2, CG, 128], BF16)
    Bp = kb.sb("Bp", [128, 2, CG, 128], BF16)
    hq = [kb.sb(f"hq{i}", [128, 2, CG, 128], BF16) for i in range(2)]
    tt = [kb.sb(f"tt{i}", [128, 512], F32) for i in range(4)]
    ystage = uf2[:].bitcast(BF16)[:, :, 0:128]

    ld_c = kb.track("t_ldc")
    ld_p = kb.track("t_ldp")
    ld_h = kb.track("t_ldh")
    st_y = kb.track("t_sty")

    ev_c = ld_c.inc(nc.sync.dma_start(out=tw_sb[:], in_=TW), 16)
    ev_c = ld_c.inc(nc.sync.dma_start(out=cw_sb[:].rearrange("p a b c -> p (a b c)"), in_=cw), 16)
    ev_c = ld_c.inc(nc.sync.dma_start(out=cb_sb[:].rearrange("p a c -> p (a c)"), in_=cb), 16)
    ev_c = ld_c.inc(nc.sync.dma_start(out=fb_sb[:].rearrange("p a c -> p (a c)"), in_=fb), 16)
    ev_c = ld_c.inc(nc.sync.dma_start(out=mask_sb[:], in_=mask), 16)
    ev_c = ld_c.inc(nc.gpsimd.dma_start(out=fs1_sb[:], in_=FS1), 16)
    ev_c = ld_c.inc(nc.gpsimd.dma_start(out=fs3_sb[:], in_=FS3), 16)
    ev_c = ld_c.inc(nc.gpsimd.dma_start(out=r12_sb[:], in_=R12), 16)
    ev_c = ld_c.inc(nc.gpsimd.dma_start(out=lab_sb[:], in_=LAB), 16)
    for e in ("tensor", "vector", "scalar", "gpsimd"):
        kb.wait(e, ev_c)

    twr2 = tw_sb[:, 0:128].unsqueeze(1).to_broadcast([128, 2, 128])
    twi2 = tw_sb[:, 128:256].unsqueeze(1).to_broadcast([128, 2, 128])
    v2 = [t[:, 0:256].rearrange("p (s k) -> p s k", s=2) for t in tt]
    v4 = [t[:, 0:512].rearrange("p (s k) -> p s k", s=4) for t in tt]

    st = {"p_free": None, "u_free": None, "hq_free": [None, None], "y_free": None, "ap_free": None, "y2_free": None,
          "bp_free": None, "z_pe_free": None}

    def fft_conv(src_fn, src_ev, hq_t, hq_ev, gate):
        kb.wait("tensor", src_ev)
        ev_a = None
        for pr in range(CG // 2):
            bank = pr % 2
            kb.wait("tensor", kb.bank_free[bank])
            mm = None
            for j in range(2):
                mm = nc.tensor.matmul(kb.banks[bank][:, j * 256:(j + 1) * 256], src_fn(2 * pr + j), fs1_sb[:, :], start=True, stop=True)
            ev_m = kb.pe.inc(mm)
            kb.wait("vector", ev_m)
            if pr == 0:
                kb.wait("vector", st["ap_free"])
            psv = kb.banks[bank][:, 0:512].rearrange("p (s r k) -> p s r k", s=2, r=2)
            nc.vector.tensor_tensor(out=v2[0], in0=psv[:, :, 0, :], in1=twr2, op=ALU.mult)
            nc.vector.tensor_tensor(out=v2[1], in0=psv[:, :, 1, :], in1=twi2, op=ALU.mult)
            nc.vector.tensor_tensor(out=v2[2], in0=psv[:, :, 0, :], in1=twi2, op=ALU.mult)
            ev_b = kb.dve.inc(nc.vector.tensor_tensor(out=v2[3], in0=psv[:, :, 1, :], in1=twr2, op=ALU.mult))
            kb.bank_free[bank] = ev_b
            nc.vector.tensor_tensor(out=Ap[:, 0, 2 * pr:2 * pr + 2, :], in0=v2[0], in1=v2[1], op=ALU.subtract)
            ev_a = kb.dve.inc(nc.vector.tensor_tensor(out=Ap[:, 1, 2 * pr:2 * pr + 2, :], in0=v2[2], in1=v2[3], op=ALU.add))
        src_done = (kb.pe, kb.pe.v)
        kb.wait("tensor", ev_a)
        kb.wait("vector", hq_ev)
        ev_y = None
        for b4 in range(CG // 4):
            c0 = b4 * 4
            br, bi = 2 + 2 * (b4 % 2), 3 + 2 * (b4 % 2)
            kb.wait("tensor", kb.bank_free[br])
            kb.wait("tensor", kb.bank_free[bi])
            rr = Ap[:, 0, c0:c0 + 4, :]
            ri_ = Ap[:, 1, c0:c0 + 4, :]
            nc.tensor.matmul(kb.banks[br][:, 0:512], fs3_sb[:, 0:128], rr, start=True, stop=False)
            nc.tensor.matmul(kb.banks[br][:, 0:512], fs3_sb[:, 128:256], ri_, start=False, stop=True)
            nc.tensor.matmul(kb.banks[bi][:, 0:512], fs3_sb[:, 256:384], rr, start=True, stop=False)
            ev_m = kb.pe.inc(nc.tensor.matmul(kb.banks[bi][:, 0:512], fs3_sb[:, 0:128], ri_, start=False, stop=True))
            kb.wait("vector", ev_m)
            if b4 == 0:
                kb.wait("vector", st["y2_free"])
            xr = kb.banks[br][:, 0:512].rearrange("p (s k) -> p s k", s=4)
            xi = kb.banks[bi][:, 0:512].rearrange("p (s k) -> p s k", s=4)
            hr = hq_t[:, 0, c0:c0 + 4, :]
            hi = hq_t[:, 1, c0:c0 + 4, :]
            nc.vector.tensor_tensor(out=v4[0], in0=xr, in1=hr, op=ALU.mult)
            nc.vector.tensor_tensor(out=v4[1], in0=xi, in1=hi, op=ALU.mult)
            nc.vector.tensor_tensor(out=v4[2], in0=xr, in1=hi, op=ALU.mult)
            ev_b = kb.dve.inc(nc.vector.tensor_tensor(out=v4[3], in0=xi, in1=hr, op=ALU.mult))
            kb.bank_free[br] = ev_b
            kb.bank_free[bi] = ev_b
            nc.vector.tensor_tensor(out=Y[:, 0, c0:c0 + 4, :], in0=v4[0], in1=v4[1], op=ALU.subtract)
            ev_y = kb.dve.inc(nc.vector.tensor_tensor(out=Y[:, 1, c0:c0 + 4, :], in0=v4[2], in1=v4[3], op=ALU.add))
        st["ap_free"] = (kb.pe, kb.pe.v)
        hq_done = ev_y
        kb.wait("tensor", ev_y)
        ev_bp = None
        for pr in range(CG // 2):
            bank = pr % 2
            kb.wait("tensor", kb.bank_free[bank])
            mm = None
            for j in range(2):
                c = 2 * pr + j
                nc.tensor.matmul(kb.banks[bank][:, j * 256:(j + 1) * 256], Y[:, 0, c, :], r12_sb[:, 0:256], start=True, stop=False)
                mm = nc.tensor.matmul(kb.banks[bank][:, j * 256:(j + 1) * 256], Y[:, 1, c, :], r12_sb[:, 256:512], start=False, stop=True)
            ev_m = kb.pe.inc(mm)
            kb.wait("vector", ev_m)
            if pr == 0:
                kb.wait("vector", st["bp_free"])
            psv = kb.banks[bank][:, 0:512].rearrange("p (s r k) -> p s r k", s=2, r=2)
            nc.vector.tensor_tensor(out=v2[0], in0=psv[:, :, 0, :], in1=twr2, op=ALU.mult)
            nc.vector.tensor_tensor(out=v2[1], in0=psv[:, :, 1, :], in1=twi2, op=ALU.mult)
            nc.vector.tensor_tensor(out=v2[2], in0=psv[:, :, 1, :], in1=twr2, op=ALU.mult)
            ev_b = kb.dve.inc(nc.vector.tensor_tensor(out=v2[3], in0=psv[:, :, 0, :], in1=twi2, op=ALU.mult))
            kb.bank_free[bank] = ev_b
            nc.vector.tensor_tensor(out=Bp[:, 0, 2 * pr:2 * pr + 2, :], in0=v2[0], in1=v2[1], op=ALU.add)
            ev_bp = kb.dve.inc(nc.vector.tensor_tensor(out=Bp[:, 1, 2 * pr:2 * pr + 2, :], in0=v2[2], in1=v2[3], op=ALU.subtract))
        st["y2_free"] = (kb.pe, kb.pe.v)
        kb.wait("tensor", ev_bp)
        for b4 in range(CG // 4):
            c0 = b4 * 4
            bank = 2 + b4 % 4
            kb.wait("tensor", kb.bank_free[bank])
            nc.tensor.matmul(kb.banks[bank][:, 0:512], lab_sb[:, 0:128], Bp[:, 0, c0:c0 + 4, :], start=True, stop=False)
            ev_m = kb.pe.inc(nc.tensor.matmul(kb.banks[bank][:, 0:512], lab_sb[:, 128:256], Bp[:, 1, c0:c0 + 4, :], start=False, stop=True))
            kb.wait("vector", ev_m)
            ev_g = gate(c0, kb.banks[bank][:, 0:512].rearrange("p (s k) -> p s k", s=4))
            kb.bank_free[bank] = ev_g
        st["bp_free"] = (kb.pe, kb.pe.v)
        return src_done, hq_done

    for g in range(NG):
        kb.wait("sync", st["p_free"])
        ev_p = ld_p.inc(nc.sync.dma_start(out=pt_[:].rearrange("p j c n -> p (j c n)"), in_=p3[g]), 16)
        hq_evs = []
        for o in range(2):
            kb.wait("scalar", st["hq_free"][o])
            src = Hq[o].rearrange("p (r c k) -> p r c k", r=2, c=128)[:, :, g * CG:(g + 1) * CG, :]
            hq_evs.append(ld_h.inc(nc.scalar.dma_start(out=hq[o][:, :, :, :], in_=src), 16))
        hq_evs[0] = hq_evs[1]
        kb.wait("gpsimd", ev_p)
        kb.wait("gpsimd", st["u_free"])
        kb.wait("gpsimd", st["y_free"])
        ev_u = None
        for j in range(3):
            wv = lambda tp: cw_sb[:, tp, j, g * CG:(g + 1) * CG].unsqueeze(2).to_broadcast([128, CG, 128])
            nc.gpsimd.tensor_tensor(out=uf[:], in0=pt_[:, j, :, 0:128], in1=wv(0), op=ALU.mult)
            nc.gpsimd.tensor_tensor(out=uf2[:], in0=pt_[:, j, :, 1:129], in1=wv(1), op=ALU.mult)
            nc.gpsimd.tensor_tensor(out=uf[:], in0=uf[:], in1=uf2[:], op=ALU.add)
            nc.gpsimd.tensor_tensor(out=uf2[:], in0=pt_[:, j, :, 2:130], in1=wv(2), op=ALU.mult)
            nc.gpsimd.tensor_tensor(out=uf[:], in0=uf[:], in1=uf2[:], op=ALU.add)
            cbv = cb_sb[:, j, g * CG:(g + 1) * CG].unsqueeze(2).to_broadcast([128, CG, 128])
            if j == 0:
                nc.gpsimd.tensor_tensor(out=uf[:], in0=uf[:], in1=cbv, op=ALU.add)
                nc.gpsimd.tensor_scalar(out=u[:, 0, :, :], in0=uf[:], scalar1=mask_sb[:, 0:1], scalar2=None, op0=ALU.mult)
                fbv = fb_sb[:, 0, g * CG:(g + 1) * CG].unsqueeze(2).to_broadcast([128, CG, 128])
                nc.gpsimd.tensor_scalar(out=uf[:], in0=uf[:], scalar1=mask_sb[:, 0:1], scalar2=None, op0=ALU.mult)
                ev_u = kb.pool.inc(nc.gpsimd.tensor_tensor(out=vfb[:], in0=uf[:], in1=fbv, op=ALU.mult))
                ev_v = ev_u
            else:
                ev_u = kb.pool.inc(nc.gpsimd.tensor_tensor(out=u[:, j, :, :], in0=uf[:], in1=cbv, op=ALU.add))
        st["p_free"] = ev_u

        def gate1(c0, ps):
            kb.wait("vector", ev_u)
            if c0 == 0:
                kb.wait("vector", st["z_pe_free"])
            nc.vector.tensor_tensor(out=v4[0], in0=ps, in1=vfb[:, c0:c0 + 4, :], op=ALU.add)
            return kb.dve.inc(nc.vector.scalar_tensor_tensor(out=z[:, c0:c0 + 4, :], in0=v4[0], scalar=mask_sb[:, 0:1],
                                                             in1=u[:, 1, c0:c0 + 4, :], op0=ALU.mult, op1=ALU.mult))

        src_done, hq_done = fft_conv(lambda c: u[:, 0, c, :], ev_v, hq[0], hq_evs[0], gate1)
        st["hq_free"][0] = hq_done
        ev_z = (kb.dve, kb.dve.v)
        kb.wait("gpsimd", ev_z)
        fbv1 = fb_sb[:, 1, g * CG:(g + 1) * CG].unsqueeze(2).to_broadcast([128, CG, 128])
        ev_zf = kb.pool.inc(nc.gpsimd.tensor_tensor(out=vfb[:], in0=z[:], in1=fbv1, op=ALU.mult))

        def gate2(c0, ps):
            kb.wait("vector", ev_zf)
            if c0 == 0:
                kb.wait("vector", st["y_free"])
            nc.vector.tensor_tensor(out=v4[0], in0=ps, in1=vfb[:, c0:c0 + 4, :], op=ALU.add)
            return kb.dve.inc(nc.vector.tensor_tensor(out=ystage[:, c0:c0 + 4, :], in0=v4[0], in1=u[:, 2, c0:c0 + 4, :], op=ALU.mult))

        src_done2, hq_done2 = fft_conv(lambda c: z[:, c, :], ev_z, hq[1], hq_evs[1], gate2)
        st["hq_free"][1] = hq_done2
        st["z_pe_free"] = src_done2
        ev_y = (kb.dve, kb.dve.v)
        st["u_free"] = ev_y
        kb.wait("sync", ev_y)
        e = st_y.inc(nc.sync.dma_start(out=yTL[:, g * CG * 128:(g + 1) * CG * 128].rearrange("p (c n) -> p c n", n=128), in_=ystage), 16)
        st["y_free"] = e
        kb.out_evs.append(e)
    return kb.finish()


import math

_PROGS = {}
_CONSTS = {}


def _prog(key, fn):
    if key not in _PROGS:
        _PROGS[key] = fn()
    return _PROGS[key]


def hyena_tables(L):
    key = ("hy", L)
    if key in _CONSTS:
        return _CONSTS[key]
    f32 = np.float32
    jp = np.arange(NFFT)
    pos = np.zeros(NFFT, np.int64)
    mF = np.zeros(NFFT, f32)
    mB = np.zeros(NFFT, f32)
    fw = jp < L
    pos[fw] = jp[fw]
    mF[fw] = 1
    bw = jp > NFFT - L
    pos[bw] = NFFT - jp[bw]
    mB[bw] = 1
    mB[0] = 1
    t = np.linspace(0.0, 1.0, L, dtype=f32)
    w = (f32(2.0 * math.pi) * np.arange(L, dtype=f32) / f32(L)).astype(f32)
    bands = np.linspace(1e-4, 15, 16, dtype=f32)
    feats = np.concatenate([t[:, None], np.cos(bands * w[:, None]), -np.sin(bands * w[:, None])], -1).astype(f32)
    featsT = np.ascontiguousarray(feats[pos].T)
    min_decay = math.log(1e-2) / 1.5
    max_decay = math.log(1e-2) / 0.3
    deltas = np.abs(np.linspace(min_decay, max_decay, D, dtype=f32))
    decay = np.exp(-t[:, None] * deltas).astype(f32)
    dec_pos = decay[pos]
    dF = (dec_pos * mF[:, None]).reshape(128, 128, D)
    dB = (dec_pos * mB[:, None]).reshape(128, 128, D)
    out = (featsT, dF, dB)
    _CONSTS[key] = out
    return out


def _run(nc, in_maps):
    res = run_bass_kernel_spmd(nc, in_maps, core_ids=list(range(NCORES)))
    return res.results


def run_hf(inp, j, L):
    nc = _prog("hf", build_hf)
    tabs = dft_tables()
    featsT, dF, dB = hyena_tables(L)
    fvec = np.ascontiguousarray(np.stack([inp["hy_f_freq"][j], inp["hy_f_b1"][j], inp["hy_f_b2"][j], inp["hy_f_b3"][j]], 1))
    w23 = np.ascontiguousarray(np.concatenate([inp["hy_f_w2"][j], inp["hy_f_w3"][j]], 1))
    wo = inp["hy_f_wout"][j].reshape(64, 2, 2, D)
    in_maps = []
    for i in range(NCORES):
        ws = wo[:, :, :, 128 * i:128 * (i + 1)].reshape(64, 2, 2, NG, CG).transpose(0, 3, 1, 2, 4)
        dFi = dF[:, :, 128 * i:128 * (i + 1)].reshape(128, 128, NG, CG).transpose(2, 0, 1, 3)
        dBi = dB[:, :, 128 * i:128 * (i + 1)].reshape(128, 128, NG, CG).transpose(2, 0, 1, 3)
        in_maps.append({"featsT": featsT, "w1": np.ascontiguousarray(inp["hy_f_w1"][j]), "w23": w23, "fvec": fvec,
                        "wout": np.ascontiguousarray(ws).reshape(64, NG * 128),
                        "decF": np.ascontiguousarray(dFi).reshape(NG, 128, 128 * CG),
                        "decB": np.ascontiguousarray(dBi).reshape(NG, 128, 128 * CG),
                        "FT1": tabs["FT1"], "TW": tabs["TW"], "FS3": tabs["FS3"]})
    return [r["Hq"] for r in _run(nc, in_maps)]


def run_hc(inp, j, P, L, hqs):
    nc = _prog("hc", build_hc)
    tabs = dft_tables()
    nrow = L // 128
    Ppad = np.zeros((2, 3 * D, L + 2), NPBF)
    Ppad[:, :, 1:L + 1] = P
    idx = (np.arange(nrow) * 128)[:, None] + np.arange(130)[None, :]
    TL = Ppad[:, :, idx]
    mask = np.zeros((2, 64), np.float32)
    mask[:, :nrow] = 1.0
    mask = mask.reshape(128, 1)
    in_maps = []
    for i in range(NCORES):
        p3 = np.zeros((NG, 2, 64, 3, CG, 130), NPBF)
        blk = TL.reshape(2, 3, D, nrow, 130)[:, :, 128 * i:128 * (i + 1)]
        blk = blk.reshape(2, 3, NG, CG, nrow, 130).transpose(2, 0, 4, 1, 3, 5)
        p3[:, :, :nrow] = blk
        rep = lambda a: np.ascontiguousarray(np.broadcast_to(a.reshape(1, -1), (128, a.size))).astype(np.float32)
        cwi = inp["hy_conv_w"][j].reshape(3, 3, D)[:, :, 128 * i:128 * (i + 1)]
        cbi = inp["hy_conv_b"][j].reshape(3, D)[:, 128 * i:128 * (i + 1)]
        fbi = inp["hy_f_bias"][j][:, 128 * i:128 * (i + 1)]
        in_maps.append({"p3": p3.reshape(NG, 128, 3 * CG * 130), "cw": rep(cwi), "cb": rep(cbi), "fb": rep(fbi), "mask": mask,
                        "Hq": hqs[i], "FS1": tabs["FS1"], "TW": tabs["TW"], "FS3": tabs["FS3"], "R12": tabs["R12"],
                        "LAB": tabs["LAB"]})
    res = _run(nc, in_maps)
    y = np.zeros((2, L, D), NPBF)
    for i in range(NCORES):
        yt = np.asarray(res[i]["yTL"]).reshape(2, 64, 128, 128)[:, :nrow]
        y[:, :, 128 * i:128 * (i + 1)] = yt.transpose(0, 1, 3, 2).reshape(2, L, 128)
    return y


def _ts_tokens(i, lat, ctx):
    b, q = i // 4, i % 4
    xs = np.concatenate([lat[b, q * T_LAT:(q + 1) * T_LAT], ctx[b, q * T_CTX:(q + 1) * T_CTX]], 0)
    return np.ascontiguousarray(xs.T)


def _vec8(v):
    return np.ascontiguousarray(np.asarray(v, np.float32).reshape(DC, 128).T)


def _ts_post_inputs(inp, i, l):
    b = i // 4
    cc = np.stack([inp["c"][b], inp["c_ctx"]], 0)
    cT = np.ascontiguousarray(cc.reshape(2, DC, 128).transpose(2, 1, 0)).reshape(128, 2 * DC)
    m = {"cT": cT, "w_mod": inp["w_mod"][l], "b_mod": np.ascontiguousarray(inp["b_mod"][l].reshape(72, 128).T),
         "norm_w": np.ascontiguousarray(inp["norm_w"][l].reshape(3, DC, 128).transpose(2, 0, 1)).reshape(128, 3 * DC),
         "w_gu1": inp["ffn_w_gate_up"][l, 0], "w_dn1": inp["ffn_w_down"][l, 0]}
    if l % 2 == 1:
        j = l // 2
        m["w_in"] = inp["hy_w_in"][j]
        m["b_in"] = np.ascontiguousarray(inp["hy_b_in"][j].reshape(24, 128).T)
    return m


def _run_ts(inp, l_pre, l_post, xT, oT, modp):
    pre = l_pre is not None
    post = None if l_post is None else ("attn" if l_post % 2 == 0 else "hyena")
    nc = _prog(("ts", pre, post), lambda: build_ts(pre, post))
    in_maps = []
    for i in range(NCORES):
        m = {"xT_in": xT[i]}
        if pre:
            if l_pre % 2 == 0:
                w_o, b_o = inp["attn_w_o"][l_pre // 2], np.zeros(D, np.float32)
            else:
                w_o, b_o = inp["hy_w_out"][l_pre // 2], inp["hy_b_out"][l_pre // 2]
            m.update({"oT_in": oT[i], "w_o": w_o, "b_o": _vec8(b_o), "modp_in": modp[i],
                      "w_gu2": inp["ffn_w_gate_up"][l_pre, 1], "w_dn2": inp["ffn_w_down"][l_pre, 1]})
        if post:
            m.update(_ts_post_inputs(inp, i, l_post))
        in_maps.append(m)
    return _run(nc, in_maps)


def _run_attn(inp, a, hTs):
    nc = _prog("attn", build_attn)
    if "attn" not in _CONSTS:
        _CONSTS["attn"] = attn_consts()
    cosT, sinT, rotm, blk = _CONSTS["attn"]
    wqkv = inp["attn_w_qkv"][a]
    nrm = np.ascontiguousarray(np.stack([np.tile(inp["attn_q_norm"][a], 2), np.tile(inp["attn_k_norm"][a], 2)], 1).astype(np.float32))
    Hb = []
    for b in range(2):
        parts = [np.asarray(hTs[4 * b + q]) for q in range(4)]
        Hb.append(np.ascontiguousarray(np.concatenate([p[:, T_LAT:] for p in parts] + [p[:, :T_LAT] for p in parts], 1)))
    in_maps = []
    for i in range(NCORES):
        b, g = i // 4, i % 4
        wk = wqkv[:, 1024 + 64 * g:1024 + 64 * (g + 1)]
        in_maps.append({"hT": Hb[b], "wq": np.ascontiguousarray(wqkv[:, 256 * g:256 * (g + 1)]),
                        "wkk": np.ascontiguousarray(np.concatenate([wk, wk], 1)),
                        "wv": np.ascontiguousarray(wqkv[:, 1280 + 64 * g:1280 + 64 * (g + 1)]),
                        "nrm": nrm, "cosT": cosT, "sinT": sinT, "rotm": rotm, "blk": blk})
    res = _run(nc, in_maps)
    oT = []
    for b in range(2):
        Ob = np.concatenate([np.asarray(res[4 * b + g]["oT"]) for g in range(4)], 0)
        for q in range(4):
            oT.append(np.ascontiguousarray(np.concatenate([Ob[:, CTX + q * T_LAT:CTX + (q + 1) * T_LAT],
                                                           Ob[:, q * T_CTX:(q + 1) * T_CTX]], 1)))
    return oT


def _run_hyena(inp, j, pTs, with_ctx):
    P_lat = np.zeros((2, 3 * D, SEQ), NPBF)
    P_ctx = np.zeros((2, 3 * D, CTX), NPBF)
    for i in range(NCORES):
        b, q = i // 4, i % 4
        p = np.asarray(pTs[i])
        P_lat[b, :, q * T_LAT:(q + 1) * T_LAT] = p[:, :T_LAT]
        P_ctx[b, :, q * T_CTX:(q + 1) * T_CTX] = p[:, T_LAT:]
    hq = run_hf(inp, j, SEQ)
    y_lat = run_hc(inp, j, P_lat, SEQ, hq)
    if with_ctx:
        hqc = run_hf(inp, j, CTX)
        y_ctx = run_hc(inp, j, P_ctx, CTX, hqc)
    else:
        y_ctx = np.zeros((2, CTX, D), NPBF)
    return [_ts_tokens(i, y_lat, y_ctx) for i in range(NCORES)]


def kernel(**inp):
    inp = {k: np.asarray(v) for k, v in inp.items()}
    xT = [_ts_tokens(i, inp["x"], inp["ctx"]) for i in range(NCORES)]
    oT = None
    modp = None
    for l in range(4):
        res = _run_ts(inp, l - 1 if l > 0 else None, l, xT, oT, modp)
        xT = [r["xT_out"] for r in res]
        modp = [r["mod_out"] for r in res]
        if l % 2 == 0:
            oT = _run_attn(inp, l // 2, [r["hT_out"] for r in res])
        else:
            oT = _run_hyena(inp, l // 2, [r["pT_out"] for r in res], with_ctx=(l < 3))
    res = _run_ts(inp, 3, None, xT, oT, modp)
    out = np.zeros((2, SEQ, D), np.float32)
    for i in range(NCORES):
        b, q = i // 4, i % 4
        out[b, q * T_LAT:(q + 1) * T_LAT] = np.asarray(res[i]["xT_out"])[:, :T_LAT].T
    return out
```

```python
import numpy as np
import ml_dtypes
from contextlib import ExitStack
import concourse.bass as bass
import concourse.mybir as mybir
from concourse.bass_utils import run_bass_kernel_spmd

F32 = mybir.dt.float32
BF16 = mybir.dt.bfloat16
AF = mybir.ActivationFunctionType
ALU = mybir.AluOpType
NPBF = ml_dtypes.bfloat16

D = 1024
DC = 8
FF = 2816
FC = 22
SEQ = 8192
CTX = 256
NCORES = 8
T_LAT = 2048
T_CTX = 64
T = T_LAT + T_CTX
TILES = [(0, 512, 0), (512, 512, 0), (1024, 512, 0), (1536, 512, 0), (2048, 64, 1)]
EPS = 1e-6


class Track:
    def __init__(self, kb, name):
        self.name = name
        self.h = kb.es.enter_context(kb.nc.semaphore(name))
        self.v = 0

    def inc(self, instr, amt=1):
        instr.then_inc(self.h, amt)
        self.v += amt
        return (self, self.v)


class KB:
    def __init__(self, wide_banks=0):
        self.nc = bass.Bass("TRN2", target_bir_lowering=False)
        self.es = ExitStack()
        self.waited = {}
        self.pe = Track(self, "t_pe")
        self.act = Track(self, "t_act")
        self.dve = Track(self, "t_dve")
        self.pool = Track(self, "t_pool")
        self.wide = [self.es.enter_context(self.nc.psum_tensor(f"wbank{i}", [128, 1024], F32)) for i in range(wide_banks)]
        self.banks = []
        for w in self.wide:
            self.banks += [w[:, 0:512], w[:, 512:1024]]
        self.banks += [self.es.enter_context(self.nc.psum_tensor(f"bank{i}", [128, 512], F32))
                       for i in range(2 * wide_banks, 8)]
        self.bank_free = [None] * 8
        self.out_evs = []

    def dram(self, name, shape, dt, kind):
        return self.nc.dram_tensor(name, list(shape), dt, kind=kind).ap()

    def sb(self, name, shape, dt=F32):
        return self.es.enter_context(self.nc.sbuf_tensor(name, list(shape), dt))

    def track(self, name):
        return Track(self, name)

    def wait(self, engname, ev):
        if ev is None:
            return
        tr, v = ev
        key = (engname, tr.name)
        if self.waited.get(key, 0) >= v:
            return
        self.waited[key] = v
        getattr(self.nc, engname).wait_ge(tr.h, v)

    def finish(self):
        for ev in self.out_evs:
            self.wait("sync", ev)
        self.es.close()
        return self.nc


class WStream:
    def __init__(self, kb, nslots=3, width=2048):
        self.kb = kb
        self.n = nslots
        self.buf = [kb.sb(f"wbuf{i}", [128, width], BF16) for i in range(nslots)]
        self.ld = [kb.track(f"t_wld{i}") for i in range(nslots)]
        self.free_ev = [None] * nslots
        self.cnt = 0

    def load(self, srcs):
        kb = self.kb
        s = self.cnt % self.n
        self.cnt += 1
        kb.wait("gpsimd", self.free_ev[s])
        ev = None
        for dst_fn, src in srcs:
            ev = self.ld[s].inc(kb.nc.gpsimd.dma_start(out=dst_fn(self.buf[s]), in_=src), 16)
        return s, ev

    def release(self, s, ev):
        self.free_ev[s] = ev


def dense(kb, ws, KC, rhs_fn, units, evac, tiles, G=1, banks=(0, 1, 2, 3), pre_ev=None):
    nc = kb.nc
    nb = len(banks) // G
    pend = []

    def issue(ui):
        srcs = []
        for g in range(G):
            srcs.append((lambda b, g=g: b[:, g * KC * 128:(g + 1) * KC * 128].rearrange("p (k m) -> p k m", m=128),
                         units[ui][g].rearrange("(k p) m -> p k m", p=128)))
        return ws.load(srcs)

    PF = ws.n - 1
    for ui in range(min(PF, len(units))):
        pend.append(issue(ui))
    it = 0
    for ui in range(len(units)):
        s, ld_ev = pend.pop(0)
        kb.wait("tensor", ld_ev)
        kb.wait("tensor", pre_ev)
        wv = ws.buf[s]
        last_ev = None
        for ti, (t0, tn, vec) in enumerate(tiles):
            bsel = [banks[(it % nb) * G + g] for g in range(G)]
            it += 1
            for g in range(G):
                kb.wait("tensor", kb.bank_free[bsel[g]])
            mm = None
            for g in range(G):
                for kc in range(KC):
                    mm = nc.tensor.matmul(kb.banks[bsel[g]][:, 0:tn],
                                          wv[:, (g * KC + kc) * 128:(g * KC + kc + 1) * 128],
                                          rhs_fn(kc, ti), start=(kc == 0), stop=(kc == KC - 1))
            pe_ev = kb.pe.inc(mm)
            last_ev = pe_ev
            fe = evac(ui, ti, [kb.banks[b][:, 0:tn] for b in bsel], pe_ev)
            for b in bsel:
                kb.bank_free[b] = fe
        ws.release(s, last_ev)
        if ui + PF < len(units):
            pend.append(issue(ui + PF))


def build_ts(pre, post):
    kb = KB()
    nc = kb.nc
    IN, OUT = "ExternalInput", "ExternalOutput"
    xT_in = kb.dram("xT_in", [D, T], F32, IN)
    xT_out = kb.dram("xT_out", [D, T], F32, OUT)
    if pre:
        oT_in = kb.dram("oT_in", [D, T], BF16, IN)
        w_o = kb.dram("w_o", [D, D], F32, IN)
        b_o = kb.dram("b_o", [128, DC], F32, IN)
        modp_in = kb.dram("modp_in", [128, DC * 18], F32, IN)
        w_gu2 = kb.dram("w_gu2", [D, 2 * FF], F32, IN)
        w_dn2 = kb.dram("w_dn2", [FF, D], F32, IN)
    if post:
        cT = kb.dram("cT", [128, DC * 2], F32, IN)
        w_mod = kb.dram("w_mod", [D, 9 * D], F32, IN)
        b_mod = kb.dram("b_mod", [128, 72], F32, IN)
        norm_w = kb.dram("norm_w", [128, 3 * DC], F32, IN)
        w_gu1 = kb.dram("w_gu1", [D, 2 * FF], F32, IN)
        w_dn1 = kb.dram("w_dn1", [FF, D], F32, IN)
        mod_out = kb.dram("mod_out", [128, DC * 18], F32, OUT)
        if post == "attn":
            hT_out = kb.dram("hT_out", [D, T], BF16, OUT)
        else:
            w_in = kb.dram("w_in", [D, 3 * D], F32, IN)
            b_in = kb.dram("b_in", [128, 24], F32, IN)
            pT_out = kb.dram("pT_out", [3 * D, T], BF16, OUT)

    xT = kb.sb("xT", [128, DC, T], F32)
    hT = kb.sb("hT", [128, DC, T], BF16)
    aT = kb.sb("aT", [128, 11, T], BF16)
    sqb = kb.sb("sqb", [128, DC, 512], BF16)
    rstd = kb.sb("rstd", [128, 512], F32)
    tmp = [kb.sb(f"tmp{i}", [128, 512], F32) for i in range(2)]
    sg = [kb.sb(f"sg{i}", [128, 512], F32) for i in range(2)]
    ones = kb.sb("ones", [128, 128], BF16)
    modt = kb.sb("modt", [128, DC, 3, 3, 2], F32)
    modp = kb.sb("modp", [128, DC, 3, 3, 2], F32)
    bo_t = kb.sb("bo_t", [128, DC], F32)
    ws = WStream(kb)
    ld = kb.track("t_ld")
    ld_m = kb.track("t_ldm")
    ld_o = kb.track("t_ldo")
    ld_c = kb.track("t_ldc")
    ld_b = kb.track("t_ldb")
    st = kb.track("t_st")
    st_slot = [kb.track("t_st0"), kb.track("t_st1")]

    ev_ones = kb.pool.inc(nc.gpsimd.memset(ones[:], 1.0))
    epsb = kb.sb("epsb", [128, 1], F32)
    ev_eps = kb.pool.inc(nc.gpsimd.memset(epsb[:], EPS))
    ev_x = None
    for c in range(DC):
        eng = nc.sync if c % 2 == 0 else nc.scalar
        ev_x = ld.inc(eng.dma_start(out=xT[:, c, :], in_=xT_in[c * 128:(c + 1) * 128, :]), 16)
    x_ready = {"vector": ev_x, "scalar": ev_x}

    tmp_free = [None, None]
    sg_free = [None, None]
    state = {"x_ev": ev_x, "sqb_free": None, "stat_free": None, "tmpi": 0, "sgi": 0, "rstd_free": None}

    def norm_mod(mt, k, h_free_ev):
        last = None
        for ti, (t0, tn, vec) in enumerate(TILES):
            kb.wait("scalar", state["x_ev"])
            kb.wait("scalar", state["sqb_free"])
            ev_sq = None
            for c in range(DC):
                ev_sq = kb.act.inc(nc.scalar.activation(out=sqb[:, c, 0:tn], in_=xT[:, c, t0:t0 + tn], func=AF.Square))
            kb.wait("tensor", ev_sq)
            kb.wait("tensor", ev_ones)
            kb.wait("tensor", kb.bank_free[4])
            mm = None
            for c in range(DC):
                mm = nc.tensor.matmul(kb.banks[4][:, 0:tn], ones[:, :], sqb[:, c, 0:tn], start=(c == 0), stop=(c == DC - 1))
            ev_stat = kb.pe.inc(mm)
            state["sqb_free"] = ev_stat
            kb.wait("scalar", ev_stat)
            kb.wait("scalar", ev_eps)
            kb.wait("scalar", state["rstd_free"])
            ev_sd = kb.act.inc(nc.scalar.activation(out=rstd[:, 0:tn], in_=kb.banks[4][:, 0:tn], func=AF.Sqrt,
                                                    bias=epsb[:, 0:1], scale=1.0 / D))
            kb.bank_free[4] = ev_sd
            kb.wait("vector", ev_sd)
            kb.wait("vector", state["x_ev"])
            ev_r = kb.dve.inc(nc.vector.reciprocal(out=rstd[:, 0:tn], in_=rstd[:, 0:tn]))
            for c in range(DC):
                i = state["tmpi"] % 2
                state["tmpi"] += 1
                kb.wait("vector", tmp_free[i])
                ev_t = kb.dve.inc(nc.vector.scalar_tensor_tensor(
                    out=tmp[i][:, 0:tn], in0=xT[:, c, t0:t0 + tn], scalar=mt[:, c, k, 0, vec:vec + 1],
                    in1=rstd[:, 0:tn], op0=ALU.mult, op1=ALU.mult))
                kb.wait("scalar", ev_t)
                kb.wait("scalar", h_free_ev)
                last = kb.act.inc(nc.scalar.activation(out=hT[:, c, t0:t0 + tn], in_=tmp[i][:, 0:tn], func=AF.Identity,
                                                       bias=mt[:, c, k, 1, vec:vec + 1], scale=1.0))
                tmp_free[i] = last
            state["rstd_free"] = (kb.dve, kb.dve.v)
        return last

    def ffn(mt, k, w_gu, w_dn, h_ev):
        wg = w_gu
        pe_last = None
        for hf in range(2):
            j0 = hf * 11
            units = [[wg[:, (j0 + j) * 128:(j0 + j + 1) * 128], wg[:, FF + (j0 + j) * 128:FF + (j0 + j + 1) * 128]]
                     for j in range(11)]
            a_free = pe_last
            dve_last = [None]

            def evac_up(ui, ti, ps, pe_ev):
                t0, tn, vec = TILES[ti]
                i = state["sgi"] % 2
                state["sgi"] += 1
                kb.wait("scalar", pe_ev)
                kb.wait("scalar", sg_free[i])
                ev_s = kb.act.inc(nc.scalar.activation(out=sg[i][:, 0:tn], in_=ps[0], func=AF.Silu))
                kb.wait("vector", ev_s)
                kb.wait("vector", a_free)
                ev_d = kb.dve.inc(nc.vector.tensor_tensor(out=aT[:, ui, t0:t0 + tn], in0=sg[i][:, 0:tn], in1=ps[1], op=ALU.mult))
                sg_free[i] = ev_d
                dve_last[0] = ev_d
                return ev_d

            dense(kb, ws, DC, lambda kc, ti: hT[:, kc, TILES[ti][0]:TILES[ti][0] + TILES[ti][1]], units, evac_up, TILES,
                  G=2, banks=(0, 1, 2, 3), pre_ev=h_ev)

            units_d = [[w_dn[j0 * 128:(j0 + 11) * 128, m * 128:(m + 1) * 128]] for m in range(DC)]
            x_last = [None]

            def evac_dn(ui, ti, ps, pe_ev):
                t0, tn, vec = TILES[ti]
                kb.wait("vector", pe_ev)
                ev = kb.dve.inc(nc.vector.scalar_tensor_tensor(
                    out=xT[:, ui, t0:t0 + tn], in0=ps[0], scalar=mt[:, ui, k, 2, vec:vec + 1],
                    in1=xT[:, ui, t0:t0 + tn], op0=ALU.mult, op1=ALU.add))
                x_last[0] = ev
                return ev

            dense(kb, ws, 11, lambda kc, ti: aT[:, kc, TILES[ti][0]:TILES[ti][0] + TILES[ti][1]], units_d, evac_dn, TILES,
                  G=1, banks=(0, 1, 2, 3), pre_ev=dve_last[0])
            pe_last = (kb.pe, kb.pe.v)
            state["x_ev"] = x_last[0]
        return pe_last

    h_free = None

    if pre:
        ev_m = ld_m.inc(nc.sync.dma_start(out=modp[:].rearrange("p a b c d -> p (a b c d)"), in_=modp_in), 16)
        ev_m = ld_m.inc(nc.sync.dma_start(out=bo_t[:], in_=b_o), 16)
        ev_o = None
        for c in range(DC):
            eng = nc.sync if c % 2 == 0 else nc.scalar
            ev_o = ld_o.inc(eng.dma_start(out=hT[:, c, :], in_=oT_in[c * 128:(c + 1) * 128, :]), 16)
        units = [[w_o[:, m * 128:(m + 1) * 128]] for m in range(DC)]
        x_last = [None]

        def evac_o(ui, ti, ps, pe_ev):
            t0, tn, vec = TILES[ti]
            i = state["tmpi"] % 2
            state["tmpi"] += 1
            kb.wait("scalar", pe_ev)
            kb.wait("scalar", ev_m)
            kb.wait("scalar", tmp_free[i])
            ev_a = kb.act.inc(nc.scalar.activation(out=tmp[i][:, 0:tn], in_=ps[0], func=AF.Identity,
                                                   bias=bo_t[:, ui:ui + 1], scale=1.0))
            kb.wait("vector", ev_a)
            kb.wait("vector", ev_m)
            kb.wait("vector", ev_x)
            ev = kb.dve.inc(nc.vector.scalar_tensor_tensor(
                out=xT[:, ui, t0:t0 + tn], in0=tmp[i][:, 0:tn], scalar=modp[:, ui, 1, 2, vec:vec + 1],
                in1=xT[:, ui, t0:t0 + tn], op0=ALU.mult, op1=ALU.add))
            tmp_free[i] = ev
            x_last[0] = ev
            return ev_a

        dense(kb, ws, DC, lambda kc, ti: hT[:, kc, TILES[ti][0]:TILES[ti][0] + TILES[ti][1]], units, evac_o, TILES,
              G=1, banks=(0, 1, 2, 3), pre_ev=ev_o)
        state["x_ev"] = x_last[0]
        h_free = (kb.pe, kb.pe.v)
        h_ev = norm_mod(modp, 2, h_free)
        h_free = ffn(modp, 2, w_gu2, w_dn2, h_ev)

    if post:
        c_sb = kb.sb("c_sb", [128, DC, 2], F32)
        sc = kb.sb("sc", [128, DC, 2], F32)
        bm = kb.sb("bm", [128, 72], F32)
        nw = kb.sb("nw", [128, 3, DC], F32)
        raw = kb.sb("raw", [128, 9, DC, 2], F32)
        wm = [kb.sb(f"wm{i}", [128, DC, 128], F32) for i in range(3)]
        wm_ld = [kb.track(f"t_wm{i}") for i in range(3)]
        wm_free = [None] * 3
        ev_c = ld_c.inc(nc.sync.dma_start(out=c_sb[:].rearrange("p a b -> p (a b)"), in_=cT), 16)
        ev_c = ld_c.inc(nc.sync.dma_start(out=bm[:], in_=b_mod), 16)
        ev_c = ld_c.inc(nc.sync.dma_start(out=nw[:].rearrange("p a b -> p (a b)"), in_=norm_w), 16)
        kb.wait("scalar", ev_c)
        ev_sc = kb.act.inc(nc.scalar.activation(out=sc[:], in_=c_sb[:], func=AF.Silu))
        kb.wait("tensor", ev_sc)
        kb.wait("tensor", kb.bank_free[5])
        psm = kb.banks[5]
        mm = None
        for n in range(72):
            s = n % 3
            kb.wait("sync", wm_free[s])
            ev_w = wm_ld[s].inc(nc.sync.dma_start(out=wm[s][:], in_=w_mod[:, n * 128:(n + 1) * 128].rearrange("(k p) m -> p k m", p=128)), 16)
            kb.wait("tensor", ev_w)
            for kc in range(DC):
                mm = nc.tensor.matmul(psm[:, 2 * n:2 * n + 2], wm[s][:, kc, :], sc[:, kc, :], start=(kc == 0), stop=(kc == DC - 1))
            wm_free[s] = kb.pe.inc(mm)
        ev_pm = wm_free[(72 - 1) % 3]
        kb.wait("vector", ev_pm)
        kb.wait("vector", ev_c)
        nc.vector.tensor_tensor(out=raw[:].rearrange("p m c v -> p (m c) v"),
                                in0=psm[:, 0:144].rearrange("p (n v) -> p n v", v=2),
                                in1=bm[:].unsqueeze(2).to_broadcast([128, 72, 2]), op=ALU.add)
        for k in range(3):
            nc.vector.scalar_tensor_tensor(out=modt[:, :, k, 0, :], in0=raw[:, 3 * k + 1, :, :], scalar=1.0,
                                           in1=nw[:, k, :].unsqueeze(2).to_broadcast([128, DC, 2]),
                                           op0=ALU.add, op1=ALU.mult)
            nc.vector.tensor_copy(out=modt[:, :, k, 1, :], in_=raw[:, 3 * k, :, :])
            ev_mt = kb.dve.inc(nc.vector.tensor_scalar(out=modt[:, :, k, 2, :], in0=raw[:, 3 * k + 2, :, :],
                                                       scalar1=(1.0 if k == 1 else 0.5), scalar2=None, op0=ALU.mult))
        kb.bank_free[5] = ev_mt
        kb.wait("scalar", ev_mt)
        kb.wait("sync", ev_mt)
        kb.out_evs.append(st.inc(nc.sync.dma_start(out=mod_out, in_=modt[:].rearrange("p a b c d -> p (a b c d)")), 16))

        h_ev = norm_mod(modt, 0, h_free)
        h_free = ffn(modt, 0, w_gu1, w_dn1, h_ev)
        h_ev = norm_mod(modt, 1, h_free)
        if post == "attn":
            kb.wait("sync", h_ev)
            for c in range(DC):
                eng = nc.sync
                kb.out_evs.append(st.inc(eng.dma_start(out=hT_out[c * 128:(c + 1) * 128, :], in_=hT[:, c, :]), 16))
        else:
            bi = kb.sb("bi", [128, 24], F32)
            ev_bi = ld_b.inc(nc.sync.dma_start(out=bi[:], in_=b_in), 16)
            units = [[w_in[:, m * 128:(m + 1) * 128]] for m in range(24)]
            st_free = {}

            def evac_p(ui, ti, ps, pe_ev):
                t0, tn, vec = TILES[ti]
                slot = ui % 2
                kb.wait("scalar", pe_ev)
                kb.wait("scalar", ev_bi)
                if ti == 0:
                    kb.wait("scalar", st_free.get(slot))
                ev_a = kb.act.inc(nc.scalar.activation(out=aT[:, slot, t0:t0 + tn], in_=ps[0], func=AF.Identity,
                                                       bias=bi[:, ui:ui + 1], scale=1.0))
                if ti == len(TILES) - 1:
                    kb.wait("sync", ev_a)
                    e = st_slot[slot].inc(nc.sync.dma_start(out=pT_out[ui * 128:(ui + 1) * 128, :], in_=aT[:, slot, :]), 16)
                    st_free[slot] = e
                    kb.out_evs.append(e)
                return ev_a

            kb.wait("scalar", h_free)
            dense(kb, ws, DC, lambda kc, ti: hT[:, kc, TILES[ti][0]:TILES[ti][0] + TILES[ti][1]], units, evac_p, TILES,
                  G=1, banks=(0, 1, 2, 3), pre_ev=h_ev)

    kb.wait("sync", state["x_ev"])
    for c in range(DC):
        kb.out_evs.append(st.inc(nc.sync.dma_start(out=xT_out[c * 128:(c + 1) * 128, :], in_=xT[:, c, :]), 16))
    return kb.finish()


NTOK = CTX + SEQ
ATILES = [(0, 256)] + [(256 + 512 * i, 512) for i in range(16)]
NKC = NTOK // 128


def build_attn():
    kb = KB(wide_banks=3)
    nc = kb.nc
    IN, OUT = "ExternalInput", "ExternalOutput"
    hT = kb.dram("hT", [D, NTOK], BF16, IN)
    wq = kb.dram("wq", [D, 256], F32, IN)
    wkk = kb.dram("wkk", [D, 128], F32, IN)
    wv = kb.dram("wv", [D, 64], F32, IN)
    nrm = kb.dram("nrm", [128, 2], F32, IN)
    cosT = kb.dram("cosT", [128, NTOK], F32, IN)
    sinT = kb.dram("sinT", [128, NTOK], F32, IN)
    rotm = kb.dram("rotm", [128, 128], F32, IN)
    blk = kb.dram("blk", [128, 128], F32, IN)
    oT = kb.dram("oT", [256, NTOK], BF16, OUT)

    qT = kb.sb("qT", [128, 2, NTOK], BF16)
    kA = kb.sb("kA", [128, NTOK], BF16)
    kB = kb.sb("kB", [128, NTOK], BF16)
    vaug = kb.sb("vaug", [128, NKC, 128], BF16)
    wq_sb = kb.sb("wq_sb", [128, DC, 256], BF16)
    wkk_sb = kb.sb("wkk_sb", [128, DC, 128], BF16)
    wv_sb = kb.sb("wv_sb", [128, DC, 64], BF16)
    rot_sb = kb.sb("rot_sb", [128, 128], BF16)
    blk_sb = kb.sb("blk_sb", [128, 128], BF16)
    nrm_sb = kb.sb("nrm_sb", [128, 2], F32)
    epsb = kb.sb("epsb", [128, 1], F32)
    htile = [kb.sb(f"htile{i}", [128, DC, 512], BF16) for i in range(2)]
    ctile = [kb.sb(f"ctile{i}", [128, 512], F32) for i in range(2)]
    stile = [kb.sb(f"stile{i}", [128, 512], F32) for i in range(2)]
    sqt = kb.sb("sqt", [128, 512], BF16)
    sd = kb.sb("sd", [128, 512], F32)
    qn = kb.sb("qn", [128, 512], BF16)
    t1 = kb.sb("t1", [128, 512], F32)
    t2 = kb.sb("t2", [128, 512], F32)
    rec = kb.sb("rec", [128, 512], F32)
    ostage = [kb.sb(f"ostage{i}", [64, 512], BF16) for i in range(2)]

    ld_w = kb.track("t_ldw")
    ld_h = [kb.track("t_ldh0"), kb.track("t_ldh1")]
    st_o = [kb.track("t_sto0"), kb.track("t_sto1")]

    ev_w = ld_w.inc(nc.gpsimd.dma_start(out=wq_sb[:], in_=wq.rearrange("(k p) m -> p k m", p=128)), 16)
    ev_w = ld_w.inc(nc.gpsimd.dma_start(out=wkk_sb[:], in_=wkk.rearrange("(k p) m -> p k m", p=128)), 16)
    ev_w = ld_w.inc(nc.gpsimd.dma_start(out=wv_sb[:], in_=wv.rearrange("(k p) m -> p k m", p=128)), 16)
    ev_w = ld_w.inc(nc.gpsimd.dma_start(out=rot_sb[:], in_=rotm), 16)
    ev_w = ld_w.inc(nc.gpsimd.dma_start(out=blk_sb[:], in_=blk), 16)
    ev_w = ld_w.inc(nc.gpsimd.dma_start(out=nrm_sb[:], in_=nrm), 16)
    nc.gpsimd.memset(epsb[:], EPS)
    nc.gpsimd.memset(kA[64:128, :], 0.0)
    nc.gpsimd.memset(kB[0:64, :], 0.0)
    ev_pool = kb.pool.inc(nc.gpsimd.memset(vaug[:, :, 64:128], 1.0))

    h_free = [None, None]
    c_free = [None, None]

    def load_tile(ti):
        t0, tn = ATILES[ti]
        s = ti % 2
        kb.wait("sync", h_free[s])
        kb.wait("sync", c_free[s])
        ev = ld_h[s].inc(nc.sync.dma_start(out=htile[s][:, :, 0:tn], in_=hT[:, t0:t0 + tn].rearrange("(k p) n -> p k n", p=128)), 16)
        ev = ld_h[s].inc(nc.sync.dma_start(out=ctile[s][:, 0:tn], in_=cosT[:, t0:t0 + tn]), 16)
        ev = ld_h[s].inc(nc.sync.dma_start(out=stile[s][:, 0:tn], in_=sinT[:, t0:t0 + tn]), 16)
        return ev

    kb.wait("tensor", ev_w)
    kb.wait("vector", ev_w)
    kb.wait("scalar", ev_w)
    kb.wait("scalar", ev_pool)
    kb.wait("vector", ev_pool)
    kb.wait("tensor", ev_pool)
    pend = load_tile(0)
    sqt_free = None
    sd_free = None
    qn_free = None
    dve_last = None
    for ti, (t0, tn) in enumerate(ATILES):
        s = ti % 2
        ev_h = pend
        if ti + 1 < len(ATILES):
            pend = load_tile(ti + 1)
        kb.wait("tensor", ev_h)
        kb.wait("vector", ev_h)
        groups = [(wq_sb, 0, 0, lambda: qT[:, 0, t0:t0 + tn]), (wq_sb, 128, 0, lambda: qT[:, 1, t0:t0 + tn]),
                  (wkk_sb, 0, 1, None)]
        for (wsb, c0, ni, dst) in groups:
            kb.wait("tensor", kb.bank_free[5])
            mm = None
            for kc in range(DC):
                mm = nc.tensor.matmul(kb.banks[5][:, 0:tn], wsb[:, kc, c0:c0 + 128], htile[s][:, kc, 0:tn],
                                      start=(kc == 0), stop=(kc == DC - 1))
            ev_a = kb.pe.inc(mm)
            kb.wait("scalar", ev_a)
            kb.wait("scalar", sqt_free)
            ev_sq = kb.act.inc(nc.scalar.activation(out=sqt[:, 0:tn], in_=kb.banks[5][:, 0:tn], func=AF.Square))
            kb.wait("tensor", ev_sq)
            kb.wait("tensor", kb.bank_free[6])
            ev_b = kb.pe.inc(nc.tensor.matmul(kb.banks[6][:, 0:tn], blk_sb[:, :], sqt[:, 0:tn], start=True, stop=True))
            sqt_free = ev_b
            kb.wait("scalar", ev_b)
            kb.wait("scalar", sd_free)
            ev_sd = kb.act.inc(nc.scalar.activation(out=sd[:, 0:tn], in_=kb.banks[6][:, 0:tn], func=AF.Sqrt,
                                                    bias=epsb[:, 0:1], scale=1.0 / 64))
            kb.bank_free[6] = ev_sd
            kb.wait("vector", ev_sd)
            nc.vector.reciprocal(out=sd[:, 0:tn], in_=sd[:, 0:tn])
            kb.wait("vector", ev_a)
            kb.wait("vector", qn_free)
            ev_qn = kb.dve.inc(nc.vector.scalar_tensor_tensor(out=qn[:, 0:tn], in0=kb.banks[5][:, 0:tn],
                                                              scalar=nrm_sb[:, ni:ni + 1], in1=sd[:, 0:tn],
                                                              op0=ALU.mult, op1=ALU.mult))
            kb.bank_free[5] = ev_qn
            sd_free = ev_qn
            kb.wait("tensor", ev_qn)
            kb.wait("tensor", kb.bank_free[7])
            ev_c = kb.pe.inc(nc.tensor.matmul(kb.banks[7][:, 0:tn], rot_sb[:, :], qn[:, 0:tn], start=True, stop=True))
            nc.vector.tensor_tensor(out=t1[:, 0:tn], in0=qn[:, 0:tn], in1=ctile[s][:, 0:tn], op=ALU.mult)
            kb.wait("vector", ev_c)
            nc.vector.tensor_tensor(out=t2[:, 0:tn], in0=kb.banks[7][:, 0:tn], in1=stile[s][:, 0:tn], op=ALU.mult)
            if dst is not None:
                dve_last = kb.dve.inc(nc.vector.tensor_tensor(out=dst(), in0=t1[:, 0:tn], in1=t2[:, 0:tn], op=ALU.add))
            else:
                nc.vector.tensor_tensor(out=kA[0:64, t0:t0 + tn], in0=t1[0:64, 0:tn], in1=t2[0:64, 0:tn], op=ALU.add)
                dve_last = kb.dve.inc(nc.vector.tensor_tensor(out=kB[64:128, t0:t0 + tn], in0=t1[64:128, 0:tn],
                                                              in1=t2[64:128, 0:tn], op=ALU.add))
            kb.bank_free[7] = dve_last
            qn_free = ev_c
        nch = tn // 128
        kb.wait("tensor", kb.bank_free[3])
        mm = None
        for ci in range(nch):
            for kc in range(DC):
                mm = nc.tensor.matmul(kb.banks[3][:, ci * 64:(ci + 1) * 64], htile[s][:, kc, ci * 128:(ci + 1) * 128],
                                      wv_sb[:, kc, :], start=(kc == 0), stop=(kc == DC - 1))
        ev_v = kb.pe.inc(mm)
        h_free[s] = ev_v
        c_free[s] = dve_last
        kb.wait("scalar", ev_v)
        c_first = t0 // 128
        ev_vc = kb.act.inc(nc.scalar.activation(out=vaug[:, c_first:c_first + nch, 0:64],
                                                in_=kb.banks[3][:, 0:nch * 64].rearrange("p (c d) -> p c d", d=64),
                                                func=AF.Identity))
        kb.bank_free[3] = ev_vc
    ev_A_dve = dve_last
    ev_A_act = (kb.act, kb.act.v)

    iters = []
    for hd in range(4):
        for qi, (q0, nq) in enumerate(ATILES):
            npair = 1 if qi == 0 else NKC // 2
            for kp in range(npair):
                iters.append((hd, qi, kp, kp == 0, kp == npair - 1))
    N = len(iters)
    evS = [None] * N
    evP = [None] * N
    pt2 = [kb.sb(f"pt2_{i}", [128, 2, 512], BF16) for i in range(3)]
    pt_free = [None] * 3
    o_st_free = [None, None]
    grp = [0]
    OB = (6, 7)

    def emit_S(n):
        hd, qi, kp, first, last = iters[n]
        q0, nq = ATILES[qi]
        r0 = (hd % 2) * 64
        w = n % 3
        kb.wait("tensor", kb.bank_free[2 * w])
        kb.wait("tensor", kb.bank_free[2 * w + 1])
        kb.wait("tensor", ev_A_dve)
        mm = None
        for j in range(2):
            kc = 2 * kp + j
            kz = kA if hd % 2 == 0 else kB
            mm = nc.tensor.matmul(kb.banks[2 * w + j][:, 0:nq], kz[:, kc * 128:(kc + 1) * 128],
                                  qT[:, hd // 2, q0:q0 + nq], start=True, stop=True)
        evS[n] = kb.pe.inc(mm)

    emit_S(0)
    for n in range(N):
        hd, qi, kp, first, last = iters[n]
        q0, nq = ATILES[qi]
        if n + 1 < N:
            emit_S(n + 1)
        w = n % 3
        sl = n % 3
        kb.wait("scalar", evS[n])
        kb.wait("scalar", pt_free[sl])
        evP[n] = kb.act.inc(nc.scalar.activation(out=pt2[sl][:, :, 0:nq],
                                                 in_=kb.wide[w][:, :].rearrange("p (j n) -> p j n", j=2)[:, :, 0:nq],
                                                 func=AF.Exp, scale=0.125))
        kb.bank_free[2 * w] = evP[n]
        kb.bank_free[2 * w + 1] = evP[n]
        ob = OB[grp[0] % 2]
        kb.wait("tensor", evP[n])
        if first:
            kb.wait("tensor", kb.bank_free[ob])
            kb.wait("tensor", ev_A_act)
        ev_pv = None
        for j in range(2):
            kc = 2 * kp + j
            ev_pv = nc.tensor.matmul(kb.banks[ob][:, 0:nq], vaug[:, kc, :], pt2[sl][:, j, 0:nq],
                                     start=(first and j == 0), stop=(last and j == 1))
        ev_pv = kb.pe.inc(ev_pv)
        pt_free[sl] = ev_pv
        if last:
            os_ = grp[0] % 2
            kb.wait("vector", ev_pv)
            nc.vector.reciprocal(out=rec[64:128, 0:nq], in_=kb.banks[ob][64:128, 0:nq])
            kb.wait("vector", o_st_free[os_])
            ev_e = kb.dve.inc(nc.vector.tensor_tensor(out=ostage[os_][:, 0:nq], in0=kb.banks[ob][0:64, 0:nq],
                                                      in1=rec[64:128, 0:nq], op=ALU.mult))
            kb.bank_free[ob] = ev_e
            kb.wait("sync", ev_e)
            e = st_o[os_].inc(nc.sync.dma_start(out=oT[hd * 64:(hd + 1) * 64, q0:q0 + nq], in_=ostage[os_][:, 0:nq]), 16)
            o_st_free[os_] = e
            kb.out_evs.append(e)
            grp[0] += 1
    return kb.finish()


def attn_consts():
    p = np.arange(128)
    d = p % 64
    fidx = (d % 16).astype(np.float32)
    freqs = (np.float32(10000.0) ** (-fidx / np.float32(16.0))).astype(np.float32)
    t = np.arange(SEQ)
    rows = (t // 64).astype(np.float32)
    cols = (t % 64).astype(np.float32)
    pos = np.where((d < 32)[:, None], rows[None, :], cols[None, :]).astype(np.float32)
    ang = (pos * freqs[:, None]).astype(np.float32)
    cosT = np.ones((128, NTOK), np.float32)
    sinT = np.zeros((128, NTOK), np.float32)
    cosT[:, CTX:] = np.cos(ang)
    sinT[:, CTX:] = np.sin(ang)
    rotm = np.zeros((128, 128), np.float32)
    for m in range(128):
        if (m % 32) < 16:
            rotm[m + 16, m] = -1.0
        else:
            rotm[m - 16, m] = 1.0
    blk = np.zeros((128, 128), np.float32)
    blk[:64, :64] = 1.0
    blk[64:, 64:] = 1.0
    return cosT, sinT, rotm, blk


NFFT = 16384
CG = 32
NG = 4
HALF_PI = float(np.pi / 2)


def dft_tables():
    a = np.arange(128)
    ang = 2.0 * np.pi * np.outer(a, a) / 128.0
    C = np.cos(ang)
    S = np.sin(ang)
    angt = 2.0 * np.pi * np.outer(a, a) / NFFT
    Twr = np.cos(angt)
    Twi = -np.sin(angt)
    f = lambda *xs: np.ascontiguousarray(np.concatenate(xs, axis=1)).astype(np.float32)
    tabs = {
        "FT1": f(C, -S),
        "FS1": np.ascontiguousarray(np.concatenate([np.concatenate([C[:64], -S[:64]], 1),
                                                    np.concatenate([S[:64], C[:64]], 1)], 0)).astype(np.float32),
        "TW": f(Twr, Twi),
        "FS3": f(C, S, -S),
        "R12": f(C, S, -S, C),
        "LAB": f(C[:, :64], S[:, :64], -S[:, :64], C[:, :64]),
    }
    return tabs


def build_hf(debug=False):
    kb = KB()
    nc = kb.nc
    IN, OUT = "ExternalInput", "ExternalOutput"
    featsT = kb.dram("featsT", [33, NFFT], F32, IN)
    w1 = kb.dram("w1", [33, 64], F32, IN)
    w23 = kb.dram("w23", [64, 128], F32, IN)
    fvec = kb.dram("fvec", [64, 4], F32, IN)
    wout = kb.dram("wout", [64, NG * 128], F32, IN)
    decF = kb.dram("decF", [NG, 128, 128 * CG], F32, IN)
    decB = kb.dram("decB", [NG, 128, 128 * CG], F32, IN)
    FT1 = kb.dram("FT1", [128, 256], F32, IN)
    TW = kb.dram("TW", [128, 256], F32, IN)
    FS3 = kb.dram("FS3", [128, 384], F32, IN)
    Hq = kb.dram("Hq", [2, 128, 2 * 128 * 128], BF16, OUT)

    if debug:
        dbg_hid = kb.dram("dbg_hid", [64, NFFT], F32, OUT)
        dbg_taps = kb.dram("dbg_taps", [128, 2 * CG * 128], BF16, OUT)
        dbg_ap = kb.dram("dbg_ap", [128, 2 * 2 * CG * 128], BF16, OUT)
    hid3 = kb.sb("hid3", [64, NFFT], F32)
    w1_sb = kb.sb("w1_sb", [33, 64], F32)
    w23_sb = kb.sb("w23_sb", [64, 128], F32)
    fv = kb.sb("fv", [64, 4], F32)
    a4 = kb.sb("a4", [64, 1], F32)
    ab4 = kb.sb("ab4", [64, 3], F32)
    hpi = kb.sb("hpi", [64, 1], F32)
    wout_sb = kb.sb("wout_sb", [64, NG * 128], F32)
    ft1_sb = kb.sb("ft1_sb", [128, 256], BF16)
    tw_sb = kb.sb("tw_sb", [128, 256], F32)
    fs3_sb = kb.sb("fs3_sb", [128, 384], BF16)
    ftile = [kb.sb(f"ftile{i}", [33, 512], F32) for i in range(2)]
    mq = [kb.sb(f"mq{i}", [64, 512], F32) for i in range(2)]
    ms = [kb.sb(f"ms{i}", [64, 512], F32) for i in range(2)]
    mc = [kb.sb(f"mc{i}", [64, 512], F32) for i in range(2)]
    mh = [kb.sb(f"mh{i}", [64, 512], F32) for i in range(2)]
    dF = kb.sb("dF", [128, 128, CG], F32)
    dB = kb.sb("dB", [128, 128, CG], F32)
    taps = kb.sb("taps", [128, 2, CG, 128], BF16)
    Ap = kb.sb("Ap", [128, 2, 2 * CG, 128], BF16)
    tA = kb.sb("tA", [128, 256], F32)
    tB = kb.sb("tB", [128, 256], F32)
    tt = [kb.sb(f"tt{i}", [128, 256], F32) for i in range(4)]
    stg = [kb.sb(f"stg{i}", [128, 2, 4, 128], BF16) for i in range(2)]

    ld_c = kb.track("t_ldc")
    ld_f = [kb.track("t_ldf0"), kb.track("t_ldf1")]
    ld_d = kb.track("t_ldd")
    st_s = [kb.track("t_sts0"), kb.track("t_sts1")]

    ev_c = ld_c.inc(nc.sync.dma_start(out=w1_sb[:], in_=w1), 16)
    ev_c = ld_c.inc(nc.sync.dma_start(out=w23_sb[:], in_=w23), 16)
    ev_c = ld_c.inc(nc.sync.dma_start(out=fv[:], in_=fvec), 16)
    ev_c = ld_c.inc(nc.sync.dma_start(out=wout_sb[:], in_=wout), 16)
    ev_c = ld_c.inc(nc.sync.dma_start(out=tw_sb[:], in_=TW), 16)
    ev_c = ld_c.inc(nc.gpsimd.dma_start(out=ft1_sb[:], in_=FT1), 16)
    ev_c = ld_c.inc(nc.gpsimd.dma_start(out=fs3_sb[:], in_=FS3), 16)
    for e in ("tensor", "vector", "scalar"):
        kb.wait(e, ev_c)
    nc.vector.memset(hpi[:], HALF_PI)
    ev_k = kb.dve.inc(nc.vector.tensor_scalar(out=a4[:], in0=fv[:, 0:1], scalar1=0.25, scalar2=None, op0=ALU.mult))
    kb.wait("vector", ev_k)
    ev_k = kb.dve.inc(nc.vector.tensor_scalar(out=ab4[:], in0=fv[:, 1:4], scalar1=a4[:, 0:1], scalar2=None, op0=ALU.mult))
    kb.wait("vector", ev_k)
    kb.wait("scalar", ev_k)

    f_free = [None, None]

    def load_feats(ti):
        s = ti % 2
        kb.wait("sync", f_free[s])
        return ld_f[s].inc(nc.sync.dma_start(out=ftile[s][:, :], in_=featsT[:, ti * 512:(ti + 1) * 512]), 16)

    buf_free = [None, None]
    q_free = [None, None]

    def layer(ti, ly, rhs_ap, K, w_ap, out_ap, rhs_ev, bank):
        s = ti % 2
        kb.wait("tensor", rhs_ev)
        kb.wait("tensor", kb.bank_free[bank])
        ev_m = kb.pe.inc(nc.tensor.matmul(kb.banks[bank][0:64, 0:512], w_ap, rhs_ap, start=True, stop=True))
        kb.wait("vector", ev_m)
        kb.wait("vector", q_free[s])
        ev_q = kb.dve.inc(nc.vector.tensor_scalar(out=mq[s][:, :], in0=kb.banks[bank][0:64, 0:512], scalar1=a4[:, 0:1],
                                                  scalar2=ab4[:, ly:ly + 1], op0=ALU.mult, op1=ALU.add))
        kb.bank_free[bank] = ev_q
        kb.wait("scalar", ev_q)
        nc.scalar.activation(out=ms[s][:, :], in_=mq[s][:, :], func=AF.Sin)
        ev_s = kb.act.inc(nc.scalar.activation(out=mc[s][:, :], in_=mq[s][:, :], func=AF.Sin, bias=hpi[:, 0:1], scale=1.0))
        q_free[s] = ev_s
        kb.wait("vector", ev_s)
        nc.vector.tensor_tensor(out=mc[s][:, :], in0=ms[s][:, :], in1=mc[s][:, :], op=ALU.mult)
        nc.vector.tensor_tensor(out=ms[s][:, :], in0=ms[s][:, :], in1=ms[s][:, :], op=ALU.mult)
        nc.vector.tensor_scalar(out=ms[s][:, :], in0=ms[s][:, :], scalar1=-8.0, scalar2=4.0, op0=ALU.mult, op1=ALU.add)
        if out_ap is None:
            kb.wait("vector", buf_free[s])
            o = mh[s][:, :]
        else:
            o = out_ap
        ev_h = kb.dve.inc(nc.vector.tensor_tensor(out=o, in0=mc[s][:, :], in1=ms[s][:, :], op=ALU.mult))
        return ev_h, ev_m

    pend = [load_feats(0), load_feats(1)]
    for tp in range(0, 32, 2):
        evs = [pend[0], pend[1]]
        hev = [None, None]
        for ly in range(3):
            for j in range(2):
                ti = tp + j
                s = ti % 2
                if ly == 0:
                    hev[j], ev_m = layer(ti, 0, ftile[s][:, :], 33, w1_sb[:, :], None, evs[j], 5 + j)
                    f_free[s] = ev_m
                elif ly == 1:
                    hev[j], ev_m = layer(ti, 1, mh[s][:, :], 64, w23_sb[:, 0:64], None, hev[j], 5 + j)
                    buf_free[s] = ev_m
                else:
                    hev[j], ev_m = layer(ti, 2, mh[s][:, :], 64, w23_sb[:, 64:128], hid3[:, ti * 512:(ti + 1) * 512], hev[j], 5 + j)
                    buf_free[s] = ev_m
        if tp + 2 < 32:
            pend = [load_feats(tp + 2), load_feats(tp + 3)]
    ev_hid = (kb.dve, kb.dve.v)

    d_free = None
    taps_free = None
    ap_free = None
    it4 = [0]
    st_free_hf = [None, None]
    for g in range(NG):
        kb.wait("sync", d_free)
        kb.wait("scalar", d_free)
        ev_d = ld_d.inc(nc.sync.dma_start(out=dF[:].rearrange("p n c -> p (n c)"), in_=decF[g]), 16)
        ev_d = ld_d.inc(nc.scalar.dma_start(out=dB[:].rearrange("p n c -> p (n c)"), in_=decB[g]), 16)
        kb.wait("vector", ev_d)
        kb.wait("tensor", ev_hid)
        for nb in range(32):
            bank = nb % 2
            kb.wait("tensor", kb.bank_free[bank])
            mm = None
            for q in range(4):
                n2 = nb * 4 + q
                mm = nc.tensor.matmul(kb.banks[bank][:, q * 128:(q + 1) * 128], hid3[:, n2:NFFT:128],
                                      wout_sb[:, g * 128:(g + 1) * 128], start=True, stop=True)
            ev_m = kb.pe.inc(mm)
            kb.wait("vector", ev_m)
            psv = kb.banks[bank][:, 0:512].rearrange("p (n o d c) -> p n o d c", n=4, o=2, d=2)
            nc.vector.tensor_tensor(out=tA[:, :].rearrange("p (n o c) -> p n o c", n=4, o=2), in0=psv[:, :, :, 0, :],
                                    in1=dF[:, nb * 4:nb * 4 + 4, :].unsqueeze(2).to_broadcast([128, 4, 2, CG]), op=ALU.mult)
            ev_b = kb.dve.inc(nc.vector.tensor_tensor(out=tB[:, :].rearrange("p (n o c) -> p n o c", n=4, o=2), in0=psv[:, :, :, 1, :],
                                                      in1=dB[:, nb * 4:nb * 4 + 4, :].unsqueeze(2).to_broadcast([128, 4, 2, CG]), op=ALU.mult))
            kb.bank_free[bank] = ev_b
            if nb == 0:
                kb.wait("vector", taps_free)
            ev_t = kb.dve.inc(nc.vector.tensor_tensor(out=taps[:, :, :, nb * 4:nb * 4 + 4].rearrange("p o c n -> p n o c"),
                                                      in0=tA[:, :].rearrange("p (n o c) -> p n o c", n=4, o=2),
                                                      in1=tB[:, :].rearrange("p (n o c) -> p n o c", n=4, o=2), op=ALU.add))
        d_free = ev_t
        kb.wait("tensor", ev_t)
        for pr in range(CG):
            bank = 2 + pr % 2
            kb.wait("tensor", kb.bank_free[bank])
            mm = None
            for j in range(2):
                sq = 2 * pr + j
                o, c = sq // CG, sq % CG
                mm = nc.tensor.matmul(kb.banks[bank][:, j * 256:(j + 1) * 256], taps[:, o, c, :], ft1_sb[:, :], start=True, stop=True)
            ev_m = kb.pe.inc(mm)
            kb.wait("vector", ev_m)
            if pr == 0:
                kb.wait("vector", ap_free)
            psv = kb.banks[bank][:, 0:512].rearrange("p (s r k) -> p s r k", s=2, r=2)
            twr = tw_sb[:, 0:128].unsqueeze(1).to_broadcast([128, 2, 128])
            twi = tw_sb[:, 128:256].unsqueeze(1).to_broadcast([128, 2, 128])
            v4 = [t[:, :].rearrange("p (s k) -> p s k", s=2) for t in tt]
            nc.vector.tensor_tensor(out=v4[0], in0=psv[:, :, 0, :], in1=twr, op=ALU.mult)
            nc.vector.tensor_tensor(out=v4[1], in0=psv[:, :, 1, :], in1=twi, op=ALU.mult)
            nc.vector.tensor_tensor(out=v4[2], in0=psv[:, :, 0, :], in1=twi, op=ALU.mult)
            ev_b = kb.dve.inc(nc.vector.tensor_tensor(out=v4[3], in0=psv[:, :, 1, :], in1=twr, op=ALU.mult))
            kb.bank_free[bank] = ev_b
            nc.vector.tensor_tensor(out=Ap[:, 0, 2 * pr:2 * pr + 2, :], in0=v4[0], in1=v4[1], op=ALU.subtract)
            ev_a = kb.dve.inc(nc.vector.tensor_tensor(out=Ap[:, 1, 2 * pr:2 * pr + 2, :], in0=v4[2], in1=v4[3], op=ALU.add))
        taps_free = (kb.pe, kb.pe.v)
        kb.wait("tensor", ev_a)
        for blk4 in range(16):
            s0 = blk4 * 4
            o, c0 = s0 // CG, s0 % CG
            for bsel in (4, 5):
                kb.wait("tensor", kb.bank_free[bsel])
            rr = Ap[:, 0, s0:s0 + 4, :]
            ri_ = Ap[:, 1, s0:s0 + 4, :]
            nc.tensor.matmul(kb.banks[4][:, 0:512], fs3_sb[:, 0:128], rr, start=True, stop=False)
            nc.tensor.matmul(kb.banks[4][:, 0:512], fs3_sb[:, 128:256], ri_, start=False, stop=True)
            nc.tensor.matmul(kb.banks[5][:, 0:512], fs3_sb[:, 256:384], rr, start=True, stop=False)
            ev_m = kb.pe.inc(nc.tensor.matmul(kb.banks[5][:, 0:512], fs3_sb[:, 0:128], ri_, start=False, stop=True))
            sl = it4[0] % 2
            it4[0] += 1
            kb.wait("scalar", ev_m)
            kb.wait("scalar", st_free_hf[sl])
            nc.scalar.activation(out=stg[sl][:, 0, :, :], in_=kb.banks[4][:, 0:512].rearrange("p (s k) -> p s k", s=4),
                                 func=AF.Identity, scale=1.0 / NFFT)
            ev_e = kb.act.inc(nc.scalar.activation(out=stg[sl][:, 1, :, :], in_=kb.banks[5][:, 0:512].rearrange("p (s k) -> p s k", s=4),
                                                   func=AF.Identity, scale=1.0 / NFFT))
            kb.bank_free[4] = ev_e
            kb.bank_free[5] = ev_e
            kb.wait("sync", ev_e)
            cc = g * CG + c0
            dst = Hq[o].rearrange("p (r c k) -> p r c k", r=2, c=128)[:, :, cc:cc + 4, :]
            e = st_s[sl].inc(nc.sync.dma_start(out=dst, in_=stg[sl][:, :, :, :]), 16)
            st_free_hf[sl] = e
            kb.out_evs.append(e)
        ap_free = (kb.pe, kb.pe.v)
    if debug:
        kb.wait("sync", (kb.dve, kb.dve.v))
        kb.wait("sync", (kb.pe, kb.pe.v))
        for dst, src in ((dbg_hid, hid3[:, :]), (dbg_taps, taps[:].rearrange("p o c n -> p (o c n)")),
                         (dbg_ap, Ap[:].rearrange("p r s k -> p (r s k)"))):
            kb.out_evs.append(st_s[0].inc(nc.sync.dma_start(out=dst, in_=src), 16))
    return kb.finish()


def build_hc():
    kb = KB()
    nc = kb.nc
    IN, OUT = "ExternalInput", "ExternalOutput"
    p3 = kb.dram("p3", [NG, 128, 3 * CG * 130], BF16, IN)
    cw = kb.dram("cw", [128, 9 * 128], F32, IN)
    cb = kb.dram("cb", [128, 3 * 128], F32, IN)
    fb = kb.dram("fb", [128, 2 * 128], F32, IN)
    mask = kb.dram("mask", [128, 1], F32, IN)
    Hq = kb.dram("Hq", [2, 128, 2 * 128 * 128], BF16, IN)
    FS1 = kb.dram("FS1", [128, 256], F32, IN)
    TW = kb.dram("TW", [128, 256], F32, IN)
    FS3 = kb.dram("FS3", [128, 384], F32, IN)
    R12 = kb.dram("R12", [128, 512], F32, IN)
    LAB = kb.dram("LAB", [128, 256], F32, IN)
    yTL = kb.dram("yTL", [128, 128 * 128], BF16, OUT)

    fs1_sb = kb.sb("fs1_sb", [128, 256], BF16)
    tw_sb = kb.sb("tw_sb", [128, 256], F32)
    fs3_sb = kb.sb("fs3_sb", [128, 384], BF16)
    r12_sb = kb.sb("r12_sb", [128, 512], BF16)
    lab_sb = kb.sb("lab_sb", [128, 256], BF16)
    cw_sb = kb.sb("cw_sb", [128, 3, 3, 128], F32)
    cb_sb = kb.sb("cb_sb", [128, 3, 128], F32)
    fb_sb = kb.sb("fb_sb", [128, 2, 128], F32)
    mask_sb = kb.sb("mask_sb", [128, 1], F32)
    pt_ = kb.sb("pt_", [128, 3, CG, 130], BF16)
    u = kb.sb("u", [128, 3, CG, 128], BF16)
    uf = kb.sb("uf", [128, CG, 128], F32)
    uf2 = kb.sb("uf2", [128, CG, 128], F32)
    z = kb.sb("z", [128, CG, 128], BF16)
    vfb = kb.sb("vfb", [128, CG, 128], F32)
    Ap = kb.sb("Ap", [128, 2, CG, 128], BF16)
    Y = kb.sb("Y", [128, 2, CG, 128], BF16)
    Bp = kb.sb("Bp", [128, 2, CG, 128], BF16)
    hq = [kb.sb(f"hq{i}", [128, 2, CG, 128], BF16) for i in range(2)]
    tt = [kb.sb(f"tt{i}", [128, 512], F32) for i in range(4)]
    ystage = uf2[:].bitcast(BF16)[:, :, 0:128]

    ld_c = kb.track("t_ldc")
    ld_p = kb.track("t_ldp")
    ld_h = kb.track("t_ldh")
    st_y = kb.track("t_sty")

    ev_c = ld_c.inc(nc.sync.dma_start(out=tw_sb[:], in_=TW), 16)
    ev_c = ld_c.inc(nc.sync.dma_start(out=cw_sb[:].rearrange("p a b c -> p (a b c)"), in_=cw), 16)
    ev_c = ld_c.inc(nc.sync.dma_start(out=cb_sb[:].rearrange("p a c -> p (a c)"), in_=cb), 16)
    ev_c = ld_c.inc(nc.sync.dma_start(out=fb_sb[:].rearrange("p a c -> p (a c)"), in_=fb), 16)
    ev_c = ld_c.inc(nc.sync.dma_start(out=mask_sb[:], in_=mask), 16)
    ev_c = ld_c.inc(nc.gpsimd.dma_start(out=fs1_sb[:], in_=FS1), 16)
    ev_c = ld_c.inc(nc.gpsimd.dma_start(out=fs3_sb[:], in_=FS3), 16)
    ev_c = ld_c.inc(nc.gpsimd.dma_start(out=r12_sb[:], in_=R12), 16)
    ev_c = ld_c.inc(nc.gpsimd.dma_start(out=lab_sb[:], in_=LAB), 16)
    for e in ("tensor", "vector", "scalar", "gpsimd"):
        kb.wait(e, ev_c)

    twr2 = tw_sb[:, 0:128].unsqueeze(1).to_broadcast([128, 2, 128])
    twi2 = tw_sb[:, 128:256].unsqueeze(1).to_broadcast([128, 2, 128])
    v2 = [t[:, 0:256].rearrange("p (s k) -> p s k", s=2) for t in tt]
    v4 = [t[:, 0:512].rearrange("p (s k) -> p s k", s=4) for t in tt]

    st = {"p_free": None, "u_free": None, "hq_free": [None, None], "y_free": None, "ap_free": None, "y2_free": None,
          "bp_free": None, "z_pe_free": None}

    def fft_conv(src_fn, src_ev, hq_t, hq_ev, gate):
        kb.wait("tensor", src_ev)
        ev_a = None
        for pr in range(CG // 2):
            bank = pr % 2
            kb.wait("tensor", kb.bank_free[bank])
            mm = None
            for j in range(2):
                mm = nc.tensor.matmul(kb.banks[bank][:, j * 256:(j + 1) * 256], src_fn(2 * pr + j), fs1_sb[:, :], start=True, stop=True)
            ev_m = kb.pe.inc(mm)
            kb.wait("vector", ev_m)
            if pr == 0:
                kb.wait("vector", st["ap_free"])
            psv = kb.banks[bank][:, 0:512].rearrange("p (s r k) -> p s r k", s=2, r=2)
            nc.vector.tensor_tensor(out=v2[0], in0=psv[:, :, 0, :], in1=twr2, op=ALU.mult)
            nc.vector.tensor_tensor(out=v2[1], in0=psv[:, :, 1, :], in1=twi2, op=ALU.mult)
            nc.vector.tensor_tensor(out=v2[2], in0=psv[:, :, 0, :], in1=twi2, op=ALU.mult)
            ev_b = kb.dve.inc(nc.vector.tensor_tensor(out=v2[3], in0=psv[:, :, 1, :], in1=twr2, op=ALU.mult))
            kb.bank_free[bank] = ev_b
            nc.vector.tensor_tensor(out=Ap[:, 0, 2 * pr:2 * pr + 2, :], in0=v2[0], in1=v2[1], op=ALU.subtract)
            ev_a = kb.dve.inc(nc.vector.tensor_tensor(out=Ap[:, 1, 2 * pr:2 * pr + 2, :], in0=v2[2], in1=v2[3], op=ALU.add))
        src_done = (kb.pe, kb.pe.v)
        kb.wait("tensor", ev_a)
        kb.wait("vector", hq_ev)
        ev_y = None
        for b4 in range(CG // 4):
            c0 = b4 * 4
            br, bi = 2 + 2 * (b4 % 2), 3 + 2 * (b4 % 2)
            kb.wait("tensor", kb.bank_free[br])
            kb.wait("tensor", kb.bank_free[bi])
            rr = Ap[:, 0, c0:c0 + 4, :]
            ri_ = Ap[:, 1, c0:c0 + 4, :]
            nc.tensor.matmul(kb.banks[br][:, 0:512], fs3_sb[:, 0:128], rr, start=True, stop=False)
            nc.tensor.matmul(kb.banks[br][:, 0:512], fs3_sb[:, 128:256], ri_, start=False, stop=True)
            nc.tensor.matmul(kb.banks[bi][:, 0:512], fs3_sb[:, 256:384], rr, start=True, stop=False)
            ev_m = kb.pe.inc(nc.tensor.matmul(kb.banks[bi][:, 0:512], fs3_sb[:, 0:128], ri_, start=False, stop=True))
            kb.wait("vector", ev_m)
            if b4 == 0:
                kb.wait("vector", st["y2_free"])
            xr = kb.banks[br][:, 0:512].rearrange("p (s k) -> p s k", s=4)
            xi = kb.banks[bi][:, 0:512].rearrange("p (s k) -> p s k", s=4)
            hr = hq_t[:, 0, c0:c0 + 4, :]
            hi = hq_t[:, 1, c0:c0 + 4, :]
            nc.vector.tensor_tensor(out=v4[0], in0=xr, in1=hr, op=ALU.mult)
            nc.vector.tensor_tensor(out=v4[1], in0=xi, in1=hi, op=ALU.mult)
            nc.vector.tensor_tensor(out=v4[2], in0=xr, in1=hi, op=ALU.mult)
            ev_b = kb.dve.inc(nc.vector.tensor_tensor(out=v4[3], in0=xi, in1=hr, op=ALU.mult))
            kb.bank_free[br] = ev_b
            kb.bank_free[bi] = ev_b
            nc.vector.tensor_tensor(out=Y[:, 0, c0:c0 + 4, :], in0=v4[0], in1=v4[1], op=ALU.subtract)
            ev_y = kb.dve.inc(nc.vector.tensor_tensor(out=Y[:, 1, c0:c0 + 4, :], in0=v4[2], in1=v4[3], op=ALU.add))
        st["ap_free"] = (kb.pe, kb.pe.v)
        hq_done = ev_y
        kb.wait("tensor", ev_y)
        ev_bp = None
        for pr in range(CG // 2):
            bank = pr % 2
            kb.wait("tensor", kb.bank_free[bank])
            mm = None
            for j in range(2):
                c = 2 * pr + j
                nc.tensor.matmul(kb.banks[bank][:, j * 256:(j + 1) * 256], Y[:, 0, c, :], r12_sb[:, 0:256], start=True, stop=False)
                mm = nc.tensor.matmul(kb.banks[bank][:, j * 256:(j + 1) * 256], Y[:, 1, c, :], r12_sb[:, 256:512], start=False, stop=True)
            ev_m = kb.pe.inc(mm)
            kb.wait("vector", ev_m)
            if pr == 0:
                kb.wait("vector", st["bp_free"])
            psv = kb.banks[bank][:, 0:512].rearrange("p (s r k) -> p s r k", s=2, r=2)
            nc.vector.tensor_tensor(out=v2[0], in0=psv[:, :, 0, :], in1=twr2, op=ALU.mult)
            nc.vector.tensor_tensor(out=v2[1], in0=psv[:, :, 1, :], in1=twi2, op=ALU.mult)
            nc.vector.tensor_tensor(out=v2[2], in0=psv[:, :, 1, :], in1=twr2, op=ALU.mult)
            ev_b = kb.dve.inc(nc.vector.tensor_tensor(out=v2[3], in0=psv[:, :, 0, :], in1=twi2, op=ALU.mult))
            kb.bank_free[bank] = ev_b
            nc.vector.tensor_tensor(out=Bp[:, 0, 2 * pr:2 * pr + 2, :], in0=v2[0], in1=v2[1], op=ALU.add)
            ev_bp = kb.dve.inc(nc.vector.tensor_tensor(out=Bp[:, 1, 2 * pr:2 * pr + 2, :], in0=v2[2], in1=v2[3], op=ALU.subtract))
        st["y2_free"] = (kb.pe, kb.pe.v)
        kb.wait("tensor", ev_bp)
        for b4 in range(CG // 4):
            c0 = b4 * 4
            bank = 2 + b4 % 4
            kb.wait("tensor", kb.bank_free[bank])
            nc.tensor.matmul(kb.banks[bank][:, 0:512], lab_sb[:, 0:128], Bp[:, 0, c0:c0 + 4, :], start=True, stop=False)
            ev_m = kb.pe.inc(nc.tensor.matmul(kb.banks[bank][:, 0:512], lab_sb[:, 128:256], Bp[:, 1, c0:c0 + 4, :], start=False, stop=True))
            kb.wait("vector", ev_m)
            ev_g = gate(c0, kb.banks[bank][:, 0:512].rearrange("p (s k) -> p s k", s=4))
            kb.bank_free[bank] = ev_g
        st["bp_free"] = (kb.pe, kb.pe.v)
        return src_done, hq_done

    for g in range(NG):
        kb.wait("sync", st["p_free"])
        ev_p = ld_p.inc(nc.sync.dma_start(out=pt_[:].rearrange("p j c n -> p (j c n)"), in_=p3[g]), 16)
        hq_evs = []
        for o in range(2):
            kb.wait("scalar", st["hq_free"][o])
            src = Hq[o].rearrange("p (r c k) -> p r c k", r=2, c=128)[:, :, g * CG:(g + 1) * CG, :]
            hq_evs.append(ld_h.inc(nc.scalar.dma_start(out=hq[o][:, :, :, :], in_=src), 16))
        hq_evs[0] = hq_evs[1]
        kb.wait("gpsimd", ev_p)
        kb.wait("gpsimd", st["u_free"])
        kb.wait("gpsimd", st["y_free"])
        ev_u = None
        for j in range(3):
            wv = lambda tp: cw_sb[:, tp, j, g * CG:(g + 1) * CG].unsqueeze(2).to_broadcast([128, CG, 128])
            nc.gpsimd.tensor_tensor(out=uf[:], in0=pt_[:, j, :, 0:128], in1=wv(0), op=ALU.mult)
            nc.gpsimd.tensor_tensor(out=uf2[:], in0=pt_[:, j, :, 1:129], in1=wv(1), op=ALU.mult)
            nc.gpsimd.tensor_tensor(out=uf[:], in0=uf[:], in1=uf2[:], op=ALU.add)
            nc.gpsimd.tensor_tensor(out=uf2[:], in0=pt_[:, j, :, 2:130], in1=wv(2), op=ALU.mult)
            nc.gpsimd.tensor_tensor(out=uf[:], in0=uf[:], in1=uf2[:], op=ALU.add)
            cbv = cb_sb[:, j, g * CG:(g + 1) * CG].unsqueeze(2).to_broadcast([128, CG, 128])
            if j == 0:
                nc.gpsimd.tensor_tensor(out=uf[:], in0=uf[:], in1=cbv, op=ALU.add)
                nc.gpsimd.tensor_scalar(out=u[:, 0, :, :], in0=uf[:], scalar1=mask_sb[:, 0:1], scalar2=None, op0=ALU.mult)
                fbv = fb_sb[:, 0, g * CG:(g + 1) * CG].unsqueeze(2).to_broadcast([128, CG, 128])
                nc.gpsimd.tensor_scalar(out=uf[:], in0=uf[:], scalar1=mask_sb[:, 0:1], scalar2=None, op0=ALU.mult)
                ev_u = kb.pool.inc(nc.gpsimd.tensor_tensor(out=vfb[:], in0=uf[:], in1=fbv, op=ALU.mult))
                ev_v = ev_u
            else:
                ev_u = kb.pool.inc(nc.gpsimd.tensor_tensor(out=u[:, j, :, :], in0=uf[:], in1=cbv, op=ALU.add))
        st["p_free"] = ev_u

        def gate1(c0, ps):
            kb.wait("vector", ev_u)
            if c0 == 0:
                kb.wait("vector", st["z_pe_free"])
            nc.vector.tensor_tensor(out=v4[0], in0=ps, in1=vfb[:, c0:c0 + 4, :], op=ALU.add)
            return kb.dve.inc(nc.vector.scalar_tensor_tensor(out=z[:, c0:c0 + 4, :], in0=v4[0], scalar=mask_sb[:, 0:1],
                                                             in1=u[:, 1, c0:c0 + 4, :], op0=ALU.mult, op1=ALU.mult))

        src_done, hq_done = fft_conv(lambda c: u[:, 0, c, :], ev_v, hq[0], hq_evs[0], gate1)
        st["hq_free"][0] = hq_done
        ev_z = (kb.dve, kb.dve.v)
        kb.wait("gpsimd", ev_z)
        fbv1 = fb_sb[:, 1, g * CG:(g + 1) * CG].unsqueeze(2).to_broadcast([128, CG, 128])
        ev_zf = kb.pool.inc(nc.gpsimd.tensor_tensor(out=vfb[:], in0=z[:], in1=fbv1, op=ALU.mult))

        def gate2(c0, ps):
            kb.wait("vector", ev_zf)
            if c0 == 0:
                kb.wait("vector", st["y_free"])
            nc.vector.tensor_tensor(out=v4[0], in0=ps, in1=vfb[:, c0:c0 + 4, :], op=ALU.add)
            return kb.dve.inc(nc.vector.tensor_tensor(out=ystage[:, c0:c0 + 4, :], in0=v4[0], in1=u[:, 2, c0:c0 + 4, :], op=ALU.mult))

        src_done2, hq_done2 = fft_conv(lambda c: z[:, c, :], ev_z, hq[1], hq_evs[1], gate2)
        st["hq_free"][1] = hq_done2
        st["z_pe_free"] = src_done2
        ev_y = (kb.dve, kb.dve.v)
        st["u_free"] = ev_y
        kb.wait("sync", ev_y)
        e = st_y.inc(nc.sync.dma_start(out=yTL[:, g * CG * 128:(g + 1) * CG * 128].rearrange("p (c n) -> p c n", n=128), in_=ystage), 16)
        st["y_free"] = e
        kb.out_evs.append(e)
    return kb.finish()


import math

_PROGS = {}
_CONSTS = {}


def _prog(key, fn):
    if key not in _PROGS:
        _PROGS[key] = fn()
    return _PROGS[key]


def hyena_tables(L):
    key = ("hy", L)
    if key in _CONSTS:
        return _CONSTS[key]
    f32 = np.float32
    jp = np.arange(NFFT)
    pos = np.zeros(NFFT, np.int64)
    mF = np.zeros(NFFT, f32)
    mB = np.zeros(NFFT, f32)
    fw = jp < L
    pos[fw] = jp[fw]
    mF[fw] = 1
    bw = jp > NFFT - L
    pos[bw] = NFFT - jp[bw]
    mB[bw] = 1
    mB[0] = 1
    t = np.linspace(0.0, 1.0, L, dtype=f32)
    w = (f32(2.0 * math.pi) * np.arange(L, dtype=f32) / f32(L)).astype(f32)
    bands = np.linspace(1e-4, 15, 16, dtype=f32)
    feats = np.concatenate([t[:, None], np.cos(bands * w[:, None]), -np.sin(bands * w[:, None])], -1).astype(f32)
    featsT = np.ascontiguousarray(feats[pos].T)
    min_decay = math.log(1e-2) / 1.5
    max_decay = math.log(1e-2) / 0.3
    deltas = np.abs(np.linspace(min_decay, max_decay, D, dtype=f32))
    decay = np.exp(-t[:, None] * deltas).astype(f32)
    dec_pos = decay[pos]
    dF = (dec_pos * mF[:, None]).reshape(128, 128, D)
    dB = (dec_pos * mB[:, None]).reshape(128, 128, D)
    out = (featsT, dF, dB)
    _CONSTS[key] = out
    return out


def _run(nc, in_maps):
    res = run_bass_kernel_spmd(nc, in_maps, core_ids=list(range(NCORES)))
    return res.results


def run_hf(inp, j, L):
    nc = _prog("hf", build_hf)
    tabs = dft_tables()
    featsT, dF, dB = hyena_tables(L)
    fvec = np.ascontiguousarray(np.stack([inp["hy_f_freq"][j], inp["hy_f_b1"][j], inp["hy_f_b2"][j], inp["hy_f_b3"][j]], 1))
    w23 = np.ascontiguousarray(np.concatenate([inp["hy_f_w2"][j], inp["hy_f_w3"][j]], 1))
    wo = inp["hy_f_wout"][j].reshape(64, 2, 2, D)
    in_maps = []
    for i in range(NCORES):
        ws = wo[:, :, :, 128 * i:128 * (i + 1)].reshape(64, 2, 2, NG, CG).transpose(0, 3, 1, 2, 4)
        dFi = dF[:, :, 128 * i:128 * (i + 1)].reshape(128, 128, NG, CG).transpose(2, 0, 1, 3)
        dBi = dB[:, :, 128 * i:128 * (i + 1)].reshape(128, 128, NG, CG).transpose(2, 0, 1, 3)
        in_maps.append({"featsT": featsT, "w1": np.ascontiguousarray(inp["hy_f_w1"][j]), "w23": w23, "fvec": fvec,
                        "wout": np.ascontiguousarray(ws).reshape(64, NG * 128),
                        "decF": np.ascontiguousarray(dFi).reshape(NG, 128, 128 * CG),
                        "decB": np.ascontiguousarray(dBi).reshape(NG, 128, 128 * CG),
                        "FT1": tabs["FT1"], "TW": tabs["TW"], "FS3": tabs["FS3"]})
    return [r["Hq"] for r in _run(nc, in_maps)]


def run_hc(inp, j, P, L, hqs):
    nc = _prog("hc", build_hc)
    tabs = dft_tables()
    nrow = L // 128
    Ppad = np.zeros((2, 3 * D, L + 2), NPBF)
    Ppad[:, :, 1:L + 1] = P
    idx = (np.arange(nrow) * 128)[:, None] + np.arange(130)[None, :]
    TL = Ppad[:, :, idx]
    mask = np.zeros((2, 64), np.float32)
    mask[:, :nrow] = 1.0
    mask = mask.reshape(128, 1)
    in_maps = []
    for i in range(NCORES):
        p3 = np.zeros((NG, 2, 64, 3, CG, 130), NPBF)
        blk = TL.reshape(2, 3, D, nrow, 130)[:, :, 128 * i:128 * (i + 1)]
        blk = blk.reshape(2, 3, NG, CG, nrow, 130).transpose(2, 0, 4, 1, 3, 5)
        p3[:, :, :nrow] = blk
        rep = lambda a: np.ascontiguousarray(np.broadcast_to(a.reshape(1, -1), (128, a.size))).astype(np.float32)
        cwi = inp["hy_conv_w"][j].reshape(3, 3, D)[:, :, 128 * i:128 * (i + 1)]
        cbi = inp["hy_conv_b"][j].reshape(3, D)[:, 128 * i:128 * (i + 1)]
        fbi = inp["hy_f_bias"][j][:, 128 * i:128 * (i + 1)]
        in_maps.append({"p3": p3.reshape(NG, 128, 3 * CG * 130), "cw": rep(cwi), "cb": rep(cbi), "fb": rep(fbi), "mask": mask,
                        "Hq": hqs[i], "FS1": tabs["FS1"], "TW": tabs["TW"], "FS3": tabs["FS3"], "R12": tabs["R12"],
                        "LAB": tabs["LAB"]})
    res = _run(nc, in_maps)
    y = np.zeros((2, L, D), NPBF)
    for i in range(NCORES):
        yt = np.asarray(res[i]["yTL"]).reshape(2, 64, 128, 128)[:, :nrow]
        y[:, :, 128 * i:128 * (i + 1)] = yt.transpose(0, 1, 3, 2).reshape(2, L, 128)
    return y


def _ts_tokens(i, lat, ctx):
    b, q = i // 4, i % 4
    xs = np.concatenate([lat[b, q * T_LAT:(q + 1) * T_LAT], ctx[b, q * T_CTX:(q + 1) * T_CTX]], 0)
    return np.ascontiguousarray(xs.T)


def _vec8(v):
    return np.ascontiguousarray(np.asarray(v, np.float32).reshape(DC, 128).T)


def _ts_post_inputs(inp, i, l):
    b = i // 4
    cc = np.stack([inp["c"][b], inp["c_ctx"]], 0)
    cT = np.ascontiguousarray(cc.reshape(2, DC, 128).transpose(2, 1, 0)).reshape(128, 2 * DC)
    m = {"cT": cT, "w_mod": inp["w_mod"][l], "b_mod": np.ascontiguousarray(inp["b_mod"][l].reshape(72, 128).T),
         "norm_w": np.ascontiguousarray(inp["norm_w"][l].reshape(3, DC, 128).transpose(2, 0, 1)).reshape(128, 3 * DC),
         "w_gu1": inp["ffn_w_gate_up"][l, 0], "w_dn1": inp["ffn_w_down"][l, 0]}
    if l % 2 == 1:
        j = l // 2
        m["w_in"] = inp["hy_w_in"][j]
        m["b_in"] = np.ascontiguousarray(inp["hy_b_in"][j].reshape(24, 128).T)
    return m


def _run_ts(inp, l_pre, l_post, xT, oT, modp):
    pre = l_pre is not None
    post = None if l_post is None else ("attn" if l_post % 2 == 0 else "hyena")
    nc = _prog(("ts", pre, post), lambda: build_ts(pre, post))
    in_maps = []
    for i in range(NCORES):
        m = {"xT_in": xT[i]}
        if pre:
            if l_pre % 2 == 0:
                w_o, b_o = inp["attn_w_o"][l_pre // 2], np.zeros(D, np.float32)
            else:
                w_o, b_o = inp["hy_w_out"][l_pre // 2], inp["hy_b_out"][l_pre // 2]
            m.update({"oT_in": oT[i], "w_o": w_o, "b_o": _vec8(b_o), "modp_in": modp[i],
                      "w_gu2": inp["ffn_w_gate_up"][l_pre, 1], "w_dn2": inp["ffn_w_down"][l_pre, 1]})
        if post:
            m.update(_ts_post_inputs(inp, i, l_post))
        in_maps.append(m)
    return _run(nc, in_maps)


def _run_attn(inp, a, hTs):
    nc = _prog("attn", build_attn)
    if "attn" not in _CONSTS:
        _CONSTS["attn"] = attn_consts()
    cosT, sinT, rotm, blk = _CONSTS["attn"]
    wqkv = inp["attn_w_qkv"][a]
    nrm = np.ascontiguousarray(np.stack([np.tile(inp["attn_q_norm"][a], 2), np.tile(inp["attn_k_norm"][a], 2)], 1).astype(np.float32))
    Hb = []
    for b in range(2):
        parts = [np.asarray(hTs[4 * b + q]) for q in range(4)]
        Hb.append(np.ascontiguousarray(np.concatenate([p[:, T_LAT:] for p in parts] + [p[:, :T_LAT] for p in parts], 1)))
    in_maps = []
    for i in range(NCORES):
        b, g = i // 4, i % 4
        wk = wqkv[:, 1024 + 64 * g:1024 + 64 * (g + 1)]
        in_maps.append({"hT": Hb[b], "wq": np.ascontiguousarray(wqkv[:, 256 * g:256 * (g + 1)]),
                        "wkk": np.ascontiguousarray(np.concatenate([wk, wk], 1)),
                        "wv": np.ascontiguousarray(wqkv[:, 1280 + 64 * g:1280 + 64 * (g + 1)]),
                        "nrm": nrm, "cosT": cosT, "sinT": sinT, "rotm": rotm, "blk": blk})
    res = _run(nc, in_maps)
    oT = []
    for b in range(2):
        Ob = np.concatenate([np.asarray(res[4 * b + g]["oT"]) for g in range(4)], 0)
        for q in range(4):
            oT.append(np.ascontiguousarray(np.concatenate([Ob[:, CTX + q * T_LAT:CTX + (q + 1) * T_LAT],
                                                           Ob[:, q * T_CTX:(q + 1) * T_CTX]], 1)))
    return oT


def _run_hyena(inp, j, pTs, with_ctx):
    P_lat = np.zeros((2, 3 * D, SEQ), NPBF)
    P_ctx = np.zeros((2, 3 * D, CTX), NPBF)
    for i in range(NCORES):
        b, q = i // 4, i % 4
        p = np.asarray(pTs[i])
        P_lat[b, :, q * T_LAT:(q + 1) * T_LAT] = p[:, :T_LAT]
        P_ctx[b, :, q * T_CTX:(q + 1) * T_CTX] = p[:, T_LAT:]
    hq = run_hf(inp, j, SEQ)
    y_lat = run_hc(inp, j, P_lat, SEQ, hq)
    if with_ctx:
        hqc = run_hf(inp, j, CTX)
        y_ctx = run_hc(inp, j, P_ctx, CTX, hqc)
    else:
        y_ctx = np.zeros((2, CTX, D), NPBF)
    return [_ts_tokens(i, y_lat, y_ctx) for i in range(NCORES)]


def kernel(**inp):
    inp = {k: np.asarray(v) for k, v in inp.items()}
    xT = [_ts_tokens(i, inp["x"], inp["ctx"]) for i in range(NCORES)]
    oT = None
    modp = None
    for l in range(4):
        res = _run_ts(inp, l - 1 if l > 0 else None, l, xT, oT, modp)
        xT = [r["xT_out"] for r in res]
        modp = [r["mod_out"] for r in res]
        if l % 2 == 0:
            oT = _run_attn(inp, l // 2, [r["hT_out"] for r in res])
        else:
            oT = _run_hyena(inp, l // 2, [r["pT_out"] for r in res], with_ctx=(l < 3))
    res = _run_ts(inp, 3, None, xT, oT, modp)
    out = np.zeros((2, SEQ, D), np.float32)
    for i in range(NCORES):
        b, q = i // 4, i % 4
        out[b, q * T_LAT:(q + 1) * T_LAT] = np.asarray(res[i]["xT_out"])[:, :T_LAT].T
    return out
```

```python
import numpy as np
import ml_dtypes
from contextlib import ExitStack
import concourse.bass as bass
import concourse.mybir as mybir
from concourse.bass_utils import run_bass_kernel_spmd

F32 = mybir.dt.float32
BF16 = mybir.dt.bfloat16
AF = mybir.ActivationFunctionType
ALU = mybir.AluOpType
NPBF = ml_dtypes.bfloat16

D = 1024
DC = 8
FF = 2816
FC = 22
SEQ = 8192
CTX = 256
NCORES = 8
T_LAT = 2048
T_CTX = 64
T = T_LAT + T_CTX
TILES = [(0, 512, 0), (512, 512, 0), (1024, 512, 0), (1536, 512, 0), (2048, 64, 1)]
EPS = 1e-6


class Track:
    def __init__(self, kb, name):
        self.name = name
        self.h = kb.es.enter_context(kb.nc.semaphore(name))
        self.v = 0

    def inc(self, instr, amt=1):
        instr.then_inc(self.h, amt)
        self.v += amt
        return (self, self.v)


class KB:
    def __init__(self, wide_banks=0):
        self.nc = bass.Bass("TRN2", target_bir_lowering=False)
        self.es = ExitStack()
        self.waited = {}
        self.pe = Track(self, "t_pe")
        self.act = Track(self, "t_act")
        self.dve = Track(self, "t_dve")
        self.pool = Track(self, "t_pool")
        self.wide = [self.es.enter_context(self.nc.psum_tensor(f"wbank{i}", [128, 1024], F32)) for i in range(wide_banks)]
        self.banks = []
        for w in self.wide:
            self.banks += [w[:, 0:512], w[:, 512:1024]]
        self.banks += [self.es.enter_context(self.nc.psum_tensor(f"bank{i}", [128, 512], F32))
                       for i in range(2 * wide_banks, 8)]
        self.bank_free = [None] * 8
        self.out_evs = []

    def dram(self, name, shape, dt, kind):
        return self.nc.dram_tensor(name, list(shape), dt, kind=kind).ap()

    def sb(self, name, shape, dt=F32):
        return self.es.enter_context(self.nc.sbuf_tensor(name, list(shape), dt))

    def track(self, name):
        return Track(self, name)

    def wait(self, engname, ev):
        if ev is None:
            return
        tr, v = ev
        key = (engname, tr.name)
        if self.waited.get(key, 0) >= v:
            return
        self.waited[key] = v
        getattr(self.nc, engname).wait_ge(tr.h, v)

    def finish(self):
        for ev in self.out_evs:
            self.wait("sync", ev)
        self.es.close()
        return self.nc


class WStream:
    def __init__(self, kb, nslots=3, width=2048):
        self.kb = kb
        self.n = nslots
        self.buf = [kb.sb(f"wbuf{i}", [128, width], BF16) for i in range(nslots)]
        self.ld = [kb.track(f"t_wld{i}") for i in range(nslots)]
        self.free_ev = [None] * nslots
        self.cnt = 0

    def load(self, srcs):
        kb = self.kb
        s = self.cnt % self.n
        self.cnt += 1
        kb.wait("gpsimd", self.free_ev[s])
        ev = None
        for dst_fn, src in srcs:
            ev = self.ld[s].inc(kb.nc.gpsimd.dma_start(out=dst_fn(self.buf[s]), in_=src), 16)
        return s, ev

    def release(self, s, ev):
        self.free_ev[s] = ev


def dense(kb, ws, KC, rhs_fn, units, evac, tiles, G=1, banks=(0, 1, 2, 3), pre_ev=None):
    nc = kb.nc
    nb = len(banks) // G
    pend = []

    def issue(ui):
        srcs = []
        for g in range(G):
            srcs.append((lambda b, g=g: b[:, g * KC * 128:(g + 1) * KC * 128].rearrange("p (k m) -> p k m", m=128),
                         units[ui][g].rearrange("(k p) m -> p k m", p=128)))
        return ws.load(srcs)

    PF = ws.n - 1
    for ui in range(min(PF, len(units))):
        pend.append(issue(ui))
    it = 0
    for ui in range(len(units)):
        s, ld_ev = pend.pop(0)
        kb.wait("tensor", ld_ev)
        kb.wait("tensor", pre_ev)
        wv = ws.buf[s]
        last_ev = None
        for ti, (t0, tn, vec) in enumerate(tiles):
            bsel = [banks[(it % nb) * G + g] for g in range(G)]
            it += 1
            for g in range(G):
                kb.wait("tensor", kb.bank_free[bsel[g]])
            mm = None
            for g in range(G):
                for kc in range(KC):
                    mm = nc.tensor.matmul(kb.banks[bsel[g]][:, 0:tn],
                                          wv[:, (g * KC + kc) * 128:(g * KC + kc + 1) * 128],
                                          rhs_fn(kc, ti), start=(kc == 0), stop=(kc == KC - 1))
            pe_ev = kb.pe.inc(mm)
            last_ev = pe_ev
            fe = evac(ui, ti, [kb.banks[b][:, 0:tn] for b in bsel], pe_ev)
            for b in bsel:
                kb.bank_free[b] = fe
        ws.release(s, last_ev)
        if ui + PF < len(units):
            pend.append(issue(ui + PF))


def build_ts(pre, post):
    kb = KB()
    nc = kb.nc
    IN, OUT = "ExternalInput", "ExternalOutput"
    xT_in = kb.dram("xT_in", [D, T], F32, IN)
    xT_out = kb.dram("xT_out", [D, T], F32, OUT)
    if pre:
        oT_in = kb.dram("oT_in", [D, T], BF16, IN)
        w_o = kb.dram("w_o", [D, D], F32, IN)
        b_o = kb.dram("b_o", [128, DC], F32, IN)
        modp_in = kb.dram("modp_in", [128, DC * 18], F32, IN)
        w_gu2 = kb.dram("w_gu2", [D, 2 * FF], F32, IN)
        w_dn2 = kb.dram("w_dn2", [FF, D], F32, IN)
    if post:
        cT = kb.dram("cT", [128, DC * 2], F32, IN)
        w_mod = kb.dram("w_mod", [D, 9 * D], F32, IN)
        b_mod = kb.dram("b_mod", [128, 72], F32, IN)
        norm_w = kb.dram("norm_w", [128, 3 * DC], F32, IN)
        w_gu1 = kb.dram("w_gu1", [D, 2 * FF], F32, IN)
        w_dn1 = kb.dram("w_dn1", [FF, D], F32, IN)
        mod_out = kb.dram("mod_out", [128, DC * 18], F32, OUT)
        if post == "attn":
            hT_out = kb.dram("hT_out", [D, T], BF16, OUT)
        else:
            w_in = kb.dram("w_in", [D, 3 * D], F32, IN)
            b_in = kb.dram("b_in", [128, 24], F32, IN)
            pT_out = kb.dram("pT_out", [3 * D, T], BF16, OUT)

    xT = kb.sb("xT", [128, DC, T], F32)
    hT = kb.sb("hT", [128, DC, T], BF16)
    aT = kb.sb("aT", [128, 11, T], BF16)
    sqb = kb.sb("sqb", [128, DC, 512], BF16)
    rstd = kb.sb("rstd", [128, 512], F32)
    tmp = [kb.sb(f"tmp{i}", [128, 512], F32) for i in range(2)]
    sg = [kb.sb(f"sg{i}", [128, 512], F32) for i in range(2)]
    ones = kb.sb("ones", [128, 128], BF16)
    modt = kb.sb("modt", [128, DC, 3, 3, 2], F32)
    modp = kb.sb("modp", [128, DC, 3, 3, 2], F32)
    bo_t = kb.sb("bo_t", [128, DC], F32)
    ws = WStream(kb)
    ld = kb.track("t_ld")
    ld_m = kb.track("t_ldm")
    ld_o = kb.track("t_ldo")
    ld_c = kb.track("t_ldc")
    ld_b = kb.track("t_ldb")
    st = kb.track("t_st")
    st_slot = [kb.track("t_st0"), kb.track("t_st1")]

    ev_ones = kb.pool.inc(nc.gpsimd.memset(ones[:], 1.0))
    epsb = kb.sb("epsb", [128, 1], F32)
    ev_eps = kb.pool.inc(nc.gpsimd.memset(epsb[:], EPS))
    ev_x = None
    for c in range(DC):
        eng = nc.sync if c % 2 == 0 else nc.scalar
        ev_x = ld.inc(eng.dma_start(out=xT[:, c, :], in_=xT_in[c * 128:(c + 1) * 128, :]), 16)
    x_ready = {"vector": ev_x, "scalar": ev_x}

    tmp_free = [None, None]
    sg_free = [None, None]
    state = {"x_ev": ev_x, "sqb_free": None, "stat_free": None, "tmpi": 0, "sgi": 0, "rstd_free": None}

    def norm_mod(mt, k, h_free_ev):
        last = None
        for ti, (t0, tn, vec) in enumerate(TILES):
            kb.wait("scalar", state["x_ev"])
            kb.wait("scalar", state["sqb_free"])
            ev_sq = None
            for c in range(DC):
                ev_sq = kb.act.inc(nc.scalar.activation(out=sqb[:, c, 0:tn], in_=xT[:, c, t0:t0 + tn], func=AF.Square))
            kb.wait("tensor", ev_sq)
            kb.wait("tensor", ev_ones)
            kb.wait("tensor", kb.bank_free[4])
            mm = None
            for c in range(DC):
                mm = nc.tensor.matmul(kb.banks[4][:, 0:tn], ones[:, :], sqb[:, c, 0:tn], start=(c == 0), stop=(c == DC - 1))
            ev_stat = kb.pe.inc(mm)
            state["sqb_free"] = ev_stat
            kb.wait("scalar", ev_stat)
            kb.wait("scalar", ev_eps)
            kb.wait("scalar", state["rstd_free"])
            ev_sd = kb.act.inc(nc.scalar.activation(out=rstd[:, 0:tn], in_=kb.banks[4][:, 0:tn], func=AF.Sqrt,
                                                    bias=epsb[:, 0:1], scale=1.0 / D))
            kb.bank_free[4] = ev_sd
            kb.wait("vector", ev_sd)
            kb.wait("vector", state["x_ev"])
            ev_r = kb.dve.inc(nc.vector.reciprocal(out=rstd[:, 0:tn], in_=rstd[:, 0:tn]))
            for c in range(DC):
                i = state["tmpi"] % 2
                state["tmpi"] += 1
                kb.wait("vector", tmp_free[i])
                ev_t = kb.dve.inc(nc.vector.scalar_tensor_tensor(
                    out=tmp[i][:, 0:tn], in0=xT[:, c, t0:t0 + tn], scalar=mt[:, c, k, 0, vec:vec + 1],
                    in1=rstd[:, 0:tn], op0=ALU.mult, op1=ALU.mult))
                kb.wait("scalar", ev_t)
                kb.wait("scalar", h_free_ev)
                last = kb.act.inc(nc.scalar.activation(out=hT[:, c, t0:t0 + tn], in_=tmp[i][:, 0:tn], func=AF.Identity,
                                                       bias=mt[:, c, k, 1, vec:vec + 1], scale=1.0))
                tmp_free[i] = last
            state["rstd_free"] = (kb.dve, kb.dve.v)
        return last

    def ffn(mt, k, w_gu, w_dn, h_ev):
        wg = w_gu
        pe_last = None
        for hf in range(2):
            j0 = hf * 11
            units = [[wg[:, (j0 + j) * 128:(j0 + j + 1) * 128], wg[:, FF + (j0 + j) * 128:FF + (j0 + j + 1) * 128]]
                     for j in range(11)]
            a_free = pe_last
            dve_last = [None]

            def evac_up(ui, ti, ps, pe_ev):
                t0, tn, vec = TILES[ti]
                i = state["sgi"] % 2
                state["sgi"] += 1
                kb.wait("scalar", pe_ev)
                kb.wait("scalar", sg_free[i])
                ev_s = kb.act.inc(nc.scalar.activation(out=sg[i][:, 0:tn], in_=ps[0], func=AF.Silu))
                kb.wait("vector", ev_s)
                kb.wait("vector", a_free)
                ev_d = kb.dve.inc(nc.vector.tensor_tensor(out=aT[:, ui, t0:t0 + tn], in0=sg[i][:, 0:tn], in1=ps[1], op=ALU.mult))
                sg_free[i] = ev_d
                dve_last[0] = ev_d
                return ev_d

            dense(kb, ws, DC, lambda kc, ti: hT[:, kc, TILES[ti][0]:TILES[ti][0] + TILES[ti][1]], units, evac_up, TILES,
                  G=2, banks=(0, 1, 2, 3), pre_ev=h_ev)

            units_d = [[w_dn[j0 * 128:(j0 + 11) * 128, m * 128:(m + 1) * 128]] for m in range(DC)]
            x_last = [None]

            def evac_dn(ui, ti, ps, pe_ev):
                t0, tn, vec = TILES[ti]
                kb.wait("vector", pe_ev)
                ev = kb.dve.inc(nc.vector.scalar_tensor_tensor(
                    out=xT[:, ui, t0:t0 + tn], in0=ps[0], scalar=mt[:, ui, k, 2, vec:vec + 1],
                    in1=xT[:, ui, t0:t0 + tn], op0=ALU.mult, op1=ALU.add))
                x_last[0] = ev
                return ev

            dense(kb, ws, 11, lambda kc, ti: aT[:, kc, TILES[ti][0]:TILES[ti][0] + TILES[ti][1]], units_d, evac_dn, TILES,
                  G=1, banks=(0, 1, 2, 3), pre_ev=dve_last[0])
            pe_last = (kb.pe, kb.pe.v)
            state["x_ev"] = x_last[0]
        return pe_last

    h_free = None

    if pre:
        ev_m = ld_m.inc(nc.sync.dma_start(out=modp[:].rearrange("p a b c d -> p (a b c d)"), in_=modp_in), 16)
        ev_m = ld_m.inc(nc.sync.dma_start(out=bo_t[:], in_=b_o), 16)
        ev_o = None
        for c in range(DC):
            eng = nc.sync if c % 2 == 0 else nc.scalar
            ev_o = ld_o.inc(eng.dma_start(out=hT[:, c, :], in_=oT_in[c * 128:(c + 1) * 128, :]), 16)
        units = [[w_o[:, m * 128:(m + 1) * 128]] for m in range(DC)]
        x_last = [None]

        def evac_o(ui, ti, ps, pe_ev):
            t0, tn, vec = TILES[ti]
            i = state["tmpi"] % 2
            state["tmpi"] += 1
            kb.wait("scalar", pe_ev)
            kb.wait("scalar", ev_m)
            kb.wait("scalar", tmp_free[i])
            ev_a = kb.act.inc(nc.scalar.activation(out=tmp[i][:, 0:tn], in_=ps[0], func=AF.Identity,
                                                   bias=bo_t[:, ui:ui + 1], scale=1.0))
            kb.wait("vector", ev_a)
            kb.wait("vector", ev_m)
            kb.wait("vector", ev_x)
            ev = kb.dve.inc(nc.vector.scalar_tensor_tensor(
                out=xT[:, ui, t0:t0 + tn], in0=tmp[i][:, 0:tn], scalar=modp[:, ui, 1, 2, vec:vec + 1],
                in1=xT[:, ui, t0:t0 + tn], op0=ALU.mult, op1=ALU.add))
            tmp_free[i] = ev
            x_last[0] = ev
            return ev_a

        dense(kb, ws, DC, lambda kc, ti: hT[:, kc, TILES[ti][0]:TILES[ti][0] + TILES[ti][1]], units, evac_o, TILES,
              G=1, banks=(0, 1, 2, 3), pre_ev=ev_o)
        state["x_ev"] = x_last[0]
        h_free = (kb.pe, kb.pe.v)
        h_ev = norm_mod(modp, 2, h_free)
        h_free = ffn(modp, 2, w_gu2, w_dn2, h_ev)

    if post:
        c_sb = kb.sb("c_sb", [128, DC, 2], F32)
        sc = kb.sb("sc", [128, DC, 2], F32)
        bm = kb.sb("bm", [128, 72], F32)
        nw = kb.sb("nw", [128, 3, DC], F32)
        raw = kb.sb("raw", [128, 9, DC, 2], F32)
        wm = [kb.sb(f"wm{i}", [128, DC, 128], F32) for i in range(3)]
        wm_ld = [kb.track(f"t_wm{i}") for i in range(3)]
        wm_free = [None] * 3
        ev_c = ld_c.inc(nc.sync.dma_start(out=c_sb[:].rearrange("p a b -> p (a b)"), in_=cT), 16)
        ev_c = ld_c.inc(nc.sync.dma_start(out=bm[:], in_=b_mod), 16)
        ev_c = ld_c.inc(nc.sync.dma_start(out=nw[:].rearrange("p a b -> p (a b)"), in_=norm_w), 16)
        kb.wait("scalar", ev_c)
        ev_sc = kb.act.inc(nc.scalar.activation(out=sc[:], in_=c_sb[:], func=AF.Silu))
        kb.wait("tensor", ev_sc)
        kb.wait("tensor", kb.bank_free[5])
        psm = kb.banks[5]
        mm = None
        for n in range(72):
            s = n % 3
            kb.wait("sync", wm_free[s])
            ev_w = wm_ld[s].inc(nc.sync.dma_start(out=wm[s][:], in_=w_mod[:, n * 128:(n + 1) * 128].rearrange("(k p) m -> p k m", p=128)), 16)
            kb.wait("tensor", ev_w)
            for kc in range(DC):
                mm = nc.tensor.matmul(psm[:, 2 * n:2 * n + 2], wm[s][:, kc, :], sc[:, kc, :], start=(kc == 0), stop=(kc == DC - 1))
            wm_free[s] = kb.pe.inc(mm)
        ev_pm = wm_free[(72 - 1) % 3]
        kb.wait("vector", ev_pm)
        kb.wait("vector", ev_c)
        nc.vector.tensor_tensor(out=raw[:].rearrange("p m c v -> p (m c) v"),
                                in0=psm[:, 0:144].rearrange("p (n v) -> p n v", v=2),
                                in1=bm[:].unsqueeze(2).to_broadcast([128, 72, 2]), op=ALU.add)
        for k in range(3):
            nc.vector.scalar_tensor_tensor(out=modt[:, :, k, 0, :], in0=raw[:, 3 * k + 1, :, :], scalar=1.0,
                                           in1=nw[:, k, :].unsqueeze(2).to_broadcast([128, DC, 2]),
                                           op0=ALU.add, op1=ALU.mult)
            nc.vector.tensor_copy(out=modt[:, :, k, 1, :], in_=raw[:, 3 * k, :, :])
            ev_mt = kb.dve.inc(nc.vector.tensor_scalar(out=modt[:, :, k, 2, :], in0=raw[:, 3 * k + 2, :, :],
                                                       scalar1=(1.0 if k == 1 else 0.5), scalar2=None, op0=ALU.mult))
        kb.bank_free[5] = ev_mt
        kb.wait("scalar", ev_mt)
        kb.wait("sync", ev_mt)
        kb.out_evs.append(st.inc(nc.sync.dma_start(out=mod_out, in_=modt[:].rearrange("p a b c d -> p (a b c d)")), 16))

        h_ev = norm_mod(modt, 0, h_free)
        h_free = ffn(modt, 0, w_gu1, w_dn1, h_ev)
        h_ev = norm_mod(modt, 1, h_free)
        if post == "attn":
            kb.wait("sync", h_ev)
            for c in range(DC):
                eng = nc.sync
                kb.out_evs.append(st.inc(eng.dma_start(out=hT_out[c * 128:(c + 1) * 128, :], in_=hT[:, c, :]), 16))
        else:
            bi = kb.sb("bi", [128, 24], F32)
            ev_bi = ld_b.inc(nc.sync.dma_start(out=bi[:], in_=b_in), 16)
            units = [[w_in[:, m * 128:(m + 1) * 128]] for m in range(24)]
            st_free = {}

            def evac_p(ui, ti, ps, pe_ev):
                t0, tn, vec = TILES[ti]
                slot = ui % 2
                kb.wait("scalar", pe_ev)
                kb.wait("scalar", ev_bi)
                if ti == 0:
                    kb.wait("scalar", st_free.get(slot))
                ev_a = kb.act.inc(nc.scalar.activation(out=aT[:, slot, t0:t0 + tn], in_=ps[0], func=AF.Identity,
                                                       bias=bi[:, ui:ui + 1], scale=1.0))
                if ti == len(TILES) - 1:
                    kb.wait("sync", ev_a)
                    e = st_slot[slot].inc(nc.sync.dma_start(out=pT_out[ui * 128:(ui + 1) * 128, :], in_=aT[:, slot, :]), 16)
                    st_free[slot] = e
                    kb.out_evs.append(e)
                return ev_a

            kb.wait("scalar", h_free)
            dense(kb, ws, DC, lambda kc, ti: hT[:, kc, TILES[ti][0]:TILES[ti][0] + TILES[ti][1]], units, evac_p, TILES,
                  G=1, banks=(0, 1, 2, 3), pre_ev=h_ev)

    kb.wait("sync", state["x_ev"])
    for c in range(DC):
        kb.out_evs.append(st.inc(nc.sync.dma_start(out=xT_out[c * 128:(c + 1) * 128, :], in_=xT[:, c, :]), 16))
    return kb.finish()


NTOK = CTX + SEQ
ATILES = [(0, 256)] + [(256 + 512 * i, 512) for i in range(16)]
NKC = NTOK // 128


def build_attn():
    kb = KB(wide_banks=3)
    nc = kb.nc
    IN, OUT = "ExternalInput", "ExternalOutput"
    hT = kb.dram("hT", [D, NTOK], BF16, IN)
    wq = kb.dram("wq", [D, 256], F32, IN)
    wkk = kb.dram("wkk", [D, 128], F32, IN)
    wv = kb.dram("wv", [D, 64], F32, IN)
    nrm = kb.dram("nrm", [128, 2], F32, IN)
    cosT = kb.dram("cosT", [128, NTOK], F32, IN)
    sinT = kb.dram("sinT", [128, NTOK], F32, IN)
    rotm = kb.dram("rotm", [128, 128], F32, IN)
    blk = kb.dram("blk", [128, 128], F32, IN)
    oT = kb.dram("oT", [256, NTOK], BF16, OUT)

    qT = kb.sb("qT", [128, 2, NTOK], BF16)
    kA = kb.sb("kA", [128, NTOK], BF16)
    kB = kb.sb("kB", [128, NTOK], BF16)
    vaug = kb.sb("vaug", [128, NKC, 128], BF16)
    wq_sb = kb.sb("wq_sb", [128, DC, 256], BF16)
    wkk_sb = kb.sb("wkk_sb", [128, DC, 128], BF16)
    wv_sb = kb.sb("wv_sb", [128, DC, 64], BF16)
    rot_sb = kb.sb("rot_sb", [128, 128], BF16)
    blk_sb = kb.sb("blk_sb", [128, 128], BF16)
    nrm_sb = kb.sb("nrm_sb", [128, 2], F32)
    epsb = kb.sb("epsb", [128, 1], F32)
    htile = [kb.sb(f"htile{i}", [128, DC, 512], BF16) for i in range(2)]
    ctile = [kb.sb(f"ctile{i}", [128, 512], F32) for i in range(2)]
    stile = [kb.sb(f"stile{i}", [128, 512], F32) for i in range(2)]
    sqt = kb.sb("sqt", [128, 512], BF16)
    sd = kb.sb("sd", [128, 512], F32)
    qn = kb.sb("qn", [128, 512], BF16)
    t1 = kb.sb("t1", [128, 512], F32)
    t2 = kb.sb("t2", [128, 512], F32)
    rec = kb.sb("rec", [128, 512], F32)
    ostage = [kb.sb(f"ostage{i}", [64, 512], BF16) for i in range(2)]

    ld_w = kb.track("t_ldw")
    ld_h = [kb.track("t_ldh0"), kb.track("t_ldh1")]
    st_o = [kb.track("t_sto0"), kb.track("t_sto1")]

    ev_w = ld_w.inc(nc.gpsimd.dma_start(out=wq_sb[:], in_=wq.rearrange("(k p) m -> p k m", p=128)), 16)
    ev_w = ld_w.inc(nc.gpsimd.dma_start(out=wkk_sb[:], in_=wkk.rearrange("(k p) m -> p k m", p=128)), 16)
    ev_w = ld_w.inc(nc.gpsimd.dma_start(out=wv_sb[:], in_=wv.rearrange("(k p) m -> p k m", p=128)), 16)
    ev_w = ld_w.inc(nc.gpsimd.dma_start(out=rot_sb[:], in_=rotm), 16)
    ev_w = ld_w.inc(nc.gpsimd.dma_start(out=blk_sb[:], in_=blk), 16)
    ev_w = ld_w.inc(nc.gpsimd.dma_start(out=nrm_sb[:], in_=nrm), 16)
    nc.gpsimd.memset(epsb[:], EPS)
    nc.gpsimd.memset(kA[64:128, :], 0.0)
    nc.gpsimd.memset(kB[0:64, :], 0.0)
    ev_pool = kb.pool.inc(nc.gpsimd.memset(vaug[:, :, 64:128], 1.0))

    h_free = [None, None]
    c_free = [None, None]

    def load_tile(ti):
        t0, tn = ATILES[ti]
        s = ti % 2
        kb.wait("sync", h_free[s])
        kb.wait("sync", c_free[s])
        ev = ld_h[s].inc(nc.sync.dma_start(out=htile[s][:, :, 0:tn], in_=hT[:, t0:t0 + tn].rearrange("(k p) n -> p k n", p=128)), 16)
        ev = ld_h[s].inc(nc.sync.dma_start(out=ctile[s][:, 0:tn], in_=cosT[:, t0:t0 + tn]), 16)
        ev = ld_h[s].inc(nc.sync.dma_start(out=stile[s][:, 0:tn], in_=sinT[:, t0:t0 + tn]), 16)
        return ev

    kb.wait("tensor", ev_w)
    kb.wait("vector", ev_w)
    kb.wait("scalar", ev_w)
    kb.wait("scalar", ev_pool)
    kb.wait("vector", ev_pool)
    kb.wait("tensor", ev_pool)
    pend = load_tile(0)
    sqt_free = None
    sd_free = None
    qn_free = None
    dve_last = None
    for ti, (t0, tn) in enumerate(ATILES):
        s = ti % 2
        ev_h = pend
        if ti + 1 < len(ATILES):
            pend = load_tile(ti + 1)
        kb.wait("tensor", ev_h)
        kb.wait("vector", ev_h)
        groups = [(wq_sb, 0, 0, lambda: qT[:, 0, t0:t0 + tn]), (wq_sb, 128, 0, lambda: qT[:, 1, t0:t0 + tn]),
                  (wkk_sb, 0, 1, None)]
        for (wsb, c0, ni, dst) in groups:
            kb.wait("tensor", kb.bank_free[5])
            mm = None
            for kc in range(DC):
                mm = nc.tensor.matmul(kb.banks[5][:, 0:tn], wsb[:, kc, c0:c0 + 128], htile[s][:, kc, 0:tn],
                                      start=(kc == 0), stop=(kc == DC - 1))
            ev_a = kb.pe.inc(mm)
            kb.wait("scalar", ev_a)
            kb.wait("scalar", sqt_free)
            ev_sq = kb.act.inc(nc.scalar.activation(out=sqt[:, 0:tn], in_=kb.banks[5][:, 0:tn], func=AF.Square))
            kb.wait("tensor", ev_sq)
            kb.wait("tensor", kb.bank_free[6])
            ev_b = kb.pe.inc(nc.tensor.matmul(kb.banks[6][:, 0:tn], blk_sb[:, :], sqt[:, 0:tn], start=True, stop=True))
            sqt_free = ev_b
            kb.wait("scalar", ev_b)
            kb.wait("scalar", sd_free)
            ev_sd = kb.act.inc(nc.scalar.activation(out=sd[:, 0:tn], in_=kb.banks[6][:, 0:tn], func=AF.Sqrt,
                                                    bias=epsb[:, 0:1], scale=1.0 / 64))
            kb.bank_free[6] = ev_sd
            kb.wait("vector", ev_sd)
            nc.vector.reciprocal(out=sd[:, 0:tn], in_=sd[:, 0:tn])
            kb.wait("vector", ev_a)
            kb.wait("vector", qn_free)
            ev_qn = kb.dve.inc(nc.vector.scalar_tensor_tensor(out=qn[:, 0:tn], in0=kb.banks[5][:, 0:tn],
                                                              scalar=nrm_sb[:, ni:ni + 1], in1=sd[:, 0:tn],
                                                              op0=ALU.mult, op1=ALU.mult))
            kb.bank_free[5] = ev_qn
            sd_free = ev_qn
            kb.wait("tensor", ev_qn)
            kb.wait("tensor", kb.bank_free[7])
            ev_c = kb.pe.inc(nc.tensor.matmul(kb.banks[7][:, 0:tn], rot_sb[:, :], qn[:, 0:tn], start=True, stop=True))
            nc.vector.tensor_tensor(out=t1[:, 0:tn], in0=qn[:, 0:tn], in1=ctile[s][:, 0:tn], op=ALU.mult)
            kb.wait("vector", ev_c)
            nc.vector.tensor_tensor(out=t2[:, 0:tn], in0=kb.banks[7][:, 0:tn], in1=stile[s][:, 0:tn], op=ALU.mult)
            if dst is not None:
                dve_last = kb.dve.inc(nc.vector.tensor_tensor(out=dst(), in0=t1[:, 0:tn], in1=t2[:, 0:tn], op=ALU.add))
            else:
                nc.vector.tensor_tensor(out=kA[0:64, t0:t0 + tn], in0=t1[0:64, 0:tn], in1=t2[0:64, 0:tn], op=ALU.add)
                dve_last = kb.dve.inc(nc.vector.tensor_tensor(out=kB[64:128, t0:t0 + tn], in0=t1[64:128, 0:tn],
                                                              in1=t2[64:128, 0:tn], op=ALU.add))
            kb.bank_free[7] = dve_last
            qn_free = ev_c
        nch = tn // 128
        kb.wait("tensor", kb.bank_free[3])
        mm = None
        for ci in range(nch):
            for kc in range(DC):
                mm = nc.tensor.matmul(kb.banks[3][:, ci * 64:(ci + 1) * 64], htile[s][:, kc, ci * 128:(ci + 1) * 128],
                                      wv_sb[:, kc, :], start=(kc == 0), stop=(kc == DC - 1))
        ev_v = kb.pe.inc(mm)
        h_free[s] = ev_v
        c_free[s] = dve_last
        kb.wait("scalar", ev_v)
        c_first = t0 // 128
        ev_vc = kb.act.inc(nc.scalar.activation(out=vaug[:, c_first:c_first + nch, 0:64],
                                                in_=kb.banks[3][:, 0:nch * 64].rearrange("p (c d) -> p c d", d=64),
                                                func=AF.Identity))
        kb.bank_free[3] = ev_vc
    ev_A_dve = dve_last
    ev_A_act = (kb.act, kb.act.v)

    iters = []
    for hd in range(4):
        for qi, (q0, nq) in enumerate(ATILES):
            npair = 1 if qi == 0 else NKC // 2
            for kp in range(npair):
                iters.append((hd, qi, kp, kp == 0, kp == npair - 1))
    N = len(iters)
    evS = [None] * N
    evP = [None] * N
    pt2 = [kb.sb(f"pt2_{i}", [128, 2, 512], BF16) for i in range(3)]
    pt_free = [None] * 3
    o_st_free = [None, None]
    grp = [0]
    OB = (6, 7)

    def emit_S(n):
        hd, qi, kp, first, last = iters[n]
        q0, nq = ATILES[qi]
        r0 = (hd % 2) * 64
        w = n % 3
        kb.wait("tensor", kb.bank_free[2 * w])
        kb.wait("tensor", kb.bank_free[2 * w + 1])
        kb.wait("tensor", ev_A_dve)
        mm = None
        for j in range(2):
            kc = 2 * kp + j
            kz = kA if hd % 2 == 0 else kB
            mm = nc.tensor.matmul(kb.banks[2 * w + j][:, 0:nq], kz[:, kc * 128:(kc + 1) * 128],
                                  qT[:, hd // 2, q0:q0 + nq], start=True, stop=True)
        evS[n] = kb.pe.inc(mm)

    emit_S(0)
    for n in range(N):
        hd, qi, kp, first, last = iters[n]
        q0, nq = ATILES[qi]
        if n + 1 < N:
            emit_S(n + 1)
        w = n % 3
        sl = n % 3
        kb.wait("scalar", evS[n])
        kb.wait("scalar", pt_free[sl])
        evP[n] = kb.act.inc(nc.scalar.activation(out=pt2[sl][:, :, 0:nq],
                                                 in_=kb.wide[w][:, :].rearrange("p (j n) -> p j n", j=2)[:, :, 0:nq],
                                                 func=AF.Exp, scale=0.125))
        kb.bank_free[2 * w] = evP[n]
        kb.bank_free[2 * w + 1] = evP[n]
        ob = OB[grp[0] % 2]
        kb.wait("tensor", evP[n])
        if first:
            kb.wait("tensor", kb.bank_free[ob])
            kb.wait("tensor", ev_A_act)
        ev_pv = None
        for j in range(2):
            kc = 2 * kp + j
            ev_pv = nc.tensor.matmul(kb.banks[ob][:, 0:nq], vaug[:, kc, :], pt2[sl][:, j, 0:nq],
                                     start=(first and j == 0), stop=(last and j == 1))
        ev_pv = kb.pe.inc(ev_pv)
        pt_free[sl] = ev_pv
        if last:
            os_ = grp[0] % 2
            kb.wait("vector", ev_pv)
            nc.vector.reciprocal(out=rec[64:128, 0:nq], in_=kb.banks[ob][64:128, 0:nq])
            kb.wait("vector", o_st_free[os_])
            ev_e = kb.dve.inc(nc.vector.tensor_tensor(out=ostage[os_][:, 0:nq], in0=kb.banks[ob][0:64, 0:nq],
                                                      in1=rec[64:128, 0:nq], op=ALU.mult))
            kb.bank_free[ob] = ev_e
            kb.wait("sync", ev_e)
            e = st_o[os_].inc(nc.sync.dma_start(out=oT[hd * 64:(hd + 1) * 64, q0:q0 + nq], in_=ostage[os_][:, 0:nq]), 16)
            o_st_free[os_] = e
            kb.out_evs.append(e)
            grp[0] += 1
    return kb.finish()


def attn_consts():
    p = np.arange(128)
    d = p % 64
    fidx = (d % 16).astype(np.float32)
    freqs = (np.float32(10000.0) ** (-fidx / np.float32(16.0))).astype(np.float32)
    t = np.arange(SEQ)
    rows = (t // 64).astype(np.float32)
    cols = (t % 64).astype(np.float32)
    pos = np.where((d < 32)[:, None], rows[None, :], cols[None, :]).astype(np.float32)
    ang = (pos * freqs[:, None]).astype(np.float32)
    cosT = np.ones((128, NTOK), np.float32)
    sinT = np.zeros((128, NTOK), np.float32)
    cosT[:, CTX:] = np.cos(ang)
    sinT[:, CTX:] = np.sin(ang)
    rotm = np.zeros((128, 128), np.float32)
    for m in range(128):
        if (m % 32) < 16:
            rotm[m + 16, m] = -1.0
        else:
            rotm[m - 16, m] = 1.0
    blk = np.zeros((128, 128), np.float32)
    blk[:64, :64] = 1.0
    blk[64:, 64:] = 1.0
    return cosT, sinT, rotm, blk


NFFT = 16384
CG = 32
NG = 4
HALF_PI = float(np.pi / 2)


def dft_tables():
    a = np.arange(128)
    ang = 2.0 * np.pi * np.outer(a, a) / 128.0
    C = np.cos(ang)
    S = np.sin(ang)
    angt = 2.0 * np.pi * np.outer(a, a) / NFFT
    Twr = np.cos(angt)
    Twi = -np.sin(angt)
    f = lambda *xs: np.ascontiguousarray(np.concatenate(xs, axis=1)).astype(np.float32)
    tabs = {
        "FT1": f(C, -S),
        "FS1": np.ascontiguousarray(np.concatenate([np.concatenate([C[:64], -S[:64]], 1),
                                                    np.concatenate([S[:64], C[:64]], 1)], 0)).astype(np.float32),
        "TW": f(Twr, Twi),
        "FS3": f(C, S, -S),
        "R12": f(C, S, -S, C),
        "LAB": f(C[:, :64], S[:, :64], -S[:, :64], C[:, :64]),
    }
    return tabs


def build_hf(debug=False):
    kb = KB()
    nc = kb.nc
    IN, OUT = "ExternalInput", "ExternalOutput"
    featsT = kb.dram("featsT", [33, NFFT], F32, IN)
    w1 = kb.dram("w1", [33, 64], F32, IN)
    w23 = kb.dram("w23", [64, 128], F32, IN)
    fvec = kb.dram("fvec", [64, 4], F32, IN)
    wout = kb.dram("wout", [64, NG * 128], F32, IN)
    decF = kb.dram("decF", [NG, 128, 128 * CG], F32, IN)
    decB = kb.dram("decB", [NG, 128, 128 * CG], F32, IN)
    FT1 = kb.dram("FT1", [128, 256], F32, IN)
    TW = kb.dram("TW", [128, 256], F32, IN)
    FS3 = kb.dram("FS3", [128, 384], F32, IN)
    Hq = kb.dram("Hq", [2, 128, 2 * 128 * 128], BF16, OUT)

    if debug:
        dbg_hid = kb.dram("dbg_hid", [64, NFFT], BF16, OUT)
        dbg_taps = kb.dram("dbg_taps", [128, 2 * CG * 128], BF16, OUT)
        dbg_ap = kb.dram("dbg_ap", [128, 2 * 2 * CG * 128], BF16, OUT)
    hid3 = kb.sb("hid3", [64, NFFT], BF16)
    w1_sb = kb.sb("w1_sb", [33, 64], F32)
    w23_sb = kb.sb("w23_sb", [64, 128], F32)
    fv = kb.sb("fv", [64, 4], F32)
    a4 = kb.sb("a4", [64, 1], F32)
    ab4 = kb.sb("ab4", [64, 3], F32)
    hpi = kb.sb("hpi", [64, 1], F32)
    wout_sb = kb.sb("wout_sb", [64, NG * 128], BF16)
    ft1_sb = kb.sb("ft1_sb", [128, 256], BF16)
    tw_sb = kb.sb("tw_sb", [128, 256], F32)
    fs3_sb = kb.sb("fs3_sb", [128, 384], BF16)
    ftile = [kb.sb(f"ftile{i}", [33, 512], F32) for i in range(2)]
    mq = [kb.sb(f"mq{i}", [64, 512], F32) for i in range(2)]
    ms = [kb.sb(f"ms{i}", [64, 512], F32) for i in range(2)]
    mc = [kb.sb(f"mc{i}", [64, 512], F32) for i in range(2)]
    mh = [kb.sb(f"mh{i}", [64, 512], F32) for i in range(2)]
    dF = kb.sb("dF", [128, 128, CG], F32)
    dB = kb.sb("dB", [128, 128, CG], F32)
    taps = kb.sb("taps", [128, 2, CG, 128], BF16)
    Ap = kb.sb("Ap", [128, 2, 2 * CG, 128], BF16)
    tA = kb.sb("tA", [128, 256], F32)
    tB = kb.sb("tB", [128, 256], F32)
    tt = [kb.sb(f"tt{i}", [128, 256], F32) for i in range(4)]
    stg = [kb.sb(f"stg{i}", [128, 2, 4, 128], BF16) for i in range(2)]

    ld_c = kb.track("t_ldc")
    ld_f = [kb.track("t_ldf0"), kb.track("t_ldf1")]
    ld_d = kb.track("t_ldd")
    st_s = [kb.track("t_sts0"), kb.track("t_sts1")]

    ev_c = ld_c.inc(nc.sync.dma_start(out=w1_sb[:], in_=w1), 16)
    ev_c = ld_c.inc(nc.sync.dma_start(out=w23_sb[:], in_=w23), 16)
    ev_c = ld_c.inc(nc.sync.dma_start(out=fv[:], in_=fvec), 16)
    ev_c = ld_c.inc(nc.gpsimd.dma_start(out=wout_sb[:], in_=wout), 16)
    ev_c = ld_c.inc(nc.sync.dma_start(out=tw_sb[:], in_=TW), 16)
    ev_c = ld_c.inc(nc.gpsimd.dma_start(out=ft1_sb[:], in_=FT1), 16)
    ev_c = ld_c.inc(nc.gpsimd.dma_start(out=fs3_sb[:], in_=FS3), 16)
    for e in ("tensor", "vector", "scalar"):
        kb.wait(e, ev_c)
    nc.vector.memset(hpi[:], HALF_PI)
    ev_k = kb.dve.inc(nc.vector.tensor_scalar(out=a4[:], in0=fv[:, 0:1], scalar1=0.25, scalar2=None, op0=ALU.mult))
    kb.wait("vector", ev_k)
    ev_k = kb.dve.inc(nc.vector.tensor_scalar(out=ab4[:], in0=fv[:, 1:4], scalar1=a4[:, 0:1], scalar2=None, op0=ALU.mult))
    kb.wait("vector", ev_k)
    kb.wait("scalar", ev_k)

    f_free = [None, None]

    def load_feats(ti):
        s = ti % 2
        kb.wait("sync", f_free[s])
        return ld_f[s].inc(nc.sync.dma_start(out=ftile[s][:, :], in_=featsT[:, ti * 512:(ti + 1) * 512]), 16)

    buf_free = [None, None]
    q_free = [None, None]

    def layer(ti, ly, rhs_ap, K, w_ap, out_ap, rhs_ev, bank):
        s = ti % 2
        kb.wait("tensor", rhs_ev)
        kb.wait("tensor", kb.bank_free[bank])
        ev_m = kb.pe.inc(nc.tensor.matmul(kb.banks[bank][0:64, 0:512], w_ap, rhs_ap, start=True, stop=True))
        kb.wait("vector", ev_m)
        kb.wait("vector", q_free[s])
        ev_q = kb.dve.inc(nc.vector.tensor_scalar(out=mq[s][:, :], in0=kb.banks[bank][0:64, 0:512], scalar1=a4[:, 0:1],
                                                  scalar2=ab4[:, ly:ly + 1], op0=ALU.mult, op1=ALU.add))
        kb.bank_free[bank] = ev_q
        kb.wait("scalar", ev_q)
        nc.scalar.activation(out=ms[s][:, :], in_=mq[s][:, :], func=AF.Sin)
        ev_s = kb.act.inc(nc.scalar.activation(out=mc[s][:, :], in_=mq[s][:, :], func=AF.Sin, bias=hpi[:, 0:1], scale=1.0))
        q_free[s] = ev_s
        kb.wait("vector", ev_s)
        nc.vector.tensor_tensor(out=mc[s][:, :], in0=ms[s][:, :], in1=mc[s][:, :], op=ALU.mult)
        nc.vector.tensor_tensor(out=ms[s][:, :], in0=ms[s][:, :], in1=ms[s][:, :], op=ALU.mult)
        nc.vector.tensor_scalar(out=ms[s][:, :], in0=ms[s][:, :], scalar1=-8.0, scalar2=4.0, op0=ALU.mult, op1=ALU.add)
        if out_ap is None:
            kb.wait("vector", buf_free[s])
            o = mh[s][:, :]
        else:
            o = out_ap
        ev_h = kb.dve.inc(nc.vector.tensor_tensor(out=o, in0=mc[s][:, :], in1=ms[s][:, :], op=ALU.mult))
        return ev_h, ev_m

    pend = [load_feats(0), load_feats(1)]
    for tp in range(0, 32, 2):
        evs = [pend[0], pend[1]]
        hev = [None, None]
        for ly in range(3):
            for j in range(2):
                ti = tp + j
                s = ti % 2
                if ly == 0:
                    hev[j], ev_m = layer(ti, 0, ftile[s][:, :], 33, w1_sb[:, :], None, evs[j], 5 + j)
                    f_free[s] = ev_m
                elif ly == 1:
                    hev[j], ev_m = layer(ti, 1, mh[s][:, :], 64, w23_sb[:, 0:64], None, hev[j], 5 + j)
                    buf_free[s] = ev_m
                else:
                    hev[j], ev_m = layer(ti, 2, mh[s][:, :], 64, w23_sb[:, 64:128], hid3[:, ti * 512:(ti + 1) * 512], hev[j], 5 + j)
                    buf_free[s] = ev_m
        if tp + 2 < 32:
            pend = [load_feats(tp + 2), load_feats(tp + 3)]
    ev_hid = (kb.dve, kb.dve.v)

    d_free = None
    taps_free = None
    ap_free = None
    it4 = [0]
    st_free_hf = [None, None]
    for g in range(NG):
        kb.wait("sync", d_free)
        kb.wait("scalar", d_free)
        ev_d = ld_d.inc(nc.sync.dma_start(out=dF[:].rearrange("p n c -> p (n c)"), in_=decF[g]), 16)
        ev_d = ld_d.inc(nc.scalar.dma_start(out=dB[:].rearrange("p n c -> p (n c)"), in_=decB[g]), 16)
        kb.wait("vector", ev_d)
        kb.wait("tensor", ev_hid)
        for nb in range(32):
            bank = nb % 2
            kb.wait("tensor", kb.bank_free[bank])
            mm = None
            for q in range(4):
                n2 = nb * 4 + q
                mm = nc.tensor.matmul(kb.banks[bank][:, q * 128:(q + 1) * 128], hid3[:, n2:NFFT:128],
                                      wout_sb[:, g * 128:(g + 1) * 128], start=True, stop=True)
            ev_m = kb.pe.inc(mm)
            kb.wait("vector", ev_m)
            psv = kb.banks[bank][:, 0:512].rearrange("p (n o d c) -> p n o d c", n=4, o=2, d=2)
            nc.vector.tensor_tensor(out=tA[:, :].rearrange("p (n o c) -> p n o c", n=4, o=2), in0=psv[:, :, :, 0, :],
                                    in1=dF[:, nb * 4:nb * 4 + 4, :].unsqueeze(2).to_broadcast([128, 4, 2, CG]), op=ALU.mult)
            ev_b = kb.dve.inc(nc.vector.tensor_tensor(out=tB[:, :].rearrange("p (n o c) -> p n o c", n=4, o=2), in0=psv[:, :, :, 1, :],
                                                      in1=dB[:, nb * 4:nb * 4 + 4, :].unsqueeze(2).to_broadcast([128, 4, 2, CG]), op=ALU.mult))
            kb.bank_free[bank] = ev_b
            if nb == 0:
                kb.wait("vector", taps_free)
            ev_t = kb.dve.inc(nc.vector.tensor_tensor(out=taps[:, :, :, nb * 4:nb * 4 + 4].rearrange("p o c n -> p n o c"),
                                                      in0=tA[:, :].rearrange("p (n o c) -> p n o c", n=4, o=2),
                                                      in1=tB[:, :].rearrange("p (n o c) -> p n o c", n=4, o=2), op=ALU.add))
        d_free = ev_t
        kb.wait("tensor", ev_t)
        for pr in range(CG):
            bank = 2 + pr % 2
            kb.wait("tensor", kb.bank_free[bank])
            mm = None
            for j in range(2):
                sq = 2 * pr + j
                o, c = sq // CG, sq % CG
                mm = nc.tensor.matmul(kb.banks[bank][:, j * 256:(j + 1) * 256], taps[:, o, c, :], ft1_sb[:, :], start=True, stop=True)
            ev_m = kb.pe.inc(mm)
            kb.wait("vector", ev_m)
            if pr == 0:
                kb.wait("vector", ap_free)
            psv = kb.banks[bank][:, 0:512].rearrange("p (s r k) -> p s r k", s=2, r=2)
            twr = tw_sb[:, 0:128].unsqueeze(1).to_broadcast([128, 2, 128])
            twi = tw_sb[:, 128:256].unsqueeze(1).to_broadcast([128, 2, 128])
            v4 = [t[:, :].rearrange("p (s k) -> p s k", s=2) for t in tt]
            nc.vector.tensor_tensor(out=v4[0], in0=psv[:, :, 0, :], in1=twr, op=ALU.mult)
            nc.vector.tensor_tensor(out=v4[1], in0=psv[:, :, 1, :], in1=twi, op=ALU.mult)
            nc.vector.tensor_tensor(out=v4[2], in0=psv[:, :, 0, :], in1=twi, op=ALU.mult)
            ev_b = kb.dve.inc(nc.vector.tensor_tensor(out=v4[3], in0=psv[:, :, 1, :], in1=twr, op=ALU.mult))
            kb.bank_free[bank] = ev_b
            nc.vector.tensor_tensor(out=Ap[:, 0, 2 * pr:2 * pr + 2, :], in0=v4[0], in1=v4[1], op=ALU.subtract)
            ev_a = kb.dve.inc(nc.vector.tensor_tensor(out=Ap[:, 1, 2 * pr:2 * pr + 2, :], in0=v4[2], in1=v4[3], op=ALU.add))
        taps_free = (kb.pe, kb.pe.v)
        kb.wait("tensor", ev_a)
        for blk4 in range(16):
            s0 = blk4 * 4
            o, c0 = s0 // CG, s0 % CG
            for bsel in (4, 5):
                kb.wait("tensor", kb.bank_free[bsel])
            rr = Ap[:, 0, s0:s0 + 4, :]
            ri_ = Ap[:, 1, s0:s0 + 4, :]
            nc.tensor.matmul(kb.banks[4][:, 0:512], fs3_sb[:, 0:128], rr, start=True, stop=False)
            nc.tensor.matmul(kb.banks[4][:, 0:512], fs3_sb[:, 128:256], ri_, start=False, stop=True)
            nc.tensor.matmul(kb.banks[5][:, 0:512], fs3_sb[:, 256:384], rr, start=True, stop=False)
            ev_m = kb.pe.inc(nc.tensor.matmul(kb.banks[5][:, 0:512], fs3_sb[:, 0:128], ri_, start=False, stop=True))
            sl = it4[0] % 2
            it4[0] += 1
            kb.wait("scalar", ev_m)
            kb.wait("scalar", st_free_hf[sl])
            nc.scalar.activation(out=stg[sl][:, 0, :, :], in_=kb.banks[4][:, 0:512].rearrange("p (s k) -> p s k", s=4),
                                 func=AF.Identity, scale=1.0 / NFFT)
            ev_e = kb.act.inc(nc.scalar.activation(out=stg[sl][:, 1, :, :], in_=kb.banks[5][:, 0:512].rearrange("p (s k) -> p s k", s=4),
                                                   func=AF.Identity, scale=1.0 / NFFT))
            kb.bank_free[4] = ev_e
            kb.bank_free[5] = ev_e
            kb.wait("sync", ev_e)
            cc = g * CG + c0
            dst = Hq[o].rearrange("p (r c k) -> p r c k", r=2, c=128)[:, :, cc:cc + 4, :]
            e = st_s[sl].inc(nc.sync.dma_start(out=dst, in_=stg[sl][:, :, :, :]), 16)
            st_free_hf[sl] = e
            kb.out_evs.append(e)
        ap_free = (kb.pe, kb.pe.v)
    if debug:
        kb.wait("sync", (kb.dve, kb.dve.v))
        kb.wait("sync", (kb.pe, kb.pe.v))
        for dst, src in ((dbg_hid, hid3[:, :]), (dbg_taps, taps[:].rearrange("p o c n -> p (o c n)")),
                         (dbg_ap, Ap[:].rearrange("p r s k -> p (r s k)"))):
            kb.out_evs.append(st_s[0].inc(nc.sync.dma_start(out=dst, in_=src), 16))
    return kb.finish()


def build_hc():
    kb = KB()
    nc = kb.nc
    IN, OUT = "ExternalInput", "ExternalOutput"
    p3 = kb.dram("p3", [NG, 128, 3 * CG * 130], BF16, IN)
    cw = kb.dram("cw", [128, 9 * 128], F32, IN)
    cb = kb.dram("cb", [128, 3 * 128], F32, IN)
    fb = kb.dram("fb", [128, 2 * 128], F32, IN)
    mask = kb.dram("mask", [128, 1], F32, IN)
    Hq = kb.dram("Hq", [2, 128, 2 * 128 * 128], BF16, IN)
    FS1 = kb.dram("FS1", [128, 256], F32, IN)
    TW = kb.dram("TW", [128, 256], F32, IN)
    FS3 = kb.dram("FS3", [128, 384], F32, IN)
    R12 = kb.dram("R12", [128, 512], F32, IN)
    LAB = kb.dram("LAB", [128, 256], F32, IN)
    yTL = kb.dram("yTL", [128, 128 * 128], BF16, OUT)

    fs1_sb = kb.sb("fs1_sb", [128, 256], BF16)
    tw_sb = kb.sb("tw_sb", [128, 256], F32)
    fs3_sb = kb.sb("fs3_sb", [128, 384], BF16)
    r12_sb = kb.sb("r12_sb", [128, 512], BF16)
    lab_sb = kb.sb("lab_sb", [128, 256], BF16)
    cw_sb = kb.sb("cw_sb", [128, 3, 3, 128], F32)
    cb_sb = kb.sb("cb_sb", [128, 3, 128], F32)
    fb_sb = kb.sb("fb_sb", [128, 2, 128], F32)
    mask_sb = kb.sb("mask_sb", [128, 1], F32)
    pt_ = kb.sb("pt_", [128, 3, CG, 130], BF16)
    u = kb.sb("u", [128, 3, CG, 128], BF16)
    uf = kb.sb("uf", [128, CG, 128], F32)
    uf2 = kb.sb("uf2", [128, CG, 128], F32)
    z = kb.sb("z", [128, CG, 128], BF16)
    vfb = kb.sb("vfb", [128, CG, 128], F32)
    Ap = kb.sb("Ap", [128, 2, CG, 128], BF16)
    Y = kb.sb("Y", [128, 2, CG, 128], BF16)
    Bp = kb.sb("Bp", [128, 2, CG, 128], BF16)
    hq = [kb.sb(f"hq{i}", [128, 2, CG, 128], BF16) for i in range(2)]
    tt = [kb.sb(f"tt{i}", [128, 512], F32) for i in range(4)]
    ystage = uf2[:].bitcast(BF16)[:, :, 0:128]

    ld_c = kb.track("t_ldc")
    ld_p = kb.track("t_ldp")
    ld_h = kb.track("t_ldh")
    st_y = kb.track("t_sty")

    ev_c = ld_c.inc(nc.sync.dma_start(out=tw_sb[:], in_=TW), 16)
    ev_c = ld_c.inc(nc.sync.dma_start(out=cw_sb[:].rearrange("p a b c -> p (a b c)"), in_=cw), 16)
    ev_c = ld_c.inc(nc.sync.dma_start(out=cb_sb[:].rearrange("p a c -> p (a c)"), in_=cb), 16)
    ev_c = ld_c.inc(nc.sync.dma_start(out=fb_sb[:].rearrange("p a c -> p (a c)"), in_=fb), 16)
    ev_c = ld_c.inc(nc.sync.dma_start(out=mask_sb[:], in_=mask), 16)
    ev_c = ld_c.inc(nc.gpsimd.dma_start(out=fs1_sb[:], in_=FS1), 16)
    ev_c = ld_c.inc(nc.gpsimd.dma_start(out=fs3_sb[:], in_=FS3), 16)
    ev_c = ld_c.inc(nc.gpsimd.dma_start(out=r12_sb[:], in_=R12), 16)
    ev_c = ld_c.inc(nc.gpsimd.dma_start(out=lab_sb[:], in_=LAB), 16)
    for e in ("tensor", "vector", "scalar", "gpsimd"):
        kb.wait(e, ev_c)

    twr2 = tw_sb[:, 0:128].unsqueeze(1).to_broadcast([128, 2, 128])
    twi2 = tw_sb[:, 128:256].unsqueeze(1).to_broadcast([128, 2, 128])
    v2 = [t[:, 0:256].rearrange("p (s k) -> p s k", s=2) for t in tt]
    v4 = [t[:, 0:512].rearrange("p (s k) -> p s k", s=4) for t in tt]

    st = {"p_free": None, "u_free": None, "hq_free": [None, None], "y_free": None, "ap_free": None, "y2_free": None,
          "bp_free": None, "z_pe_free": None}

    def fft_conv(src_fn, src_ev, hq_t, hq_ev, gate):
        kb.wait("tensor", src_ev)
        ev_a = None
        for pr in range(CG // 2):
            bank = pr % 2
            kb.wait("tensor", kb.bank_free[bank])
            mm = None
            for j in range(2):
                mm = nc.tensor.matmul(kb.banks[bank][:, j * 256:(j + 1) * 256], src_fn(2 * pr + j), fs1_sb[:, :], start=True, stop=True)
            ev_m = kb.pe.inc(mm)
            kb.wait("vector", ev_m)
            if pr == 0:
                kb.wait("vector", st["ap_free"])
            psv = kb.banks[bank][:, 0:512].rearrange("p (s r k) -> p s r k", s=2, r=2)
            nc.vector.tensor_tensor(out=v2[0], in0=psv[:, :, 0, :], in1=twr2, op=ALU.mult)
            nc.vector.tensor_tensor(out=v2[1], in0=psv[:, :, 1, :], in1=twi2, op=ALU.mult)
            nc.vector.tensor_tensor(out=v2[2], in0=psv[:, :, 0, :], in1=twi2, op=ALU.mult)
            ev_b = kb.dve.inc(nc.vector.tensor_tensor(out=v2[3], in0=psv[:, :, 1, :], in1=twr2, op=ALU.mult))
            kb.bank_free[bank] = ev_b
            nc.vector.tensor_tensor(out=Ap[:, 0, 2 * pr:2 * pr + 2, :], in0=v2[0], in1=v2[1], op=ALU.subtract)
            ev_a = kb.dve.inc(nc.vector.tensor_tensor(out=Ap[:, 1, 2 * pr:2 * pr + 2, :], in0=v2[2], in1=v2[3], op=ALU.add))
        src_done = (kb.pe, kb.pe.v)
        kb.wait("tensor", ev_a)
        kb.wait("vector", hq_ev)
        ev_y = None
        for b4 in range(CG // 4):
            c0 = b4 * 4
            br, bi = 2 + 2 * (b4 % 2), 3 + 2 * (b4 % 2)
            kb.wait("tensor", kb.bank_free[br])
            kb.wait("tensor", kb.bank_free[bi])
            rr = Ap[:, 0, c0:c0 + 4, :]
            ri_ = Ap[:, 1, c0:c0 + 4, :]
            nc.tensor.matmul(kb.banks[br][:, 0:512], fs3_sb[:, 0:128], rr, start=True, stop=False)
            nc.tensor.matmul(kb.banks[br][:, 0:512], fs3_sb[:, 128:256], ri_, start=False, stop=True)
            nc.tensor.matmul(kb.banks[bi][:, 0:512], fs3_sb[:, 256:384], rr, start=True, stop=False)
            ev_m = kb.pe.inc(nc.tensor.matmul(kb.banks[bi][:, 0:512], fs3_sb[:, 0:128], ri_, start=False, stop=True))
            kb.wait("vector", ev_m)
            if b4 == 0:
                kb.wait("vector", st["y2_free"])
            xr = kb.banks[br][:, 0:512].rearrange("p (s k) -> p s k", s=4)
            xi = kb.banks[bi][:, 0:512].rearrange("p (s k) -> p s k", s=4)
            hr = hq_t[:, 0, c0:c0 + 4, :]
            hi = hq_t[:, 1, c0:c0 + 4, :]
            nc.vector.tensor_tensor(out=v4[0], in0=xr, in1=hr, op=ALU.mult)
            nc.vector.tensor_tensor(out=v4[1], in0=xi, in1=hi, op=ALU.mult)
            nc.vector.tensor_tensor(out=v4[2], in0=xr, in1=hi, op=ALU.mult)
            ev_b = kb.dve.inc(nc.vector.tensor_tensor(out=v4[3], in0=xi, in1=hr, op=ALU.mult))
            kb.bank_free[br] = ev_b
            kb.bank_free[bi] = ev_b
            nc.vector.tensor_tensor(out=Y[:, 0, c0:c0 + 4, :], in0=v4[0], in1=v4[1], op=ALU.subtract)
            ev_y = kb.dve.inc(nc.vector.tensor_tensor(out=Y[:, 1, c0:c0 + 4, :], in0=v4[2], in1=v4[3], op=ALU.add))
        st["ap_free"] = (kb.pe, kb.pe.v)
        hq_done = ev_y
        kb.wait("tensor", ev_y)
        ev_bp = None
        for pr in range(CG // 2):
            bank = pr % 2
            kb.wait("tensor", kb.bank_free[bank])
            mm = None
            for j in range(2):
                c = 2 * pr + j
                nc.tensor.matmul(kb.banks[bank][:, j * 256:(j + 1) * 256], Y[:, 0, c, :], r12_sb[:, 0:256], start=True, stop=False)
                mm = nc.tensor.matmul(kb.banks[bank][:, j * 256:(j + 1) * 256], Y[:, 1, c, :], r12_sb[:, 256:512], start=False, stop=True)
            ev_m = kb.pe.inc(mm)
            kb.wait("vector", ev_m)
            if pr == 0:
                kb.wait("vector", st["bp_free"])
            psv = kb.banks[bank][:, 0:512].rearrange("p (s r k) -> p s r k", s=2, r=2)
            nc.vector.tensor_tensor(out=v2[0], in0=psv[:, :, 0, :], in1=twr2, op=ALU.mult)
            nc.vector.tensor_tensor(out=v2[1], in0=psv[:, :, 1, :], in1=twi2, op=ALU.mult)
            nc.vector.tensor_tensor(out=v2[2], in0=psv[:, :, 1, :], in1=twr2, op=ALU.mult)
            ev_b = kb.dve.inc(nc.vector.tensor_tensor(out=v2[3], in0=psv[:, :, 0, :], in1=twi2, op=ALU.mult))
            kb.bank_free[bank] = ev_b
            nc.vector.tensor_tensor(out=Bp[:, 0, 2 * pr:2 * pr + 2, :], in0=v2[0], in1=v2[1], op=ALU.add)
            ev_bp = kb.dve.inc(nc.vector.tensor_tensor(out=Bp[:, 1, 2 * pr:2 * pr + 2, :], in0=v2[2], in1=v2[3], op=ALU.subtract))
        st["y2_free"] = (kb.pe, kb.pe.v)
        kb.wait("tensor", ev_bp)
        for b4 in range(CG // 4):
            c0 = b4 * 4
            bank = 2 + b4 % 4
            kb.wait("tensor", kb.bank_free[bank])
            nc.tensor.matmul(kb.banks[bank][:, 0:512], lab_sb[:, 0:128], Bp[:, 0, c0:c0 + 4, :], start=True, stop=False)
            ev_m = kb.pe.inc(nc.tensor.matmul(kb.banks[bank][:, 0:512], lab_sb[:, 128:256], Bp[:, 1, c0:c0 + 4, :], start=False, stop=True))
            kb.wait("vector", ev_m)
            ev_g = gate(c0, kb.banks[bank][:, 0:512].rearrange("p (s k) -> p s k", s=4))
            kb.bank_free[bank] = ev_g
        st["bp_free"] = (kb.pe, kb.pe.v)
        return src_done, hq_done

    for g in range(NG):
        kb.wait("sync", st["p_free"])
        ev_p = ld_p.inc(nc.sync.dma_start(out=pt_[:].rearrange("p j c n -> p (j c n)"), in_=p3[g]), 16)
        hq_evs = []
        for o in range(2):
            kb.wait("scalar", st["hq_free"][o])
            src = Hq[o].rearrange("p (r c k) -> p r c k", r=2, c=128)[:, :, g * CG:(g + 1) * CG, :]
            hq_evs.append(ld_h.inc(nc.scalar.dma_start(out=hq[o][:, :, :, :], in_=src), 16))
        hq_evs[0] = hq_evs[1]
        kb.wait("gpsimd", ev_p)
        kb.wait("gpsimd", st["u_free"])
        kb.wait("gpsimd", st["y_free"])
        ev_u = None
        for j in range(3):
            wv = lambda tp: cw_sb[:, tp, j, g * CG:(g + 1) * CG].unsqueeze(2).to_broadcast([128, CG, 128])
            nc.gpsimd.tensor_tensor(out=uf[:], in0=pt_[:, j, :, 0:128], in1=wv(0), op=ALU.mult)
            nc.gpsimd.tensor_tensor(out=uf2[:], in0=pt_[:, j, :, 1:129], in1=wv(1), op=ALU.mult)
            nc.gpsimd.tensor_tensor(out=uf[:], in0=uf[:], in1=uf2[:], op=ALU.add)
            nc.gpsimd.tensor_tensor(out=uf2[:], in0=pt_[:, j, :, 2:130], in1=wv(2), op=ALU.mult)
            nc.gpsimd.tensor_tensor(out=uf[:], in0=uf[:], in1=uf2[:], op=ALU.add)
            cbv = cb_sb[:, j, g * CG:(g + 1) * CG].unsqueeze(2).to_broadcast([128, CG, 128])
            if j == 0:
                nc.gpsimd.tensor_tensor(out=uf[:], in0=uf[:], in1=cbv, op=ALU.add)
                nc.gpsimd.tensor_scalar(out=u[:, 0, :, :], in0=uf[:], scalar1=mask_sb[:, 0:1], scalar2=None, op0=ALU.mult)
                fbv = fb_sb[:, 0, g * CG:(g + 1) * CG].unsqueeze(2).to_broadcast([128, CG, 128])
                nc.gpsimd.tensor_scalar(out=uf[:], in0=uf[:], scalar1=mask_sb[:, 0:1], scalar2=None, op0=ALU.mult)
                ev_u = kb.pool.inc(nc.gpsimd.tensor_tensor(out=vfb[:], in0=uf[:], in1=fbv, op=ALU.mult))
                ev_v = ev_u
            else:
                ev_u = kb.pool.inc(nc.gpsimd.tensor_tensor(out=u[:, j, :, :], in0=uf[:], in1=cbv, op=ALU.add))
        st["p_free"] = ev_u

        def gate1(c0, ps):
            kb.wait("vector", ev_u)
            if c0 == 0:
                kb.wait("vector", st["z_pe_free"])
            nc.vector.tensor_tensor(out=v4[0], in0=ps, in1=vfb[:, c0:c0 + 4, :], op=ALU.add)
            return kb.dve.inc(nc.vector.scalar_tensor_tensor(out=z[:, c0:c0 + 4, :], in0=v4[0], scalar=mask_sb[:, 0:1],
                                                             in1=u[:, 1, c0:c0 + 4, :], op0=ALU.mult, op1=ALU.mult))

        src_done, hq_done = fft_conv(lambda c: u[:, 0, c, :], ev_v, hq[0], hq_evs[0], gate1)
        st["hq_free"][0] = hq_done
        ev_z = (kb.dve, kb.dve.v)
        kb.wait("gpsimd", ev_z)
        fbv1 = fb_sb[:, 1, g * CG:(g + 1) * CG].unsqueeze(2).to_broadcast([128, CG, 128])
        ev_zf = kb.pool.inc(nc.gpsimd.tensor_tensor(out=vfb[:], in0=z[:], in1=fbv1, op=ALU.mult))

        def gate2(c0, ps):
            kb.wait("vector", ev_zf)
            if c0 == 0:
                kb.wait("vector", st["y_free"])
            nc.vector.tensor_tensor(out=v4[0], in0=ps, in1=vfb[:, c0:c0 + 4, :], op=ALU.add)
            return kb.dve.inc(nc.vector.tensor_tensor(out=ystage[:, c0:c0 + 4, :], in0=v4[0], in1=u[:, 2, c0:c0 + 4, :], op=ALU.mult))

        src_done2, hq_done2 = fft_conv(lambda c: z[:, c, :], ev_z, hq[1], hq_evs[1], gate2)
        st["hq_free"][1] = hq_done2
        st["z_pe_free"] = src_done2
        ev_y = (kb.dve, kb.dve.v)
        st["u_free"] = ev_y
        kb.wait("sync", ev_y)
        e = st_y.inc(nc.sync.dma_start(out=yTL[:, g * CG * 128:(g + 1) * CG * 128].rearrange("p (c n) -> p c n", n=128), in_=ystage), 16)
        st["y_free"] = e
        kb.out_evs.append(e)
    return kb.finish()


import math

_PROGS = {}
_CONSTS = {}


def _prog(key, fn):
    if key not in _PROGS:
        _PROGS[key] = fn()
    return _PROGS[key]


def hyena_tables(L):
    key = ("hy", L)
    if key in _CONSTS:
        return _CONSTS[key]
    f32 = np.float32
    jp = np.arange(NFFT)
    pos = np.zeros(NFFT, np.int64)
    mF = np.zeros(NFFT, f32)
    mB = np.zeros(NFFT, f32)
    fw = jp < L
    pos[fw] = jp[fw]
    mF[fw] = 1
    bw = jp > NFFT - L
    pos[bw] = NFFT - jp[bw]
    mB[bw] = 1
    mB[0] = 1
    t = np.linspace(0.0, 1.0, L, dtype=f32)
    w = (f32(2.0 * math.pi) * np.arange(L, dtype=f32) / f32(L)).astype(f32)
    bands = np.linspace(1e-4, 15, 16, dtype=f32)
    feats = np.concatenate([t[:, None], np.cos(bands * w[:, None]), -np.sin(bands * w[:, None])], -1).astype(f32)
    featsT = np.ascontiguousarray(feats[pos].T)
    min_decay = math.log(1e-2) / 1.5
    max_decay = math.log(1e-2) / 0.3
    deltas = np.abs(np.linspace(min_decay, max_decay, D, dtype=f32))
    decay = np.exp(-t[:, None] * deltas).astype(f32)
    dec_pos = decay[pos]
    dF = (dec_pos * mF[:, None]).reshape(128, 128, D)
    dB = (dec_pos * mB[:, None]).reshape(128, 128, D)
    out = (featsT, dF, dB)
    _CONSTS[key] = out
    return out


def _run(nc, in_maps):
    res = run_bass_kernel_spmd(nc, in_maps, core_ids=list(range(NCORES)))
    return res.results


def run_hf(inp, j, L):
    nc = _prog("hf", build_hf)
    tabs = dft_tables()
    featsT, dF, dB = hyena_tables(L)
    fvec = np.ascontiguousarray(np.stack([inp["hy_f_freq"][j], inp["hy_f_b1"][j], inp["hy_f_b2"][j], inp["hy_f_b3"][j]], 1))
    w23 = np.ascontiguousarray(np.concatenate([inp["hy_f_w2"][j], inp["hy_f_w3"][j]], 1))
    wo = inp["hy_f_wout"][j].reshape(64, 2, 2, D)
    in_maps = []
    for i in range(NCORES):
        ws = wo[:, :, :, 128 * i:128 * (i + 1)].reshape(64, 2, 2, NG, CG).transpose(0, 3, 1, 2, 4)
        dFi = dF[:, :, 128 * i:128 * (i + 1)].reshape(128, 128, NG, CG).transpose(2, 0, 1, 3)
        dBi = dB[:, :, 128 * i:128 * (i + 1)].reshape(128, 128, NG, CG).transpose(2, 0, 1, 3)
        in_maps.append({"featsT": featsT, "w1": np.ascontiguousarray(inp["hy_f_w1"][j]), "w23": w23, "fvec": fvec,
                        "wout": np.ascontiguousarray(ws).reshape(64, NG * 128),
                        "decF": np.ascontiguousarray(dFi).reshape(NG, 128, 128 * CG),
                        "decB": np.ascontiguousarray(dBi).reshape(NG, 128, 128 * CG),
                        "FT1": tabs["FT1"], "TW": tabs["TW"], "FS3": tabs["FS3"]})
    return [r["Hq"] for r in _run(nc, in_maps)]


def run_hc(inp, j, P, L, hqs):
    nc = _prog("hc", build_hc)
    tabs = dft_tables()
    nrow = L // 128
    Ppad = np.zeros((2, 3 * D, L + 2), NPBF)
    Ppad[:, :, 1:L + 1] = P
    idx = (np.arange(nrow) * 128)[:, None] + np.arange(130)[None, :]
    TL = Ppad[:, :, idx]
    mask = np.zeros((2, 64), np.float32)
    mask[:, :nrow] = 1.0
    mask = mask.reshape(128, 1)
    in_maps = []
    for i in range(NCORES):
        p3 = np.zeros((NG, 2, 64, 3, CG, 130), NPBF)
        blk = TL.reshape(2, 3, D, nrow, 130)[:, :, 128 * i:128 * (i + 1)]
        blk = blk.reshape(2, 3, NG, CG, nrow, 130).transpose(2, 0, 4, 1, 3, 5)
        p3[:, :, :nrow] = blk
        rep = lambda a: np.ascontiguousarray(np.broadcast_to(a.reshape(1, -1), (128, a.size))).astype(np.float32)
        cwi = inp["hy_conv_w"][j].reshape(3, 3, D)[:, :, 128 * i:128 * (i + 1)]
        cbi = inp["hy_conv_b"][j].reshape(3, D)[:, 128 * i:128 * (i + 1)]
        fbi = inp["hy_f_bias"][j][:, 128 * i:128 * (i + 1)]
        in_maps.append({"p3": p3.reshape(NG, 128, 3 * CG * 130), "cw": rep(cwi), "cb": rep(cbi), "fb": rep(fbi), "mask": mask,
                        "Hq": hqs[i], "FS1": tabs["FS1"], "TW": tabs["TW"], "FS3": tabs["FS3"], "R12": tabs["R12"],
                        "LAB": tabs["LAB"]})
    res = _run(nc, in_maps)
    y = np.zeros((2, L, D), NPBF)
    for i in range(NCORES):
        yt = np.asarray(res[i]["yTL"]).reshape(2, 64, 128, 128)[:, :nrow]
        y[:, :, 128 * i:128 * (i + 1)] = yt.transpose(0, 1, 3, 2).reshape(2, L, 128)
    return y


def _ts_tokens(i, lat, ctx):
    b, q = i // 4, i % 4
    xs = np.concatenate([lat[b, q * T_LAT:(q + 1) * T_LAT], ctx[b, q * T_CTX:(q + 1) * T_CTX]], 0)
    return np.ascontiguousarray(xs.T)


def _vec8(v):
    return np.ascontiguousarray(np.asarray(v, np.float32).reshape(DC, 128).T)


def _ts_post_inputs(inp, i, l):
    b = i // 4
    cc = np.stack([inp["c"][b], inp["c_ctx"]], 0)
    cT = np.ascontiguousarray(cc.reshape(2, DC, 128).transpose(2, 1, 0)).reshape(128, 2 * DC)
    m = {"cT": cT, "w_mod": inp["w_mod"][l], "b_mod": np.ascontiguousarray(inp["b_mod"][l].reshape(72, 128).T),
         "norm_w": np.ascontiguousarray(inp["norm_w"][l].reshape(3, DC, 128).transpose(2, 0, 1)).reshape(128, 3 * DC),
         "w_gu1": inp["ffn_w_gate_up"][l, 0], "w_dn1": inp["ffn_w_down"][l, 0]}
    if l % 2 == 1:
        j = l // 2
        m["w_in"] = inp["hy_w_in"][j]
        m["b_in"] = np.ascontiguousarray(inp["hy_b_in"][j].reshape(24, 128).T)
    return m


def _run_ts(inp, l_pre, l_post, xT, oT, modp):
    pre = l_pre is not None
    post = None if l_post is None else ("attn" if l_post % 2 == 0 else "hyena")
    nc = _prog(("ts", pre, post), lambda: build_ts(pre, post))
    in_maps = []
    for i in range(NCORES):
        m = {"xT_in": xT[i]}
        if pre:
            if l_pre % 2 == 0:
                w_o, b_o = inp["attn_w_o"][l_pre // 2], np.zeros(D, np.float32)
            else:
                w_o, b_o = inp["hy_w_out"][l_pre // 2], inp["hy_b_out"][l_pre // 2]
            m.update({"oT_in": oT[i], "w_o": w_o, "b_o": _vec8(b_o), "modp_in": modp[i],
                      "w_gu2": inp["ffn_w_gate_up"][l_pre, 1], "w_dn2": inp["ffn_w_down"][l_pre, 1]})
        if post:
            m.update(_ts_post_inputs(inp, i, l_post))
        in_maps.append(m)
    return _run(nc, in_maps)


def _run_attn(inp, a, hTs):
    nc = _prog("attn", build_attn)
    if "attn" not in _CONSTS:
        _CONSTS["attn"] = attn_consts()
    cosT, sinT, rotm, blk = _CONSTS["attn"]
    wqkv = inp["attn_w_qkv"][a]
    nrm = np.ascontiguousarray(np.stack([np.tile(inp["attn_q_norm"][a], 2), np.tile(inp["attn_k_norm"][a], 2)], 1).astype(np.float32))
    Hb = []
    for b in range(2):
        parts = [np.asarray(hTs[4 * b + q]) for q in range(4)]
        Hb.append(np.ascontiguousarray(np.concatenate([p[:, T_LAT:] for p in parts] + [p[:, :T_LAT] for p in parts], 1)))
    in_maps = []
    for i in range(NCORES):
        b, g = i // 4, i % 4
        wk = wqkv[:, 1024 + 64 * g:1024 + 64 * (g + 1)]
        in_maps.append({"hT": Hb[b], "wq": np.ascontiguousarray(wqkv[:, 256 * g:256 * (g + 1)]),
                        "wkk": np.ascontiguousarray(np.concatenate([wk, wk], 1)),
                        "wv": np.ascontiguousarray(wqkv[:, 1280 + 64 * g:1280 + 64 * (g + 1)]),
                        "nrm": nrm, "cosT": cosT, "sinT": sinT, "rotm": rotm, "blk": blk})
    res = _run(nc, in_maps)
    oT = []
    for b in range(2):
        Ob = np.concatenate([np.asarray(res[4 * b + g]["oT"]) for g in range(4)], 0)
        for q in range(4):
            oT.append(np.ascontiguousarray(np.concatenate([Ob[:, CTX + q * T_LAT:CTX + (q + 1) * T_LAT],
                                                           Ob[:, q * T_CTX:(q + 1) * T_CTX]], 1)))
    return oT


def _run_hyena(inp, j, pTs, with_ctx):
    P_lat = np.zeros((2, 3 * D, SEQ), NPBF)
    P_ctx = np.zeros((2, 3 * D, CTX), NPBF)
    for i in range(NCORES):
        b, q = i // 4, i % 4
        p = np.asarray(pTs[i])
        P_lat[b, :, q * T_LAT:(q + 1) * T_LAT] = p[:, :T_LAT]
        P_ctx[b, :, q * T_CTX:(q + 1) * T_CTX] = p[:, T_LAT:]
    hq = run_hf(inp, j, SEQ)
    y_lat = run_hc(inp, j, P_lat, SEQ, hq)
    if with_ctx:
        hqc = run_hf(inp, j, CTX)
        y_ctx = run_hc(inp, j, P_ctx, CTX, hqc)
    else:
        y_ctx = np.zeros((2, CTX, D), NPBF)
    return [_ts_tokens(i, y_lat, y_ctx) for i in range(NCORES)]


def kernel(**inp):
    inp = {k: np.asarray(v) for k, v in inp.items()}
    xT = [_ts_tokens(i, inp["x"], inp["ctx"]) for i in range(NCORES)]
    oT = None
    modp = None
    for l in range(4):
        res = _run_ts(inp, l - 1 if l > 0 else None, l, xT, oT, modp)
        xT = [r["xT_out"] for r in res]
        modp = [r["mod_out"] for r in res]
        if l % 2 == 0:
            oT = _run_attn(inp, l // 2, [r["hT_out"] for r in res])
        else:
            oT = _run_hyena(inp, l // 2, [r["pT_out"] for r in res], with_ctx=(l < 3))
    res = _run_ts(inp, 3, None, xT, oT, modp)
    out = np.zeros((2, SEQ, D), np.float32)
    for i in range(NCORES):
        b, q = i // 4, i % 4
        out[b, q * T_LAT:(q + 1) * T_LAT] = np.asarray(res[i]["xT_out"])[:, :T_LAT].T
    return out
```
